# Optimizing a Trainium2 kernel written in Bass

```python
import math
import jax, jax.numpy as jnp
from jax import lax
import numpy as np

D_MODEL = 2048
BATCH = 8
SEQ = 4096
DEPTH = 2

HEAD_DIM = 128
BLOCK = 128
WINDOW = 128
RMS_EPS = 1e-6
NEG_INF = -1e30
SWA_HEADS = 8
SWA_KV_HEADS = 2
SWA_WIDTH = SWA_HEADS * HEAD_DIM
SWA_KV_WIDTH = SWA_KV_HEADS * HEAD_DIM
MLA_HEADS = 4
MLA_NOPE_DIM = 128
MLA_ROPE_DIM = 64
MLA_V_DIM = 128
MLA_Q_RANK = 384
MLA_KV_RANK = 256
MLA_WIDTH = MLA_HEADS * MLA_V_DIM
ROPE_THETA = 10000.0
DIFF_HEADS = 4
DIFF_QK_DIM = 64
DIFF_V_DIM = 128
DIFF_QK_WIDTH = DIFF_HEADS * 2 * DIFF_QK_DIM
DIFF_WIDTH = DIFF_HEADS * DIFF_V_DIM

D_MIX = SWA_WIDTH + MLA_WIDTH + DIFF_WIDTH
SPLIT_SIZES = (SWA_WIDTH, SWA_KV_WIDTH, SWA_KV_WIDTH, SWA_WIDTH,
               MLA_Q_RANK, MLA_KV_RANK, MLA_ROPE_DIM, MLA_WIDTH,
               DIFF_QK_WIDTH, DIFF_QK_WIDTH, DIFF_WIDTH, DIFF_WIDTH)
SPLIT_OFFSETS = tuple(int(o) for o in np.cumsum(SPLIT_SIZES)[:-1])
D_IN = int(sum(SPLIT_SIZES))

kernel_name = "hymba_swa_mla_diff_encoder"


def rmsnorm(x, g):
    xf = x.astype(jnp.float32)
    xf = xf * lax.rsqrt(jnp.mean(xf * xf, axis=-1, keepdims=True) + RMS_EPS)
    return xf.astype(x.dtype) * g


def alibi_slopes(n):
    return jnp.asarray(np.array([2.0 ** (-8.0 * (i + 1) / n) for i in range(n)], dtype=np.float32))


def rope(x, positions):
    d = x.shape[-1]
    inv = ROPE_THETA ** (-jnp.arange(0, d, 2, dtype=jnp.float32) / d)
    ang = positions.astype(jnp.float32)[:, None] * inv[None, :]
    shape = (ang.shape[0],) + (1,) * (x.ndim - 3) + (d // 2,)
    cos = jnp.cos(ang).reshape(shape).astype(x.dtype)
    sin = jnp.sin(ang).reshape(shape).astype(x.dtype)
    x1, x2 = x[..., : d // 2], x[..., d // 2:]
    return jnp.concatenate([x1 * cos - x2 * sin, x1 * sin + x2 * cos], axis=-1)


def to_blocks(t):
    b, s = t.shape[:2]
    return jnp.moveaxis(t.reshape((b, s // BLOCK, BLOCK) + t.shape[2:]), 1, 0)


def from_blocks(t):
    nb, b = t.shape[:2]
    return jnp.moveaxis(t, 0, 1).reshape((b, nb * BLOCK) + t.shape[3:])


def swa_attention(q, k, v, sink, positions):
    b, s = q.shape[:2]
    nb = s // BLOCK
    g = SWA_HEADS // SWA_KV_HEADS
    qb = q.reshape(b, nb, BLOCK, SWA_KV_HEADS, g, HEAD_DIM)

    def band(t):
        pad = [(0, 0), (BLOCK, BLOCK)] + [(0, 0)] * (t.ndim - 2)
        tp = jnp.pad(t, pad).reshape((b, nb + 2, BLOCK) + t.shape[2:])
        return jnp.concatenate([tp[:, :-2], tp[:, 1:-1], tp[:, 2:]], axis=2)

    def band1d(t):
        tp = jnp.pad(t, (BLOCK, BLOCK)).reshape(nb + 2, BLOCK)
        return jnp.concatenate([tp[:-2], tp[1:-1], tp[2:]], axis=1)

    kb, vb = band(k), band(v)
    kpos = band1d(positions)
    valid = band1d(jnp.ones((s,), dtype=jnp.bool_))
    qpos = positions.reshape(nb, BLOCK)
    dist = jnp.abs(qpos[:, :, None] - kpos[:, None, :])
    mask = valid[:, None, :] & (dist <= WINDOW)

    scale = HEAD_DIM ** -0.5
    sc = jnp.einsum('bnqhgd,bnkhd->bnhgqk', qb, kb).astype(jnp.float32) * scale
    slopes = alibi_slopes(SWA_HEADS).reshape(SWA_KV_HEADS, g)
    sc = sc - slopes[None, None, :, :, None, None] * dist.astype(jnp.float32)[None, :, None, None]
    sc = jnp.where(mask[None, :, None, None], sc, NEG_INF)
    sk = sink.astype(jnp.float32).reshape(SWA_KV_HEADS, g)[None, None, :, :, None, None]
    m = jnp.maximum(jnp.max(sc, axis=-1, keepdims=True), sk)
    p = jnp.exp(sc - m)
    p = p / (jnp.sum(p, axis=-1, keepdims=True) + jnp.exp(sk - m))
    out = jnp.einsum('bnhgqk,bnkhd->bnqhgd', p.astype(v.dtype), vb)
    return out.reshape(b, s, SWA_WIDTH)


def mla_attention(c_q, c_kv, k_rope, q_norm, w_uq, kv_norm, w_ukv, positions):
    b, s = c_q.shape[:2]
    q = jnp.einsum('bsr,re->bse', rmsnorm(c_q, q_norm), w_uq)
    q = q.reshape(b, s, MLA_HEADS, MLA_NOPE_DIM + MLA_ROPE_DIM)
    q_nope = q[..., :MLA_NOPE_DIM]
    q_rope = rope(q[..., MLA_NOPE_DIM:], positions)
    kv = jnp.einsum('bsr,re->bse', rmsnorm(c_kv, kv_norm), w_ukv)
    kv = kv.reshape(b, s, MLA_HEADS, MLA_NOPE_DIM + MLA_V_DIM)
    k_nope, v = kv[..., :MLA_NOPE_DIM], kv[..., MLA_NOPE_DIM:]
    k_r = rope(k_rope, positions)
    scale = (MLA_NOPE_DIM + MLA_ROPE_DIM) ** -0.5

    def block(xs):
        qn, qr = xs
        sc = (jnp.einsum('bqhd,bkhd->bhqk', qn, k_nope)
              + jnp.einsum('bqhd,bkd->bhqk', qr, k_r)).astype(jnp.float32) * scale
        a = jax.nn.softmax(sc, axis=-1).astype(v.dtype)
        return jnp.einsum('bhqk,bkhd->bqhd', a, v)

    out = from_blocks(lax.map(block, (to_blocks(q_nope), to_blocks(q_rope))))
    return out.reshape(b, s, MLA_WIDTH)


def diff_attention(q, k, v, lam_params, subln_g, positions, lam_init):
    b, s = q.shape[:2]
    nb = s // BLOCK
    lp = lam_params.astype(jnp.float32)
    lam = jnp.exp(jnp.sum(lp[0] * lp[1])) - jnp.exp(jnp.sum(lp[2] * lp[3])) + lam_init
    slopes = alibi_slopes(DIFF_HEADS)
    scale = DIFF_QK_DIM ** -0.5

    def block(xs):
        qb, qpos = xs
        sc = jnp.einsum('bqhcd,bkhcd->bhcqk', qb, k).astype(jnp.float32) * scale
        dist = jnp.abs(qpos[:, None] - positions[None, :]).astype(jnp.float32)
        sc = sc - slopes[None, :, None, None, None] * dist[None, None, None]
        a = jax.nn.softmax(sc, axis=-1)
        attn = (a[:, :, 0] - lam * a[:, :, 1]).astype(v.dtype)
        return jnp.einsum('bhqk,bkhd->bqhd', attn, v)

    out = from_blocks(lax.map(block, (to_blocks(q), positions.reshape(nb, BLOCK))))
    out = rmsnorm(out, subln_g) * (1.0 - lam_init)
    return out.reshape(b, s, DIFF_WIDTH)


def setup_inputs(seed: int = 0) -> dict:
    key = jax.random.key(seed)
    ks = jax.random.split(key, 16)
    f32 = jnp.float32
    nrm = lambda k, shape, sc: jax.random.normal(k, shape, dtype=f32) * sc
    return {
        "x": nrm(ks[0], (BATCH, SEQ, D_MODEL), 1.0),
        "positions": jnp.arange(SEQ, dtype=jnp.int32),
        "norm_g": 1.0 + nrm(ks[1], (DEPTH, D_MODEL), 0.02),
        "w_in": nrm(ks[2], (DEPTH, D_MODEL, D_IN), D_MODEL ** -0.5),
        "swa_sink": nrm(ks[3], (DEPTH, SWA_HEADS), 0.5),
        "mla_q_norm": 1.0 + nrm(ks[4], (DEPTH, MLA_Q_RANK), 0.02),
        "mla_w_uq": nrm(ks[5], (DEPTH, MLA_Q_RANK, MLA_HEADS * (MLA_NOPE_DIM + MLA_ROPE_DIM)), MLA_Q_RANK ** -0.5),
        "mla_kv_norm": 1.0 + nrm(ks[6], (DEPTH, MLA_KV_RANK), 0.02),
        "mla_w_ukv": nrm(ks[7], (DEPTH, MLA_KV_RANK, MLA_HEADS * (MLA_NOPE_DIM + MLA_V_DIM)), MLA_KV_RANK ** -0.5),
        "diff_lambda": nrm(ks[8], (DEPTH, 4, DIFF_QK_DIM), 0.1),
        "diff_subln": 1.0 + nrm(ks[9], (DEPTH, DIFF_V_DIM), 0.02),
        "w_out": nrm(ks[10], (DEPTH, D_MIX, D_MODEL), D_MIX ** -0.5),
        "final_norm": 1.0 + nrm(ks[11], (D_MODEL,), 0.02),
    }


def reference(x, positions, norm_g, w_in, swa_sink, mla_q_norm, mla_w_uq, mla_kv_norm,
              mla_w_ukv, diff_lambda, diff_subln, w_out, final_norm):
    b, s, _ = x.shape
    for l in range(DEPTH):
        h = rmsnorm(x, norm_g[l])
        proj = jnp.einsum('bsd,de->bse', h, w_in[l])
        (a_q, a_k, a_v, a_g, b_cq, b_ckv, b_kr, b_g,
         c_q, c_k, c_v, c_g) = jnp.split(proj, SPLIT_OFFSETS, axis=-1)
        ya = swa_attention(a_q.reshape(b, s, SWA_HEADS, HEAD_DIM),
                           a_k.reshape(b, s, SWA_KV_HEADS, HEAD_DIM),
                           a_v.reshape(b, s, SWA_KV_HEADS, HEAD_DIM),
                           swa_sink[l], positions)
        yb = mla_attention(b_cq, b_ckv, b_kr, mla_q_norm[l], mla_w_uq[l],
                           mla_kv_norm[l], mla_w_ukv[l], positions)
        lam_init = 0.8 - 0.6 * math.exp(-0.3 * l)
        yc = diff_attention(c_q.reshape(b, s, DIFF_HEADS, 2, DIFF_QK_DIM),
                            c_k.reshape(b, s, DIFF_HEADS, 2, DIFF_QK_DIM),
                            c_v.reshape(b, s, DIFF_HEADS, DIFF_V_DIM),
                            diff_lambda[l], diff_subln[l], positions, lam_init)
        y = jnp.concatenate([ya * jax.nn.silu(a_g), yb * jax.nn.silu(b_g),
                             yc * jax.nn.silu(c_g)], axis=-1)
        x = x + jnp.einsum('bse,ed->bsd', y, w_out[l])
    return rmsnorm(x, final_norm)
```

```python
import math
from contextlib import ExitStack

import numpy as np
import concourse.bass as bass
import concourse.mybir as mybir
from concourse.bass_utils import run_bass_kernel_spmd

F32 = mybir.dt.float32
BF16 = mybir.dt.bfloat16
I32 = mybir.dt.int32
ALU = mybir.AluOpType
AF = mybir.ActivationFunctionType
AX = mybir.AxisListType

D = 2048
DC = 16
D_IN = 5824
EPS = 1e-6
NEG = -30000.0
O_AQ, O_AK, O_AV, O_AG, O_CQB, O_CKV, O_KR, O_GB, O_QC, O_KC, O_VC, O_GC = (
    0, 1024, 1280, 1536, 2560, 2944, 3200, 3264, 3776, 4288, 4800, 5312)
SWA_SLOPES = [2.0 ** (-8.0 * (i + 1) / 8) for i in range(8)]
DIF_SLOPES = [2.0 ** (-8.0 * (i + 1) / 4) for i in range(4)]
SEM_LIMIT = 30000


class Buf:
    def __init__(self, t, name):
        self.t = t
        self.name = name
        self.w = {}
        self.r = {}
        self.dsem = None
        self.dcnt = 0
        self.dram = False

    def __getitem__(self, k):
        return self.t[k]


class Eng:
    def __init__(self, kb, eng, kind):
        self.kb = kb
        self.eng = eng
        self.kind = kind
        self.sem = kb.new_sem()
        self.own = {self.sem}
        self.cnt = 0
        self.seen = {}
        self.pending = False

    def need(self, tok, raw):
        sem, val = tok
        if sem in self.own:
            if self.kind in ("pe", "sp"):
                return
        if sem in self.kb.dma_sems:
            val = self.kb.latest[sem]
        if self.seen.get(sem, 0) >= val:
            return
        self.eng.wait_ge(sem, val)
        self.seen[sem] = val

    def bump(self, ins, inc=True):
        if inc:
            self.cnt += 1
            ins.then_inc(self.sem, 1)
            tok = (self.sem, self.cnt)
            self.pending = False
            self.kb.latest[self.sem] = self.cnt
            if self.cnt >= SEM_LIMIT:
                self.sem = self.kb.new_sem()
                self.own.add(self.sem)
                self.cnt = 0
            return tok
        self.pending = True
        return (self.sem, self.cnt + 1)


class KB:
    def __init__(self, nc, es):
        self.nc = nc
        self.es = es
        self.nsem = 0
        self.dma_sems = set()
        self.stores_on_pool = False
        self.free_dsems = []
        self.scope_bufs = [[]]
        self.latest = {}
        self.scopes = [es]
        self.pe = Eng(self, nc.tensor, "pe")
        self.act = Eng(self, nc.scalar, "act")
        self.dve = Eng(self, nc.vector, "dve")
        self.pool = Eng(self, nc.gpsimd, "pool")
        self.sp = Eng(self, nc.sync, "sp")
        self.engs = [self.pe, self.act, self.dve, self.pool, self.sp]
        self.nuid = 0

    def new_sem(self):
        self.nsem += 1
        return self.es.enter_context(self.nc.semaphore(f"sm{self.nsem}"))

    def uid(self, n):
        self.nuid += 1
        return f"{n}_{self.nuid}"

    def sb(self, name, shape, dtype):
        t = self.scopes[-1].enter_context(self.nc.sbuf_tensor(self.uid(name), list(shape), dtype))
        b = Buf(t, name)
        self.scope_bufs[-1].append(b)
        return b

    def ps(self, name):
        t = self.scopes[-1].enter_context(self.nc.psum_tensor(self.uid(name), [128, 512], F32))
        return Buf(t, name)

    def dram(self, name, shape, dtype):
        t = self.nc.dram_tensor(name, list(shape), dtype, kind="Internal")
        b = Buf(t.ap(), name)
        b.dram = True
        return b

    def op(self, E, reads, writes, fn, inc=True):
        for b in reads:
            for tok in list(b.w.values()):
                E.need(tok, True)
        for b in writes:
            for tok in list(b.w.values()) + list(b.r.values()):
                E.need(tok, False)
        ins = fn(E.eng)
        tok = E.bump(ins, inc)
        for b in reads:
            b.r[tok[0]] = tok
        for b in writes:
            b.r = {}
            b.w[tok[0]] = tok
        return tok

    def dma(self, out_ap, in_ap, src, dst, Q=None, **kw):
        if Q is None:
            Q = self.pool if (dst.dram and self.stores_on_pool) else self.sp
        own = dst if not dst.dram else src
        assert not own.dram
        if own.dsem is None and self.free_dsems:
            own.dsem, own.dcnt = self.free_dsems.pop()
        if own.dsem is None or own.dcnt + 16 >= SEM_LIMIT:
            own.dsem = self.new_sem()
            self.dma_sems.add(own.dsem)
            own.dcnt = 0
        for tok in list(src.w.values()):
            Q.need(tok, True)
        toks = list(dst.r.values())
        if not dst.dram:
            toks += list(dst.w.values())
        for tok in toks:
            if tok[0] == own.dsem:
                continue
            Q.need(tok, False)
        ins = Q.eng.dma_start(out=out_ap, in_=in_ap, **kw)
        own.dcnt += 16
        ins.then_inc(own.dsem, 16)
        tok = (own.dsem, own.dcnt)
        self.latest[own.dsem] = own.dcnt
        src.r[own.dsem] = tok
        dst.r = {}
        dst.w[own.dsem] = tok
        return tok

    def barrier(self):
        assert not self.pe.pending
        items = list(self.latest.items())
        for E in self.engs:
            for sem, val in items:
                if E.seen.get(sem, 0) >= val:
                    continue
                if sem in E.own and E.kind in ("pe", "sp"):
                    continue
                E.eng.wait_ge(sem, val)
                E.seen[sem] = val

    class _Scope:
        def __init__(self, kb):
            self.kb = kb

        def __enter__(self):
            self.st = ExitStack()
            self.st.__enter__()
            self.kb.scopes.append(self.st)
            self.kb.scope_bufs.append([])
            return self

        def __exit__(self, *a):
            self.kb.barrier()
            self.kb.scopes.pop()
            for b in self.kb.scope_bufs.pop():
                if b.dsem is not None:
                    self.kb.free_dsems.append((b.dsem, b.dcnt))
                    b.dsem = None
            return self.st.__exit__(*a)

    def scope(self):
        return KB._Scope(self)


def build(S=4096, NL=2, dbg=False):
    NT = S // 128
    NTB = S // 512
    nc = bass.Bass("TRN2", target_bir_lowering=False)
    es = ExitStack()
    with es:
        es.enter_context(nc.allow_low_precision("bf16 matmul operands, fp32 accumulation"))
        try:
            es.enter_context(nc.allow_non_contiguous_dma("layout shuffles"))
        except Exception:
            pass
        _build(nc, es, S, NL, NT, NTB, dbg)
    return nc


def _build(nc, es, S, NL, NT, NTB, dbg):
    kb = KB(nc, es)
    pe, act, dve, pool, sp = kb.pe, kb.act, kb.dve, kb.pool, kb.sp

    def din(name, shape, dt=F32):
        b = Buf(nc.dram_tensor(name, list(shape), dt, kind="ExternalInput").ap(), name)
        b.dram = True
        return b

    x_in = din("x", [S, D])
    pos_in = din("positions", [S], I32)
    normg_in = din("norm_g", [2, D])
    win_in = din("w_in", [2, D, D_IN])
    sink_in = din("swa_sink", [2, 8])
    qn_in = din("mla_q_norm", [2, 384])
    wuq_in = din("mla_w_uq", [2, 384, 768])
    kvn_in = din("mla_kv_norm", [2, 256])
    wukv_in = din("mla_w_ukv", [2, 256, 1024])
    lam_in = din("diff_lambda", [2, 4, 64])
    subln_in = din("diff_subln", [2, 128])
    wout_in = din("w_out", [2, D, D])
    fnorm_in = din("final_norm", [D])
    okind = "ExternalOutput"
    out_d = Buf(nc.dram_tensor("out", [S, D], F32, kind=okind).ap(), "out")
    out_d.dram = True

    def scr(name, shape, dt=BF16):
        if dbg:
            b = Buf(nc.dram_tensor(name, list(shape), dt, kind="ExternalOutput").ap(), name)
            b.dram = True
            return b
        return kb.dram(name, shape, dt)

    QA = scr("QA", [1024, S]); KA = scr("KA", [256, S]); GA = scr("GA", [1024, S])
    CQ = scr("CQ", [384, S]); CKV = scr("CKV", [256, S]); KR = scr("KRr", [64, S]); GB = scr("GB", [512, S])
    QC = scr("QC", [512, S]); KC = scr("KC", [512, S]); GC = scr("GC", [512, S])
    VHA = scr("VHA", [2, 128, NT, 128]); VHC = scr("VHC", [4, 128, NT, 128]); VHB = scr("VHB", [4, 128, NT, 128])
    QBN = scr("QBN", [4, 128, S]); QBR = scr("QBR", [4, 64, S]); KN = scr("KN", [4, 128, S])
    YT = scr("YT", [D, S])
    X1 = scr("X1", [S, D], F32)
    HT1 = scr("HT1", [D, S])
    WOB = scr("WOB", [D, D])
    COS = scr("COS", [64, S], F32); SIN = scr("SIN", [64, S], F32)
    QAUG = scr("QAUG", [4, S]); KAUG = scr("KAUG", [4, 2, 4, S])

    ident = kb.sb("ident", [128, 128], BF16)
    ones = kb.sb("ones", [128, 128], BF16)
    BH = [kb.sb(f"bh{h}", [128, 384], BF16) for h in range(8)]
    CH = [kb.sb(f"ch{h}", [128, 128], BF16) for h in range(4)]
    esink = kb.sb("esink", [128, 16], F32)
    nlam = kb.sb("nlam", [128, 2], F32)
    subg = kb.sb("subg", [128, 2], F32)
    gcol = kb.sb("gcol", [128, 2, 16], F32)
    qncol = kb.sb("qncol", [128, 2, 3], F32)
    kvncol = kb.sb("kvncol", [128, 2, 2], F32)
    psbig = es.enter_context(nc.psum_tensor("psbig", [128, 4096], F32))
    PS = [Buf(psbig[:, i * 512:(i + 1) * 512], f"ps{i}") for i in range(8)]
    epsb = {}
    for addc_ in (float(D) * EPS, 384.0 * EPS, 256.0 * EPS, 128.0 * EPS):
        epsb[addc_] = kb.sb("epsb", [128, 1], F32)
        kb.op(dve, [], [epsb[addc_]], lambda e, a=addc_: e.memset(epsb[a][:], float(a)))

    sel32 = kb.sb("sel32", [128, 128], F32)
    selA = kb.sb("selA", [128, 128], F32)
    selB = kb.sb("selB", [128, 128], F32)
    kb.op(dve, [], [sel32], lambda e: e.memset(sel32[:], 1.0 / 32.0))
    kb.op(dve, [], [selA], lambda e: e.memset(selA[:], 0.0))
    kb.op(dve, [], [selB], lambda e: e.memset(selB[:], 0.0))
    for p0 in (0, 64):
        kb.op(dve, [], [selA], lambda e, p0=p0: e.memset(selA[p0:p0 + 32, :], 1.0 / 32.0))
        kb.op(dve, [], [selB], lambda e, p0=p0: e.memset(selB[p0 + 32:p0 + 64, :], 1.0 / 32.0))
    onecol = kb.sb("onecol", [128, 1], F32)
    kb.op(dve, [], [onecol], lambda e: e.memset(onecol[:], 1.0))

    def recip(src, src_ap, dst, dst_ap, bias_buf=None, bias_ap=None):
        rd_ = [src] + ([bias_buf] if bias_buf is not None else [])
        if bias_ap is not None:
            kb.op(act, rd_, [dst], lambda e: e.activation(out=dst_ap, in_=src_ap, func=AF.Ln, bias=bias_ap))
        else:
            kb.op(act, rd_, [dst], lambda e: e.activation(out=dst_ap, in_=src_ap, func=AF.Ln))
        kb.op(act, [dst], [dst], lambda e: e.activation(out=dst_ap, in_=dst_ap, func=AF.Exp, scale=-1.0))

    def rsqrt(src, src_ap, dst, dst_ap, addc):
        np_ = dst_ap.shape[0]
        kb.op(act, [src, epsb[addc]], [dst], lambda e: e.activation(out=dst_ap, in_=src_ap, func=AF.Ln, bias=epsb[addc][0:np_, 0:1]))
        kb.op(act, [dst], [dst], lambda e: e.activation(out=dst_ap, in_=dst_ap, func=AF.Exp, scale=-0.5))

    with kb.scope():
        ii = kb.sb("ii", [128, 384], I32)
        ff = kb.sb("ff", [128, 384], F32)
        f2 = kb.sb("f2", [128, 384], F32)
        kb.op(pool, [], [ii], lambda e: e.iota(ii[:, 0:128], [[1, 128]], base=0, channel_multiplier=-1))
        kb.op(dve, [ii], [ff], lambda e: e.tensor_copy(out=ff[:, 0:128], in_=ii[:, 0:128]))
        kb.op(dve, [ff], [ident], lambda e: e.tensor_single_scalar(out=ident[:], in_=ff[:, 0:128], scalar=0.0, op=ALU.is_equal))
        kb.op(dve, [], [ones], lambda e: e.memset(ones[:], 1.0))
        kb.op(pool, [], [ii], lambda e: e.iota(ii[:, 0:128], [[-1, 128]], base=0, channel_multiplier=1))
        kb.op(dve, [ii], [ff], lambda e: e.tensor_copy(out=ff[:, 0:128], in_=ii[:, 0:128]))
        for h in range(4):
            kb.op(dve, [ff], [CH[h]], lambda e, h=h: e.tensor_scalar(
                out=CH[h][:], in0=ff[:, 0:128], scalar1=0.0, scalar2=-2.0 * DIF_SLOPES[h], op0=ALU.max, op1=ALU.mult))
        kb.op(pool, [], [ii], lambda e: e.iota(ii[:], [[1, 384]], base=-128, channel_multiplier=-1))
        kb.op(dve, [ii], [ff], lambda e: e.tensor_copy(out=ff[:], in_=ii[:]))
        kb.op(act, [ff], [ff], lambda e: e.activation(out=ff[:], in_=ff[:], func=AF.Abs))
        kb.op(dve, [ff], [f2], lambda e: e.tensor_scalar(
            out=f2[:], in0=ff[:], scalar1=128.5, scalar2=NEG, op0=ALU.is_gt, op1=ALU.mult))
        for h in range(8):
            kb.op(dve, [ff, f2], [BH[h]], lambda e, h=h: e.scalar_tensor_tensor(
                out=BH[h][:], in0=ff[:], scalar=-SWA_SLOPES[h], in1=f2[:], op0=ALU.mult, op1=ALU.add))

        sk = kb.sb("sk", [128, 16], F32)
        kb.dma(sk[:], sink_in.t.rearrange("l h -> (l h)").partition_broadcast(128), sink_in, sk)
        kb.op(act, [sk], [esink], lambda e: e.activation(out=esink[:], in_=sk[:], func=AF.Exp))
        lm = kb.sb("lm", [128, 2, 4, 64], F32)
        kb.dma(lm[:].rearrange("p l a b -> p (l a b)"),
               lam_in.t.rearrange("l a b -> (l a b)").partition_broadcast(128), lam_in, lm)
        pr = kb.sb("pr", [128, 2, 2, 64], F32)
        sm = kb.sb("sm", [128, 4], F32)
        for l in range(2):
            for j in range(2):
                kb.op(dve, [lm], [pr], lambda e, l=l, j=j: e.tensor_tensor(
                    out=pr[:, l, j, :], in0=lm[:, l, 2 * j, :], in1=lm[:, l, 2 * j + 1, :], op=ALU.mult))
                kb.op(dve, [pr], [sm], lambda e, l=l, j=j: e.reduce_sum(
                    out=sm[:, 2 * l + j:2 * l + j + 1], in_=pr[:, l, j, :], axis=AX.X))
        se = kb.sb("se", [128, 4], F32)
        kb.op(act, [sm], [se], lambda e: e.activation(out=se[:], in_=sm[:], func=AF.Exp))
        for l in range(2):
            lam_init = 0.8 - 0.6 * math.exp(-0.3 * l)
            kb.op(dve, [se], [nlam], lambda e, l=l, li=lam_init: e.scalar_tensor_tensor(
                out=nlam[:, l:l + 1], in0=se[:, 2 * l + 1:2 * l + 2], scalar=-li, in1=se[:, 2 * l:2 * l + 1],
                op0=ALU.add, op1=ALU.subtract))
        sg = kb.sb("sg", [128, 2], F32)
        kb.dma(sg[:].unsqueeze(2), subln_in.t.rearrange("l (p o) -> p l o", o=1), subln_in, sg)
        for l in range(2):
            lam_init = 0.8 - 0.6 * math.exp(-0.3 * l)
            kb.op(dve, [sg], [subg], lambda e, l=l, li=lam_init: e.tensor_single_scalar(
                out=subg[:, l:l + 1], in_=sg[:, l:l + 1], scalar=(1.0 - li) * math.sqrt(128.0), op=ALU.mult))
        kb.dma(gcol[:].unsqueeze(3), normg_in.t.rearrange("l (c p o) -> p l c o", p=128, o=1), normg_in, gcol)
        kb.dma(qncol[:].unsqueeze(3), qn_in.t.rearrange("l (c p o) -> p l c o", p=128, o=1), qn_in, qncol)
        kb.dma(kvncol[:].unsqueeze(3), kvn_in.t.rearrange("l (c p o) -> p l c o", p=128, o=1), kvn_in, kvncol)

        pi_ = kb.sb("pi_", [64, S], I32)
        kb.dma(pi_[:], pos_in.t.partition_broadcast(64), pos_in, pi_)
        pf = kb.sb("pf", [64, S], F32)
        kb.op(dve, [pi_], [pf], lambda e: e.tensor_copy(out=pf[:], in_=pi_[:]))
        invrow = kb.sb("invrow", [1, 2, 32], F32)
        for i_ in range(32):
            val = float(np.float32(10000.0) ** np.float32(-(2.0 * i_) / 64.0))
            kb.op(dve, [], [invrow], lambda e, i_=i_, val=val: e.memset(invrow[:, :, i_:i_ + 1], val))
        INVD = kb.dram("INVD", [64], F32)
        kb.dma(INVD.t.rearrange("(o n) -> o n", o=1), invrow[:].rearrange("o a b -> o (a b)"), invrow, INVD)
        inv = kb.sb("inv", [64, 1], F32)
        kb.dma(inv[:], INVD.t.rearrange("(p o) -> p o", o=1), INVD, inv)
        ang = kb.sb("ang", [64, S], F32)
        kb.op(dve, [pf, inv], [ang], lambda e: e.tensor_scalar(
            out=ang[:], in0=pf[:], scalar1=inv[:, 0:1], scalar2=None, op0=ALU.mult))
        tr = kb.sb("tr", [64, S], F32)
        kf = kb.sb("kf", [64, S], F32)
        C1 = 6.28125
        C2 = 2.0 * math.pi - 6.28125
        for shift, dst in ((0.0, SIN), (0.5 * math.pi, COS)):
            kb.op(dve, [ang], [tr], lambda e, sh=shift: e.tensor_scalar(
                out=tr[:], in0=ang[:], scalar1=sh, scalar2=1.0 / (2.0 * math.pi), op0=ALU.add, op1=ALU.mult))
            kb.op(dve, [tr], [pi_], lambda e: e.tensor_copy(out=pi_[:], in_=tr[:]))
            kb.op(dve, [pi_], [kf], lambda e: e.tensor_copy(out=kf[:], in_=pi_[:]))
            kb.op(dve, [ang], [tr], lambda e, sh=shift: e.tensor_single_scalar(out=tr[:], in_=ang[:], scalar=sh, op=ALU.add))
            kb.op(dve, [kf, tr], [tr], lambda e: e.scalar_tensor_tensor(
                out=tr[:], in0=kf[:], scalar=-C1, in1=tr[:], op0=ALU.mult, op1=ALU.add))
            kb.op(dve, [kf, tr], [tr], lambda e: e.scalar_tensor_tensor(
                out=tr[:], in0=kf[:], scalar=-C2, in1=tr[:], op0=ALU.mult, op1=ALU.add))
            for thr, adj in ((math.pi, -2.0 * math.pi), (None, 2.0 * math.pi)):
                if thr is not None:
                    kb.op(dve, [tr], [kf], lambda e: e.tensor_scalar(
                        out=kf[:], in0=tr[:], scalar1=math.pi, scalar2=-2.0 * math.pi, op0=ALU.is_gt, op1=ALU.mult))
                else:
                    kb.op(dve, [tr], [kf], lambda e: e.tensor_scalar(
                        out=kf[:], in0=tr[:], scalar1=-math.pi, scalar2=2.0 * math.pi, op0=ALU.is_lt, op1=ALU.mult))
                kb.op(dve, [kf, tr], [tr], lambda e: e.tensor_tensor(out=tr[:], in0=tr[:], in1=kf[:], op=ALU.add))
            kb.op(dve, [tr], [tr], lambda e: e.tensor_scalar(
                out=tr[:], in0=tr[:], scalar1=3.1415925, scalar2=-3.1415925, op0=ALU.min, op1=ALU.max))
            kb.op(act, [tr], [pf], lambda e: e.activation(out=pf[:], in_=tr[:], func=AF.Sin))
            kb.dma(dst.t, pf[:], pf, dst)

        NJ = S // 128
        p2 = kb.sb("p2", [128, NJ], I32)
        kb.dma(p2[:], pos_in.t.rearrange("(p j) -> p j", p=128), pos_in, p2)
        hi_i = kb.sb("hi_i", [128, NJ], I32)
        lo_i = kb.sb("lo_i", [128, NJ], I32)
        kb.op(dve, [p2], [hi_i], lambda e: e.tensor_single_scalar(out=hi_i[:], in_=p2[:], scalar=6, op=ALU.arith_shift_right))
        kb.op(dve, [p2], [lo_i], lambda e: e.tensor_single_scalar(out=lo_i[:], in_=p2[:], scalar=63, op=ALU.bitwise_and))
        hi_f = kb.sb("hi_f", [128, NJ], F32)
        lo_f = kb.sb("lo_f", [128, NJ], F32)
        kb.op(dve, [hi_i], [hi_f], lambda e: e.tensor_copy(out=hi_f[:], in_=hi_i[:]))
        kb.op(dve, [lo_i], [lo_f], lambda e: e.tensor_copy(out=lo_f[:], in_=lo_i[:]))
        qa = kb.sb("qa", [128, 4, NJ], BF16)
        kb.op(dve, [hi_f], [qa], lambda e: e.tensor_copy(out=qa[:, 0, :], in_=hi_f[:]))
        kb.op(dve, [lo_f], [qa], lambda e: e.tensor_copy(out=qa[:, 1, :], in_=lo_f[:]))
        kb.op(dve, [], [qa], lambda e: e.memset(qa[:, 2:4, :], 1.0))
        kb.dma(QAUG.t.rearrange("r (p j) -> p r j", p=128), qa[:], qa, QAUG)
        ka = kb.sb("ka", [128, 4, 2, 4, NJ], BF16)
        for h in range(4):
            s_ = DIF_SLOPES[h]
            for v, sg_ in ((0, 1.0), (1, -1.0)):
                kb.op(dve, [], [ka], lambda e, h=h, v=v, c=-64.0 * s_ * sg_: e.memset(ka[:, h, v, 0, :], c))
                kb.op(dve, [], [ka], lambda e, h=h, v=v, c=-s_ * sg_: e.memset(ka[:, h, v, 1, :], c))
                kb.op(dve, [hi_f], [ka], lambda e, h=h, v=v, c=64.0 * s_ * sg_: e.tensor_single_scalar(
                    out=ka[:, h, v, 2, :], in_=hi_f[:], scalar=c, op=ALU.mult))
                kb.op(dve, [lo_f], [ka], lambda e, h=h, v=v, c=s_ * sg_: e.tensor_single_scalar(
                    out=ka[:, h, v, 3, :], in_=lo_f[:], scalar=c, op=ALU.mult))
        kb.dma(KAUG.t.rearrange("h v r (p j) -> p (h v r) j", p=128),
               ka[:].rearrange("p h v r j -> p (h v r) j"), ka, KAUG)

    kb.stores_on_pool = False
    for l in range(NL):
        xsrc = x_in if l == 0 else X1
        lam_init = 0.8 - 0.6 * math.exp(-0.3 * l)
        with kb.scope():
            hT = kb.sb("hT", [128, DC, S], BF16)
            hTv = [Buf(hT.t[:, :, tb * 512:(tb + 1) * 512], f"hTv{tb}") for tb in range(NTB)]
            if l > 0:
                for tb in range(NTB):
                    kb.dma(hTv[tb][:], HT1.t[:, tb * 512:(tb + 1) * 512].rearrange("(c p) s -> p c s", p=128), HT1, hTv[tb])
            with kb.scope():
                NA1 = 4
                xs = [kb.sb(f"xs{i}", [128, D], F32) for i in range(NA1)]
                hb = [kb.sb(f"hb{i}", [128, D], BF16) for i in range(NA1)]
                junk = kb.sb("junk", [128, D], BF16)
                ssq = [kb.sb(f"ssq{i}", [128, 1], F32) for i in range(NA1)]
                rstd = [kb.sb(f"rstd{i}", [128, 1], F32) for i in range(NA1)]
                def a1_front(t):
                    i = t % NA1
                    kb.dma(xs[i][:], xsrc.t[t * 128:(t + 1) * 128, :], xsrc, xs[i])
                    kb.op(act, [xs[i]], [junk, ssq[i]], lambda e, i=i: e.activation(
                        out=junk[:], in_=xs[i][:], func=AF.Square, accum_out=ssq[i][:, 0:1]))
                    rsqrt(ssq[i], ssq[i][:], rstd[i], rstd[i][:], float(D) * EPS)
                    kb.op(dve, [xs[i], rstd[i]], [hb[i]], lambda e, i=i: e.tensor_scalar(
                        out=hb[i][:], in0=xs[i][:], scalar1=rstd[i][:, 0:1], scalar2=math.sqrt(D), op0=ALU.mult, op1=ALU.mult))

                def a1_back(t):
                    i = t % NA1
                    for half in range(2):
                        pb = PS[(2 * t + half) % 8]
                        pv = pb.t[:].bitcast(BF16)
                        for c8 in range(8):
                            c = half * 8 + c8
                            kb.op(pe, [hb[i], ident], [pb], lambda e, c=c, c8=c8, pv=pv, i=i: e.transpose(
                                pv[:, c8 * 128:(c8 + 1) * 128], hb[i][:, c * 128:(c + 1) * 128], ident[:]), inc=(c8 == 7))
                        kb.op(dve, [pb], [hTv[t // 4]], lambda e, half=half, pv=pv, t=t: e.tensor_copy(
                            out=hT[:, half * 8:(half + 1) * 8, t * 128:(t + 1) * 128],
                            in_=pv[:, 0:1024].rearrange("p (c k) -> p c k", c=8)))

                if l == 0:
                    a1_front(0)
                    a1_front(1)
                    for t in range(NT):
                        if t + 2 < NT:
                            a1_front(t + 2)
                        a1_back(t)
            with kb.scope():
                wst = [kb.sb(f"wst{i}", [128, DC, 128], F32) for i in range(2)]
                wb = [kb.sb(f"wb{i}", [128, DC, 128], BF16) for i in range(2)]
                stg = [kb.sb(f"stg{i}", [128, 2048], BF16) for i in range(2)]
                cs = [kb.sb(f"cs{i}", [64, 512], F32) for i in range(2)]
                sn = [kb.sb(f"sn{i}", [64, 512], F32) for i in range(2)]
                r1 = kb.sb("r1", [64, 512], F32)
                r2 = kb.sb("r2", [64, 512], F32)
                gtmp = [kb.sb(f"gtmp{i}", [128, 512], F32) for i in range(2)]
                gb_ = gcol[:, l, :].unsqueeze(2).to_broadcast([128, DC, 128])
                chunks = []
                for j in range(8):
                    chunks.append(("fm", O_AQ + j * 128, QA, j * 128, 128.0 ** -0.5))
                for j in range(2):
                    chunks.append(("fm", O_AK + j * 128, KA, j * 128, 1.0))
                for j in range(2):
                    chunks.append(("tm", O_AV + j * 128, VHA, j, 1.0))
                for j in range(8):
                    chunks.append(("fmg", O_AG + j * 128, GA, j * 128, 1.0))
                for j in range(3):
                    chunks.append(("fm", O_CQB + j * 128, CQ, j * 128, 1.0))
                for j in range(2):
                    chunks.append(("fm", O_CKV + j * 128, CKV, j * 128, 1.0))
                chunks.append(("kr", O_KR, KR, 0, 1.0))
                for j in range(4):
                    chunks.append(("fmg", O_GB + j * 128, GB, j * 128, 1.0))
                for j in range(4):
                    chunks.append(("fm", O_QC + j * 128, QC, j * 128, 0.125))
                for j in range(4):
                    chunks.append(("fm", O_KC + j * 128, KC, j * 128, 1.0))
                for j in range(4):
                    chunks.append(("tm", O_VC + j * 128, VHC, j, 1.0))
                for j in range(4):
                    chunks.append(("fmg", O_GC + j * 128, GC, j * 128, 1.0))

                def load_w(ci):
                    kind, c0, _, _, _ = chunks[ci]
                    i = ci % 2
                    ew = 64 if kind == "kr" else 128
                    kb.dma(wst[i][:, :, 0:ew],
                           win_in.t[l, :, c0:c0 + ew].rearrange("(c p) e -> p c e", p=128), win_in, wst[i])

                def cast_w(ci):
                    kind = chunks[ci][0]
                    i = ci % 2
                    E = dve if ci % 2 == 0 else pool
                    if kind != "kr":
                        kb.op(E, [wst[i], gcol], [wb[i]], lambda e: e.tensor_tensor(
                            out=wb[i][:], in0=wst[i][:], in1=gb_, op=ALU.mult))
                    else:
                        g64 = gcol[:, l, :].unsqueeze(2).to_broadcast([128, DC, 64])
                        g32 = gcol[:, l, :].unsqueeze(2).to_broadcast([128, DC, 32])
                        kb.op(dve, [wst[i], gcol], [wb[i]], lambda e: e.tensor_tensor(
                            out=wb[i][:, :, 0:64], in0=wst[i][:, :, 0:64], in1=g64, op=ALU.mult))
                        kb.op(dve, [wst[i], gcol], [wb[i]], lambda e: e.scalar_tensor_tensor(
                            out=wb[i][:, :, 64:96], in0=wst[i][:, :, 32:64], scalar=-1.0, in1=g32, op0=ALU.mult, op1=ALU.mult))
                        kb.op(dve, [wst[i], gcol], [wb[i]], lambda e: e.tensor_tensor(
                            out=wb[i][:, :, 96:128], in0=wst[i][:, :, 0:32], in1=g32, op=ALU.mult))

                load_w(0)
                cast_w(0)
                psi = 0
                evi = 0
                for ci, (kind, c0, dst, r0, scl) in enumerate(chunks):
                    if ci + 1 < len(chunks):
                        load_w(ci + 1)
                    i = ci % 2
                    w = wb[i]
                    if kind in ("fm", "fmg"):
                        for tb in range(NTB):
                            if tb == NTB // 2 and ci + 1 < len(chunks):
                                cast_w(ci + 1)
                            pb = PS[psi % 4]; psi += 1
                            for c in range(DC):
                                kb.op(pe, [w, hTv[tb]], [pb], lambda e, c=c, pb=pb, tb=tb: e.matmul(
                                    pb[:, 0:512], lhsT=w[:, c, :], rhs=hT[:, c, tb * 512:(tb + 1) * 512],
                                    start=(c == 0), stop=(c == DC - 1)), inc=(c == DC - 1))
                            sg_ = stg[(tb // 4) % 2]
                            so = (tb % 4) * 512
                            if kind == "fmg":
                                tg = gtmp[evi % 2]
                                kb.op(act, [pb], [tg], lambda e, pb=pb, tg=tg: e.activation(
                                    out=tg[:], in_=pb[:, 0:512], func=AF.Exp, scale=-1.0))
                                kb.op(act, [tg, onecol], [tg], lambda e, tg=tg: e.activation(
                                    out=tg[:], in_=tg[:], func=AF.Ln, bias=onecol[:, 0:1]))
                                kb.op(act, [tg], [tg], lambda e, tg=tg: e.activation(
                                    out=tg[:], in_=tg[:], func=AF.Exp, scale=-1.0))
                                kb.op(dve, [pb, tg], [sg_], lambda e, pb=pb, sg_=sg_, so=so, tg=tg: e.tensor_tensor(
                                    out=sg_[:, so:so + 512], in0=pb[:, 0:512], in1=tg[:], op=ALU.mult))
                            elif evi % 2 == 0:
                                kb.op(act, [pb], [sg_], lambda e, pb=pb, sg_=sg_, so=so: e.activation(
                                    out=sg_[:, so:so + 512], in_=pb[:, 0:512], func=AF.Copy, scale=float(scl)))
                            else:
                                kb.op(dve, [pb], [sg_], lambda e, pb=pb, sg_=sg_, so=so: e.tensor_single_scalar(
                                    out=sg_[:, so:so + 512], in_=pb[:, 0:512], scalar=float(scl), op=ALU.mult))
                            evi += 1
                            if tb % 4 == 3 or tb == NTB - 1:
                                t0 = (tb // 4) * 2048
                                n = (tb % 4 + 1) * 512
                                kb.dma(dst.t[r0:r0 + 128, t0:t0 + n], sg_[:, 0:n], sg_, dst)
                    elif kind == "tm":
                        for tb in range(NTB):
                            if tb == NTB // 2 and ci + 1 < len(chunks):
                                cast_w(ci + 1)
                            pb = PS[psi % 4]; psi += 1
                            for ti in range(4):
                                t = tb * 4 + ti
                                for c in range(DC):
                                    kb.op(pe, [w, hTv[tb]], [pb], lambda e, c=c, pb=pb, t=t, ti=ti: e.matmul(
                                        pb[:, ti * 128:(ti + 1) * 128], lhsT=hT[:, c, t * 128:(t + 1) * 128], rhs=w[:, c, :],
                                        start=(c == 0), stop=(c == DC - 1)), inc=(c == DC - 1))
                            sg_ = stg[(tb // 4) % 2]
                            so = (tb % 4) * 512
                            kb.op(dve, [pb], [sg_], lambda e, pb=pb, sg_=sg_, so=so: e.tensor_copy(
                                out=sg_[:, so:so + 512], in_=pb[:, 0:512]))
                            if tb % 4 == 3 or tb == NTB - 1:
                                n0 = (tb // 4) * 16
                                nn = (tb % 4 + 1) * 4
                                kb.dma(dst.t[r0, :, n0:n0 + nn, :],
                                       sg_[:, 0:nn * 128].rearrange("p (n e) -> p n e", e=128), sg_, dst)
                    else:
                        for tb in range(NTB):
                            if tb == NTB // 2 and ci + 1 < len(chunks):
                                cast_w(ci + 1)
                            pa = PS[psi % 4]; psi += 1
                            pb2 = PS[psi % 4]; psi += 1
                            j = tb % 2
                            kb.dma(cs[j][:], COS.t[:, tb * 512:(tb + 1) * 512], COS, cs[j])
                            kb.dma(sn[j][:], SIN.t[:, tb * 512:(tb + 1) * 512], SIN, sn[j])
                            for (pp, o) in ((pa, 0), (pb2, 64)):
                                for c in range(DC):
                                    kb.op(pe, [w, hTv[tb]], [pp], lambda e, c=c, pp=pp, o=o, tb=tb: e.matmul(
                                        pp[0:64, 0:512], lhsT=w[:, c, o:o + 64], rhs=hT[:, c, tb * 512:(tb + 1) * 512],
                                        start=(c == 0), stop=(c == DC - 1)), inc=(c == DC - 1))
                            kb.op(dve, [pa, cs[j]], [r1], lambda e, pa=pa, j=j: e.tensor_tensor(
                                out=r1[:], in0=pa[0:64, 0:512], in1=cs[j][:], op=ALU.mult))
                            kb.op(dve, [pb2, sn[j]], [r2], lambda e, pb2=pb2, j=j: e.tensor_tensor(
                                out=r2[:], in0=pb2[0:64, 0:512], in1=sn[j][:], op=ALU.mult))
                            sg_ = stg[(tb // 4) % 2]
                            so = (tb % 4) * 512
                            kb.op(dve, [r1, r2], [sg_], lambda e, sg_=sg_, so=so: e.tensor_tensor(
                                out=sg_[0:64, so:so + 512], in0=r1[:], in1=r2[:], op=ALU.add))
                            if tb % 4 == 3 or tb == NTB - 1:
                                t0 = (tb // 4) * 2048
                                n = (tb % 4 + 1) * 512
                                kb.dma(dst.t[0:64, t0:t0 + n], sg_[0:64, 0:n], sg_, dst)

        with kb.scope():
            QSC = math.sqrt(384.0) / math.sqrt(192.0)
            wq_st = kb.sb("wq_st", [128, 3, 768], F32)
            wkv_st = kb.sb("wkv_st", [128, 2, 1024], F32)
            wq = kb.sb("wq", [128, 3, 1024], BF16)
            wkv = kb.sb("wkv", [128, 2, 1024], BF16)
            kb.dma(wq_st[:], wuq_in.t[l].rearrange("(c p) e -> p c e", p=128), wuq_in, wq_st)
            kb.dma(wkv_st[:], wukv_in.t[l].rearrange("(c p) e -> p c e", p=128), wukv_in, wkv_st)
            for c in range(3):
                kb.op(dve, [wq_st, qncol], [wq], lambda e, c=c: e.tensor_scalar(
                    out=wq[:, c, 0:768], in0=wq_st[:, c, :], scalar1=qncol[:, l, c:c + 1], scalar2=QSC, op0=ALU.mult, op1=ALU.mult))
                for h in range(4):
                    b0 = h * 192 + 128
                    kb.op(dve, [wq_st, qncol], [wq], lambda e, c=c, h=h, b0=b0: e.tensor_scalar(
                        out=wq[:, c, 768 + h * 64:768 + h * 64 + 32], in0=wq_st[:, c, b0 + 32:b0 + 64],
                        scalar1=qncol[:, l, c:c + 1], scalar2=-QSC, op0=ALU.mult, op1=ALU.mult))
                    kb.op(dve, [wq_st, qncol], [wq], lambda e, c=c, h=h, b0=b0: e.tensor_scalar(
                        out=wq[:, c, 768 + h * 64 + 32:768 + h * 64 + 64], in0=wq_st[:, c, b0:b0 + 32],
                        scalar1=qncol[:, l, c:c + 1], scalar2=QSC, op0=ALU.mult, op1=ALU.mult))
            for c in range(2):
                for h in range(4):
                    kb.op(dve, [wkv_st, kvncol], [wkv], lambda e, c=c, h=h: e.tensor_scalar(
                        out=wkv[:, c, h * 128:(h + 1) * 128], in0=wkv_st[:, c, h * 256:h * 256 + 128],
                        scalar1=kvncol[:, l, c:c + 1], scalar2=16.0, op0=ALU.mult, op1=ALU.mult))
                    kb.op(dve, [wkv_st, kvncol], [wkv], lambda e, c=c, h=h: e.tensor_scalar(
                        out=wkv[:, c, 512 + h * 128:512 + (h + 1) * 128], in0=wkv_st[:, c, h * 256 + 128:h * 256 + 256],
                        scalar1=kvncol[:, l, c:c + 1], scalar2=16.0, op0=ALU.mult, op1=ALU.mult))
            NB0 = 3
            cq = [kb.sb(f"cq{i}", [128, 3, 512], BF16) for i in range(NB0)]
            ckv = [kb.sb(f"ckv{i}", [128, 2, 512], BF16) for i in range(NB0)]
            sq_ = [kb.sb(f"sq{i}", [128, 3, 512], BF16) for i in range(NB0)]
            sqk_ = [kb.sb(f"sqk{i}", [128, 2, 512], BF16) for i in range(NB0)]
            rq_ = [kb.sb(f"rq{i}", [128, 512], F32) for i in range(NB0)]
            rk_ = [kb.sb(f"rk{i}", [128, 512], F32) for i in range(NB0)]
            cqn_ = [kb.sb(f"cqn{i}", [128, 3, 512], BF16) for i in range(NB0)]
            ckvn_ = [kb.sb(f"ckvn{i}", [128, 2, 512], BF16) for i in range(NB0)]
            cs2 = kb.sb("cs2", [64, 512], F32)
            sn2 = kb.sb("sn2", [64, 512], F32)
            t1 = kb.sb("t1", [64, 512], F32)
            t2 = kb.sb("t2", [64, 512], F32)
            so_ = [kb.sb(f"so{i}", [128, 512], BF16) for i in range(4)]
            soi = 0
            psi = 0

            def ldc(tb):
                i = tb % NB0
                kb.dma(cq[i][:], CQ.t[:, tb * 512:(tb + 1) * 512].rearrange("(c p) s -> p c s", p=128), CQ, cq[i])
                kb.dma(ckv[i][:], CKV.t[:, tb * 512:(tb + 1) * 512].rearrange("(c p) s -> p c s", p=128), CKV, ckv[i])

            def prologue(tb):
                if tb + 1 < NTB:
                    ldc(tb + 1)
                i = tb % NB0
                sq, sqk, rq, rk, cqn, ckvn = sq_[i], sqk_[i], rq_[i], rk_[i], cqn_[i], ckvn_[i]
                kb.op(pool, [cq[i]], [sq], lambda e, i=i: e.tensor_tensor(out=sq[:], in0=cq[i][:], in1=cq[i][:], op=ALU.mult))
                kb.op(pool, [ckv[i]], [sqk], lambda e, i=i: e.tensor_tensor(out=sqk[:], in0=ckv[i][:], in1=ckv[i][:], op=ALU.mult))
                pA = PS[psi_[0] % 8]; psi_[0] += 1
                for c in range(3):
                    kb.op(pe, [ones, sq], [pA], lambda e, c=c, pA=pA: e.matmul(
                        pA[:, 0:512], lhsT=ones[:], rhs=sq[:, c, :], start=(c == 0), stop=(c == 2)), inc=(c == 2))
                rsqrt(pA, pA[:, 0:512], rq, rq[:], 384.0 * EPS)
                pB = PS[psi_[0] % 8]; psi_[0] += 1
                for c in range(2):
                    kb.op(pe, [ones, sqk], [pB], lambda e, c=c, pB=pB: e.matmul(
                        pB[:, 0:512], lhsT=ones[:], rhs=sqk[:, c, :], start=(c == 0), stop=(c == 1)), inc=(c == 1))
                rsqrt(pB, pB[:, 0:512], rk, rk[:], 256.0 * EPS)
                kb.op(pool, [cq[i], rq], [cqn], lambda e, i=i: e.tensor_tensor(
                    out=cqn[:], in0=cq[i][:], in1=rq[:].unsqueeze(1).to_broadcast([128, 3, 512]), op=ALU.mult))
                kb.op(pool, [ckv[i], rk], [ckvn], lambda e, i=i: e.tensor_tensor(
                    out=ckvn[:], in0=ckv[i][:], in1=rk[:].unsqueeze(1).to_broadcast([128, 2, 512]), op=ALU.mult))

            psi_ = [0]
            ldc(0)
            prologue(0)
            if NTB > 1:
                prologue(1)
            for tb in range(NTB):
                if tb + 2 < NTB:
                    prologue(tb + 2)
                i = tb % NB0
                cqn, ckvn = cqn_[i], ckvn_[i]
                psi = psi_[0]
                tsl = slice(tb * 512, (tb + 1) * 512)
                kb.dma(cs2[:], COS.t[:, tsl], COS, cs2)
                kb.dma(sn2[:], SIN.t[:, tsl], SIN, sn2)
                for h in range(4):
                    pq = PS[psi % 8]; psi += 1
                    for c in range(3):
                        kb.op(pe, [wq, cqn], [pq], lambda e, c=c, h=h, pq=pq, i=i: e.matmul(
                            pq[:, 0:512], lhsT=wq[:, c, h * 192:h * 192 + 128], rhs=cqn[:, c, :],
                            start=(c == 0), stop=(c == 2)), inc=(c == 2))
                    s_ = so_[soi % 4]; soi += 1
                    kb.op(act, [pq], [s_], lambda e, pq=pq, s_=s_: e.activation(out=s_[:], in_=pq[:, 0:512], func=AF.Copy))
                    kb.dma(QBN.t[h, :, tsl], s_[:], s_, QBN)
                    pr1 = PS[psi % 8]; psi += 1
                    pr2 = PS[psi % 8]; psi += 1
                    for (pp, o) in ((pr1, h * 192 + 128), (pr2, 768 + h * 64)):
                        for c in range(3):
                            kb.op(pe, [wq, cqn], [pp], lambda e, c=c, pp=pp, o=o, i=i: e.matmul(
                                pp[0:64, 0:512], lhsT=wq[:, c, o:o + 64], rhs=cqn[:, c, :],
                                start=(c == 0), stop=(c == 2)), inc=(c == 2))
                    kb.op(dve, [pr1, cs2], [t1], lambda e, pr1=pr1: e.tensor_tensor(
                        out=t1[:], in0=pr1[0:64, 0:512], in1=cs2[:], op=ALU.mult))
                    kb.op(dve, [pr2, sn2], [t2], lambda e, pr2=pr2: e.tensor_tensor(
                        out=t2[:], in0=pr2[0:64, 0:512], in1=sn2[:], op=ALU.mult))
                    s_ = so_[soi % 4]; soi += 1
                    kb.op(dve, [t1, t2], [s_], lambda e, s_=s_: e.tensor_tensor(out=s_[0:64, :], in0=t1[:], in1=t2[:], op=ALU.add))
                    kb.dma(QBR.t[h, :, tsl], s_[0:64, :], s_, QBR)
                    pk = PS[psi % 8]; psi += 1
                    for c in range(2):
                        kb.op(pe, [wkv, ckvn], [pk], lambda e, c=c, h=h, pk=pk, i=i: e.matmul(
                            pk[:, 0:512], lhsT=wkv[:, c, h * 128:(h + 1) * 128], rhs=ckvn[:, c, :],
                            start=(c == 0), stop=(c == 1)), inc=(c == 1))
                    s_ = so_[soi % 4]; soi += 1
                    kb.op(act, [pk], [s_], lambda e, pk=pk, s_=s_: e.activation(out=s_[:], in_=pk[:, 0:512], func=AF.Copy))
                    kb.dma(KN.t[h, :, tsl], s_[:], s_, KN)
                for ti in range(4):
                    pvv = PS[psi % 8]; psi += 1
                    for c in range(2):
                        kb.op(pe, [wkv, ckvn], [pvv], lambda e, c=c, ti=ti, pvv=pvv, i=i: e.matmul(
                            pvv[:, 0:512], lhsT=ckvn[:, c, ti * 128:(ti + 1) * 128], rhs=wkv[:, c, 512:1024],
                            start=(c == 0), stop=(c == 1)), inc=(c == 1))
                    s_ = so_[soi % 4]; soi += 1
                    if ti % 2 == 0:
                        kb.op(dve, [pvv], [s_], lambda e, pvv=pvv, s_=s_: e.tensor_copy(out=s_[:], in_=pvv[:, 0:512]))
                    else:
                        kb.op(act, [pvv], [s_], lambda e, pvv=pvv, s_=s_: e.activation(out=s_[:], in_=pvv[:, 0:512], func=AF.Copy))
                    kb.dma(VHB.t[:, :, tb * 4 + ti, :].rearrange("h p e -> p h e"),
                           s_[:].rearrange("p (h e) -> p h e", e=128), s_, VHB)
                psi_[0] = psi

        with kb.scope():
            NPT = 12
            ptbig = kb.sb("ptbig", [128, NPT * 512], BF16)
            pT = [Buf(ptbig.t[:, i * 512:(i + 1) * 512], f"pT{i}") for i in range(NPT)]
            pdsb = kb.sb("pdsb", [128, 512], F32)
            wcs = [kb.sb(f"wcs{i}", [128, D], F32) for i in range(2)]
            wcb = [kb.sb(f"wcb{i}", [128, D], BF16) for i in range(2)]

            def wo_convert(c):
                if c >= 1:
                    kb.dma(WOB.t[(c - 1) * 128:c * 128, :], wcb[(c - 1) % 2][:], wcb[(c - 1) % 2], WOB)
                if c < DC:
                    kb.dma(wcs[c % 2][:], wout_in.t[l, c * 128:(c + 1) * 128, :], wout_in, wcs[c % 2])
                    kb.op(pool, [wcs[c % 2]], [wcb[c % 2]], lambda e: e.tensor_copy(out=wcb[c % 2][:], in_=wcs[c % 2][:]))
            rd = kb.sb("rd", [128, 512], F32)
            rd2 = kb.sb("rd2", [128, 512], F32)
            oa = kb.sb("oa", [128, 512], F32)
            ob = kb.sb("ob", [128, 512], F32)
            osq = kb.sb("osq", [128, 512], BF16)
            rs = kb.sb("rs", [128, 512], F32)
            yo = [kb.sb(f"yo{i}", [128, 512], BF16) for i in range(2)]
            st = {"pt": 0, "g": 0, "y": 0}

            ghd = [kb.sb(f"ghd{i}", [128, S], BF16) for i in range(2)]

            def gate_load(slot, Gsrc, grow):
                kb.dma(ghd[slot][:], Gsrc.t[grow:grow + 128, :], Gsrc, ghd[slot])

            def gate_and_store(o_buf, slot, yrow, q0, n=512):
                g = ghd[slot]
                y = yo[st["y"] % 2]; st["y"] += 1
                kb.op(dve, [g, o_buf], [y], lambda e: e.tensor_tensor(out=y[:, 0:n], in0=g[:, q0:q0 + n], in1=o_buf[:, 0:n], op=ALU.mult))
                kb.dma(YT.t[yrow:yrow + 128, q0:q0 + n], y[:, 0:n], y, YT)

            deferred = []

            def defer(k, fn):
                deferred.append([k, fn])

            def run_deferred(flush=False):
                while True:
                    for d in deferred:
                        d[0] -= 1
                    due = [d for d in deferred if d[0] <= 0]
                    for d in due:
                        deferred.remove(d)
                        d[1]()
                    if not flush or not deferred:
                        break

            def run_pipeline(steps, LA=2):
                n = len(steps)
                inflight = []
                for i_ in range(n + LA):
                    if i_ < n:
                        qk_fn, sbanks, pv_fn, ncols = steps[i_]
                        qk_fn()
                        pts = []
                        if len(sbanks) == 2:
                            if st["pt"] % 2:
                                st["pt"] += 1
                            i0 = st["pt"] % NPT
                            st["pt"] += 2
                            b0 = PS.index(sbanks[0])
                            assert PS.index(sbanks[1]) == b0 + 1 and ncols == 512
                            kb.op(act, list(sbanks), [pT[i0], pT[i0 + 1]], lambda e, i0=i0, b0=b0: e.activation(
                                out=ptbig.t[:, i0 * 512:(i0 + 2) * 512], in_=psbig[:, b0 * 512:(b0 + 2) * 512], func=AF.Exp))
                            pts = [pT[i0], pT[i0 + 1]]
                        else:
                            for sbk in sbanks:
                                p_ = pT[st["pt"] % NPT]; st["pt"] += 1
                                kb.op(act, [sbk], [p_], lambda e, sbk=sbk, p_=p_, ncols=ncols: e.activation(
                                    out=p_[:, 0:ncols], in_=sbk[:, 0:ncols], func=AF.Exp))
                                pts.append(p_)
                        inflight.append((pv_fn, pts))
                    if i_ - LA >= 0:
                        pv_fn, pts = inflight[i_ - LA]
                        pv_fn(pts)
                    run_deferred()
                run_deferred(flush=True)

            with kb.scope():
                knb = [kb.sb(f"knb{i}", [128, S], BF16) for i in range(2)]
                vhb = [kb.sb(f"vhb{i}", [128, NT, 128], BF16) for i in range(2)]
                krb2 = kb.sb("krb2", [128, S], BF16)
                qn = [kb.sb(f"qn{i}", [128, 512], BF16) for i in range(2)]
                qr2 = [kb.sb(f"qr2{i}", [128, 512], BF16) for i in range(2)]
                kb.dma(krb2[0:64, :], KR.t, KR, krb2)
                kb.dma(krb2[64:128, :], KR.t, KR, krb2)
                SP_ = [(PS[0], PS[1]), (PS[2], PS[3])]
                POs = [PS[4], PS[7]]
                pd = PS[5]
                aux = PS[6]
                steps = []
                pend = []
                cnt = {"s": 0}
                jobs = [(h, qb) for h in range(4) for qb in range(NTB)]
                NKP = NT // 2

                def load_head(h):
                    kb.dma(knb[h % 2][:], KN.t[h], KN, knb[h % 2])
                    kb.dma(vhb[h % 2][:], VHB.t[h], VHB, vhb[h % 2])
                    if h == 0:
                        gate_load(0, GB, 0)

                def load_q(ji):
                    h, qb = jobs[ji]
                    kb.dma(qn[ji % 2][:], QBN.t[h, :, qb * 512:(qb + 1) * 512], QBN, qn[ji % 2])
                    kb.dma(qr2[ji % 2][0:64, :], QBR.t[h, :, qb * 512:(qb + 1) * 512], QBR, qr2[ji % 2])
                    kb.dma(qr2[ji % 2][64:128, :], QBR.t[h, :, qb * 512:(qb + 1) * 512], QBR, qr2[ji % 2])

                for ji, (h, qb) in enumerate(jobs):
                    po = POs[ji % 2]
                    for kp in range(NKP):
                        sb2 = SP_[cnt["s"] % 2]; cnt["s"] += 1

                        def qk(ji=ji, h=h, qb=qb, kp=kp, sb2=sb2):
                            if kp == 0:
                                if qb == 0 and h == 0:
                                    load_head(0)
                                if ji == 0:
                                    load_q(0)
                                if ji + 1 < len(jobs):
                                    load_q(ji + 1)
                            if kp == 2 and qb == 0 and h + 1 < 4:
                                load_head(h + 1)
                            if kp == 0 and qb == NTB - 1 and h + 1 < 4:
                                gate_load((h + 1) % 2, GB, (h + 1) * 128)
                            kA, kB = 2 * kp, 2 * kp + 1
                            q_ = qn[ji % 2]
                            r_ = qr2[ji % 2]
                            for kk, sbk in ((kA, sb2[0]), (kB, sb2[1])):
                                kb.op(pe, [knb[h % 2], q_], [sbk], lambda e, kk=kk, sbk=sbk: e.matmul(
                                    sbk[:, 0:512], lhsT=knb[h % 2][:, kk * 128:(kk + 1) * 128], rhs=q_[:],
                                    start=True, stop=False), inc=False)
                            kb.op(pe, [krb2, r_], [sb2[0]], lambda e: e.matmul(
                                sb2[0][:, 0:512], lhsT=krb2[0:64, kA * 128:(kA + 1) * 128], rhs=r_[0:64, :],
                                start=False, stop=True, tile_position=(0, 0)), inc=False)
                            kb.op(pe, [krb2, r_], [sb2[1]], lambda e: e.matmul(
                                sb2[1][:, 0:512], lhsT=krb2[64:128, kB * 128:(kB + 1) * 128], rhs=r_[64:128, :],
                                start=False, stop=True, tile_position=(64, 0)))

                        def pv(pts, ji=ji, h=h, qb=qb, kp=kp, po=po):
                            for idx, kk in enumerate((2 * kp, 2 * kp + 1)):
                                p_ = pts[idx]
                                kb.op(pe, [vhb[h % 2], p_], [po], lambda e, kk=kk, p_=p_: e.matmul(
                                    po[:, 0:512], lhsT=vhb[h % 2][:, kk, :], rhs=p_[:], start=(kk == 0), stop=(kk == NT - 1)))
                                pend.append(p_)
                            if kp % 2 == 1:
                                for j, pj in enumerate(pend):
                                    kb.op(pe, [ones, pj], [pd], lambda e, j=j, pj=pj: e.matmul(
                                        pd[32 * j:32 * j + 32, 0:512], lhsT=ones[:, 0:32], rhs=pj[:], start=(kp == 1),
                                        stop=(kp == NKP - 1), tile_position=(0, 32 * j), skip_group_check=True), inc=(j == 3))
                                del pend[:]
                            if kp == NKP - 1:
                                kb.op(dve, [pd], [pdsb], lambda e: e.tensor_copy(out=pdsb[:], in_=pd[:, 0:512]))

                                def ep1():
                                    kb.op(pe, [sel32, pdsb], [aux], lambda e: e.matmul(
                                        aux[:, 0:512], lhsT=sel32[:], rhs=pdsb[:], start=True, stop=True))
                                    recip(aux, aux[:, 0:512], rd, rd[:])

                                def ep2():
                                    kb.op(dve, [po, rd], [oa], lambda e: e.tensor_tensor(out=oa[:], in0=po[:, 0:512], in1=rd[:], op=ALU.mult))
                                    gate_and_store(oa, h % 2, 1024 + h * 128, qb * 512)
                                defer(1, ep1)
                                defer(2, ep2)

                        steps.append((qk, list(sb2), pv, 512))
                run_pipeline(steps)

            with kb.scope():
                ktl = [[kb.sb(f"ktl{i}{c}", [68, S], BF16) for c in range(2)] for i in range(2)]
                ktu = [[kb.sb(f"ktu{i}{c}", [68, S], BF16) for c in range(2)] for i in range(2)]
                vhc = [kb.sb(f"vhc{i}", [128, NT, 128], BF16) for i in range(2)]
                qtc = [[kb.sb(f"qtc{i}{c}", [68, 512], BF16) for c in range(2)] for i in range(2)]
                SBk = [(PS[0], PS[1]), (PS[2], PS[3])]
                po1, po2, pdd = PS[4], PS[5], PS[6]
                pend = []
                jobs = [(h, qb) for h in range(4) for qb in range(NTB)]
                cnt = {"s": 0}

                def load_head_d(h):
                    i = h % 2
                    for c in range(2):
                        r0 = h * 128 + c * 64
                        kb.dma(ktl[i][c][0:64, :], KC.t[r0:r0 + 64, :], KC, ktl[i][c])
                        kb.dma(ktl[i][c][64:68, :], KAUG.t[h, 0], KAUG, ktl[i][c])
                        kb.dma(ktu[i][c][0:64, :], KC.t[r0:r0 + 64, :], KC, ktu[i][c])
                        kb.dma(ktu[i][c][64:68, :], KAUG.t[h, 1], KAUG, ktu[i][c])
                    kb.dma(vhc[i][:], VHC.t[h], VHC, vhc[i])
                    if h == 0:
                        gate_load(0, GC, 0)

                def load_q_d(ji):
                    h, qb = jobs[ji]
                    for c in range(2):
                        r0 = h * 128 + c * 64
                        kb.dma(qtc[ji % 2][c][0:64, :], QC.t[r0:r0 + 64, qb * 512:(qb + 1) * 512], QC, qtc[ji % 2][c])
                        kb.dma(qtc[ji % 2][c][64:68, :], QAUG.t[:, qb * 512:(qb + 1) * 512], QAUG, qtc[ji % 2][c])

                steps = []
                for ji, (h, qb) in enumerate(jobs):
                    for kbk in range(NT):
                        sb2 = SBk[cnt["s"] % 2]; cnt["s"] += 1

                        def qk(ji=ji, h=h, qb=qb, kbk=kbk, sb2=sb2):
                            if kbk == 0:
                                if qb == 0 and h == 0:
                                    load_head_d(0)
                                if ji == 0:
                                    load_q_d(0)
                                if ji + 1 < len(jobs):
                                    load_q_d(ji + 1)
                            if kbk == 3 and qb == 0 and h + 1 < 4:
                                load_head_d(h + 1)
                            if kbk == 0 and qb == NTB - 1 and h + 1 < 4:
                                gate_load((h + 1) % 2, GC, (h + 1) * 128)
                            i = h % 2
                            ks = slice(kbk * 128, (kbk + 1) * 128)
                            for c in range(2):
                                sbk = sb2[c]
                                q_ = qtc[ji % 2][c]
                                if kbk < 4 * qb or kbk >= 4 * qb + 4:
                                    kt = ktl[i][c] if kbk < 4 * qb else ktu[i][c]
                                    kb.op(pe, [kt, q_], [sbk], lambda e, kt=kt, sbk=sbk, q_=q_: e.matmul(
                                        sbk[:, 0:512], lhsT=kt[:, ks], rhs=q_[:], start=True, stop=True))
                                else:
                                    d_ = kbk - 4 * qb
                                    if d_ > 0:
                                        kb.op(pe, [ktu[i][c], q_], [sbk], lambda e, sbk=sbk, q_=q_, c=c, d_=d_: e.matmul(
                                            sbk[:, 0:d_ * 128], lhsT=ktu[i][c][:, ks], rhs=q_[:, 0:d_ * 128],
                                            start=True, stop=True, skip_group_check=True), inc=False)
                                    if d_ < 3:
                                        kb.op(pe, [ktl[i][c], q_], [sbk], lambda e, sbk=sbk, q_=q_, c=c, d_=d_: e.matmul(
                                            sbk[:, (d_ + 1) * 128:512], lhsT=ktl[i][c][:, ks], rhs=q_[:, (d_ + 1) * 128:512],
                                            start=True, stop=True, skip_group_check=True), inc=False)
                                    kb.op(pe, [ktl[i][c], q_], [sbk], lambda e, sbk=sbk, q_=q_, c=c, d_=d_: e.matmul(
                                        sbk[:, d_ * 128:(d_ + 1) * 128], lhsT=ktl[i][c][:, ks], rhs=q_[:, d_ * 128:(d_ + 1) * 128],
                                        start=True, stop=False, skip_group_check=True), inc=False)
                                    kb.op(pe, [ident, CH[h]], [sbk], lambda e, sbk=sbk, d_=d_: e.matmul(
                                        sbk[:, d_ * 128:(d_ + 1) * 128], lhsT=ident[:], rhs=CH[h][:],
                                        start=False, stop=True, skip_group_check=True))

                        def pv(pts, ji=ji, h=h, qb=qb, kbk=kbk):
                            i = h % 2
                            first, last = (kbk == 0), (kbk == NT - 1)
                            for (p_, po) in ((pts[0], po1), (pts[1], po2)):
                                kb.op(pe, [vhc[i], p_], [po], lambda e, p_=p_, po=po: e.matmul(
                                    po[:, 0:512], lhsT=vhc[i][:, kbk, :], rhs=p_[:], start=first, stop=last))
                                pend.append(p_)
                            if kbk % 2 == 1:
                                for j, pj in enumerate(pend):
                                    kb.op(pe, [ones, pj], [pdd], lambda e, j=j, pj=pj: e.matmul(
                                        pdd[32 * j:32 * j + 32, 0:512], lhsT=ones[:, 0:32], rhs=pj[:], start=(kbk == 1),
                                        stop=last, tile_position=(0, 32 * j), skip_group_check=True), inc=(j == 3))
                                del pend[:]
                            if last:
                                aux = PS[7]
                                kb.op(dve, [pdd], [pdsb], lambda e: e.tensor_copy(out=pdsb[:], in_=pdd[:, 0:512]))
                                kb.op(dve, [po1], [oa], lambda e: e.tensor_copy(out=oa[:], in_=po1[:, 0:512]))
                                kb.op(dve, [po2], [ob], lambda e: e.tensor_copy(out=ob[:], in_=po2[:, 0:512]))

                                def ep1():
                                    kb.op(pe, [selA, pdsb], [aux], lambda e: e.matmul(
                                        aux[:, 0:512], lhsT=selA[:], rhs=pdsb[:], start=True, stop=True))
                                    recip(aux, aux[:, 0:512], rd, rd[:])

                                def ep2():
                                    kb.op(pe, [selB, pdsb], [aux], lambda e: e.matmul(
                                        aux[:, 0:512], lhsT=selB[:], rhs=pdsb[:], start=True, stop=True))
                                    recip(aux, aux[:, 0:512], rd2, rd2[:])

                                def ep3():
                                    kb.op(dve, [oa, rd], [oa], lambda e: e.tensor_tensor(out=oa[:], in0=oa[:], in1=rd[:], op=ALU.mult))
                                    kb.op(dve, [ob, rd2], [ob], lambda e: e.tensor_tensor(out=ob[:], in0=ob[:], in1=rd2[:], op=ALU.mult))
                                    kb.op(dve, [ob, nlam, oa], [oa], lambda e: e.scalar_tensor_tensor(
                                        out=oa[:], in0=ob[:], scalar=nlam[:, l:l + 1], in1=oa[:], op0=ALU.mult, op1=ALU.add))
                                    kb.op(act, [oa], [osq], lambda e: e.activation(out=osq[:], in_=oa[:], func=AF.Square))

                                def ep4():
                                    kb.op(pe, [ones, osq], [aux], lambda e: e.matmul(
                                        aux[:, 0:512], lhsT=ones[:], rhs=osq[:], start=True, stop=True))
                                    rsqrt(aux, aux[:, 0:512], rs, rs[:], 128.0 * EPS)

                                def ep5():
                                    kb.op(dve, [oa, rs, subg], [ob], lambda e: e.scalar_tensor_tensor(
                                        out=ob[:], in0=oa[:], scalar=subg[:, l:l + 1], in1=rs[:], op0=ALU.mult, op1=ALU.mult))
                                    gate_and_store(ob, h % 2, 1536 + h * 128, qb * 512)
                                dg_ = 2 if NT >= 24 else 1
                                defer(1, ep1)
                                defer(1 + dg_, ep2)
                                defer(1 + 2 * dg_, ep3)
                                defer(1 + 3 * dg_, ep4)
                                defer(1 + 4 * dg_, ep5)

                        steps.append((qk, list(sb2), pv, 512))
                run_pipeline(steps)

            with kb.scope():
                ktas = [kb.sb(f"kta{i}", [128, S], BF16) for i in range(2)]
                vhas = [kb.sb(f"vha{i}", [128, NT, 128], BF16) for i in range(2)]
                qta = [kb.sb(f"qta{i}", [128, S], BF16) for i in range(2)]
                SB_ = [PS[0], PS[1], PS[2]]
                OB_ = [(PS[3], PS[4]), (PS[5], PS[6])]
                cnt = {"s": 0}
                steps = []
                NG = NTB
                for hh in range(8):
                    kvh = hh // 4
                    kta = ktas[kvh]
                    vha = vhas[kvh]
                    for kbk in range(NT):
                        sbk = SB_[cnt["s"] % 3]; cnt["s"] += 1
                        qs0 = max(kbk - 1, 0)
                        qs1 = min(kbk + 1, NT - 1)
                        ncols = (qs1 - qs0 + 1) * 128
                        boff = (qs0 - (kbk - 1)) * 128

                        def qk(hh=hh, kvh=kvh, kbk=kbk, sbk=sbk, qs0=qs0, ncols=ncols, boff=boff, kta=kta, vha=vha):
                            gs = hh * NT + kbk
                            if gs % (NT // 2) == 0:
                                wo_convert(gs // (NT // 2))
                            if kbk == 0:
                                if hh == 0:
                                    for kv_ in range(2):
                                        kb.dma(ktas[kv_][:], KA.t[kv_ * 128:(kv_ + 1) * 128, :], KA, ktas[kv_])
                                        kb.dma(vhas[kv_][:], VHA.t[kv_], VHA, vhas[kv_])
                                if hh == 0:
                                    kb.dma(qta[0][:], QA.t[0:128, :], QA, qta[0])
                                    gate_load(0, GA, 0)
                                if hh + 1 < 8:
                                    kb.dma(qta[(hh + 1) % 2][:], QA.t[(hh + 1) * 128:(hh + 2) * 128, :], QA, qta[(hh + 1) % 2])
                            if kbk == 3 and hh + 1 < 8:
                                gate_load((hh + 1) % 2, GA, (hh + 1) * 128)
                            q_ = qta[hh % 2]
                            kb.op(pe, [kta, q_], [sbk], lambda e: e.matmul(
                                sbk[:, 0:ncols], lhsT=kta[:, kbk * 128:(kbk + 1) * 128], rhs=q_[:, qs0 * 128:qs0 * 128 + ncols],
                                start=True, stop=False), inc=False)
                            kb.op(pe, [ident, BH[hh]], [sbk], lambda e: e.matmul(
                                sbk[:, 0:ncols], lhsT=ident[:], rhs=BH[hh][:, boff:boff + ncols], start=False, stop=True))

                        def pv(pts, hh=hh, kbk=kbk, qs0=qs0, qs1=qs1, vha=vha):
                            p_ = pts[0]
                            for qs in range(qs0, qs1 + 1):
                                G = qs // 4
                                po, pd = OB_[G % 2]
                                col = (qs % 4) * 128
                                first_in_epoch = (qs == 4 * G) and (kbk == max(4 * G - 1, 0))
                                pc = (qs - qs0) * 128
                                kb.op(pe, [vha, p_], [po], lambda e, po=po, col=col, pc=pc, f=first_in_epoch: e.matmul(
                                    po[:, col:col + 128], lhsT=vha[:, kbk, :], rhs=p_[:, pc:pc + 128],
                                    start=f, stop=True, skip_group_check=True), inc=False)
                                kb.op(pe, [ones, p_], [pd], lambda e, pd=pd, col=col, pc=pc, f=first_in_epoch: e.matmul(
                                    pd[:, col:col + 128], lhsT=ones[:], rhs=p_[:, pc:pc + 128],
                                    start=f, stop=True, skip_group_check=True))
                            for G in range(NG):
                                if kbk == min(4 * G + 4, NT - 1):
                                    po, pd = OB_[G % 2]

                                    def ep1(pd=pd):
                                        recip(pd, pd[:, 0:512], rd, rd[:], esink, esink[:, l * 8 + hh:l * 8 + hh + 1])

                                    def ep2(po=po, G=G):
                                        kb.op(dve, [po, rd], [oa], lambda e, po=po: e.tensor_tensor(out=oa[:], in0=po[:, 0:512], in1=rd[:], op=ALU.mult))
                                        gate_and_store(oa, hh % 2, hh * 128, G * 512)
                                    defer(1, ep1)
                                    defer(2, ep2)

                        steps.append((qk, [sbk], pv, ncols))
                run_pipeline(steps)
                wo_convert(DC)

        with kb.scope():
            wo_t = kb.sb("wo", [128, DC, D], BF16)
            for q4 in range(4):
                kb.dma(wo_t[:, q4 * 4:(q4 + 1) * 4, :],
                       WOB.t[q4 * 512:(q4 + 1) * 512, :].rearrange("(c p) d -> p c d", p=128), WOB, wo_t)
            yt = [kb.sb(f"yt{i}", [128, DC, 512], BF16) for i in range(2)]
            xr = [kb.sb(f"xr{i}", [128, D], F32) for i in range(2)]
            xn = [kb.sb(f"xn{i}", [128, D], F32) for i in range(2)]
            last = (l == NL - 1)
            jk = kb.sb("jk", [128, D], BF16)
            ss2 = [kb.sb(f"ss2{i}", [128, 1], F32) for i in range(2)]
            rr2 = [kb.sb(f"rr2{i}", [128, 1], F32) for i in range(2)]
            if last:
                fg = kb.sb("fg", [128, D], F32)
                kb.dma(fg[:], fnorm_in.t.partition_broadcast(128), fnorm_in, fg)
            else:
                hb2 = [kb.sb(f"hb2{i}", [128, D], BF16) for i in range(2)]
                hst = [kb.sb(f"hst{i}", [128, DC, 256], BF16) for i in range(2)]
            psi_c = [0]

            def post_a(t):
                i = t % 2
                xo = xn[i]
                kb.op(act, [xo], [jk, ss2[i]], lambda e: e.activation(
                    out=jk[:], in_=xo[:], func=AF.Square, accum_out=ss2[i][:, 0:1]))
                rsqrt(ss2[i], ss2[i][:], rr2[i], rr2[i][:], float(D) * EPS)
                kb.op(dve, [xo, rr2[i]], [hb2[i]], lambda e: e.tensor_scalar(
                    out=hb2[i][:], in0=xo[:], scalar1=rr2[i][:, 0:1], scalar2=math.sqrt(D), op0=ALU.mult, op1=ALU.mult))

            def post(t):
                i = t % 2
                hs = hst[(t // 2) % 2]
                for half in range(2):
                    pb = PS[psi_c[0] % 8]; psi_c[0] += 1
                    pv = pb.t[:].bitcast(BF16)
                    for c8 in range(8):
                        c = half * 8 + c8
                        kb.op(pe, [hb2[i], ident], [pb], lambda e, c=c, c8=c8, pv=pv: e.transpose(
                            pv[:, c8 * 128:(c8 + 1) * 128], hb2[i][:, c * 128:(c + 1) * 128], ident[:]), inc=(c8 == 7))
                    o_ap = hs[:, half * 8:(half + 1) * 8, (t % 2) * 128:(t % 2 + 1) * 128]
                    i_ap = pv[:, 0:1024].rearrange("p (c k) -> p c k", c=8)
                    if half == 0:
                        kb.op(dve, [pb], [hs], lambda e: e.tensor_copy(out=o_ap, in_=i_ap))
                    else:
                        kb.op(act, [pb], [hs], lambda e: e.activation(out=o_ap, in_=i_ap, func=AF.Copy))
                if t % 2 == 1:
                    kb.dma(HT1.t[:, (t - 1) * 128:(t + 1) * 128].rearrange("(c p) s -> p c s", p=128), hs[:], hs, HT1)

            def ld_y(tb):
                kb.dma(yt[tb % 2][:], YT.t[:, tb * 512:(tb + 1) * 512].rearrange("(c p) s -> p c s", p=128), YT, yt[tb % 2])

            def ld_x(t):
                kb.dma(xr[t % 2][:], xsrc.t[t * 128:(t + 1) * 128, :], xsrc, xr[t % 2])

            ld_y(0)
            ld_x(0)
            for tb in range(NTB):
                if tb + 1 < NTB:
                    ld_y(tb + 1)
                for ti in range(4):
                    t = tb * 4 + ti
                    if t + 1 < NT:
                        ld_x(t + 1)
                    xo = xn[t % 2]
                    for dg in range(4):
                        pb = PS[psi_c[0] % 8]; psi_c[0] += 1
                        for c in range(DC):
                            kb.op(pe, [yt[tb % 2], wo_t], [pb], lambda e, c=c, pb=pb, ti=ti, dg=dg, tb=tb: e.matmul(
                                pb[:, 0:512], lhsT=yt[tb % 2][:, c, ti * 128:(ti + 1) * 128], rhs=wo_t[:, c, dg * 512:(dg + 1) * 512],
                                start=(c == 0), stop=(c == DC - 1)), inc=(c == DC - 1))
                        kb.op(dve, [pb, xr[t % 2]], [xo], lambda e, pb=pb, dg=dg, t=t, xo=xo: e.tensor_tensor(
                            out=xo[:, dg * 512:(dg + 1) * 512], in0=pb[:, 0:512], in1=xr[t % 2][:, dg * 512:(dg + 1) * 512], op=ALU.add))
                    if not last:
                        kb.dma(X1.t[t * 128:(t + 1) * 128, :], xo[:], xo, X1)
                        post_a(t)
                        if t >= 1:
                            post(t - 1)
                        if t == NT - 1:
                            post(t)
                    else:
                        i = t % 2
                        kb.op(act, [xo], [jk, ss2[i]], lambda e, i=i, xo=xo: e.activation(
                            out=jk[:], in_=xo[:], func=AF.Square, accum_out=ss2[i][:, 0:1]))
                        rsqrt(ss2[i], ss2[i][:], rr2[i], rr2[i][:], float(D) * EPS)
                        kb.op(pool, [rr2[i]], [rr2[i]], lambda e, i=i: e.tensor_single_scalar(
                            out=rr2[i][:], in_=rr2[i][:], scalar=math.sqrt(D), op=ALU.mult))
                        kb.op(dve, [xo, rr2[i], fg], [xo], lambda e, i=i, xo=xo: e.scalar_tensor_tensor(
                            out=xo[:], in0=xo[:], scalar=rr2[i][:, 0:1], in1=fg[:], op0=ALU.mult, op1=ALU.mult))
                        kb.dma(out_d.t[t * 128:(t + 1) * 128, :], xo[:], xo, out_d)
    kb.barrier()


_NC_CACHE = {}


def _get_nc(S, NL, dbg=False):
    key = (S, NL, dbg)
    if key not in _NC_CACHE:
        _NC_CACHE[key] = build(S, NL, dbg)
    return _NC_CACHE[key]


def kernel(x, positions, norm_g, w_in, swa_sink, mla_q_norm, mla_w_uq, mla_kv_norm,
           mla_w_ukv, diff_lambda, diff_subln, w_out, final_norm):
    x = np.asarray(x, dtype=np.float32)
    B, S, _ = x.shape
    nc = _get_nc(S, 2)
    shared = {
        "positions": np.ascontiguousarray(np.asarray(positions, dtype=np.int32)),
        "norm_g": np.ascontiguousarray(np.asarray(norm_g, dtype=np.float32)),
        "w_in": np.ascontiguousarray(np.asarray(w_in, dtype=np.float32)),
        "swa_sink": np.ascontiguousarray(np.asarray(swa_sink, dtype=np.float32)),
        "mla_q_norm": np.ascontiguousarray(np.asarray(mla_q_norm, dtype=np.float32)),
        "mla_w_uq": np.ascontiguousarray(np.asarray(mla_w_uq, dtype=np.float32)),
        "mla_kv_norm": np.ascontiguousarray(np.asarray(mla_kv_norm, dtype=np.float32)),
        "mla_w_ukv": np.ascontiguousarray(np.asarray(mla_w_ukv, dtype=np.float32)),
        "diff_lambda": np.ascontiguousarray(np.asarray(diff_lambda, dtype=np.float32)),
        "diff_subln": np.ascontiguousarray(np.asarray(diff_subln, dtype=np.float32)),
        "w_out": np.ascontiguousarray(np.asarray(w_out, dtype=np.float32)),
        "final_norm": np.ascontiguousarray(np.asarray(final_norm, dtype=np.float32)),
    }
    in_maps = []
    for b in range(B):
        m = dict(shared)
        m["x"] = np.ascontiguousarray(x[b])
        in_maps.append(m)
    res = run_bass_kernel_spmd(nc, in_maps, core_ids=list(range(B)))
    return np.stack([np.asarray(r["out"], dtype=np.float32) for r in res.results], axis=0)
```

```python
import math
from contextlib import ExitStack

import numpy as np
import concourse.bass as bass
import concourse.mybir as mybir
from concourse.bass_utils import run_bass_kernel_spmd

F32 = mybir.dt.float32
BF16 = mybir.dt.bfloat16
I32 = mybir.dt.int32
ALU = mybir.AluOpType
AF = mybir.ActivationFunctionType
AX = mybir.AxisListType

D = 2048
DC = 16
D_IN = 5824
EPS = 1e-6
NEG = -30000.0
O_AQ, O_AK, O_AV, O_AG, O_CQB, O_CKV, O_KR, O_GB, O_QC, O_KC, O_VC, O_GC = (
    0, 1024, 1280, 1536, 2560, 2944, 3200, 3264, 3776, 4288, 4800, 5312)
SWA_SLOPES = [2.0 ** (-8.0 * (i + 1) / 8) for i in range(8)]
DIF_SLOPES = [2.0 ** (-8.0 * (i + 1) / 4) for i in range(4)]
SEM_LIMIT = 30000


class Buf:
    def __init__(self, t, name):
        self.t = t
        self.name = name
        self.w = {}
        self.r = {}
        self.dsem = None
        self.dcnt = 0
        self.dram = False

    def __getitem__(self, k):
        return self.t[k]


class Eng:
    def __init__(self, kb, eng, kind):
        self.kb = kb
        self.eng = eng
        self.kind = kind
        self.sem = kb.new_sem()
        self.own = {self.sem}
        self.cnt = 0
        self.seen = {}
        self.pending = False

    def need(self, tok, raw):
        sem, val = tok
        if sem in self.own:
            if self.kind in ("pe", "sp"):
                return
        if sem in self.kb.dma_sems:
            val = self.kb.latest[sem]
        if self.seen.get(sem, 0) >= val:
            return
        self.eng.wait_ge(sem, val)
        self.seen[sem] = val

    def bump(self, ins, inc=True):
        if inc:
            self.cnt += 1
            ins.then_inc(self.sem, 1)
            tok = (self.sem, self.cnt)
            self.pending = False
            self.kb.latest[self.sem] = self.cnt
            if self.cnt >= SEM_LIMIT:
                self.sem = self.kb.new_sem()
                self.own.add(self.sem)
                self.cnt = 0
            return tok
        self.pending = True
        return (self.sem, self.cnt + 1)


class KB:
    def __init__(self, nc, es):
        self.nc = nc
        self.es = es
        self.nsem = 0
        self.dma_sems = set()
        self.stores_on_pool = False
        self.free_dsems = []
        self.scope_bufs = [[]]
        self.latest = {}
        self.scopes = [es]
        self.pe = Eng(self, nc.tensor, "pe")
        self.act = Eng(self, nc.scalar, "act")
        self.dve = Eng(self, nc.vector, "dve")
        self.pool = Eng(self, nc.gpsimd, "pool")
        self.sp = Eng(self, nc.sync, "sp")
        self.engs = [self.pe, self.act, self.dve, self.pool, self.sp]
        self.nuid = 0

    def new_sem(self):
        self.nsem += 1
        return self.es.enter_context(self.nc.semaphore(f"sm{self.nsem}"))

    def uid(self, n):
        self.nuid += 1
        return f"{n}_{self.nuid}"

    def sb(self, name, shape, dtype):
        t = self.scopes[-1].enter_context(self.nc.sbuf_tensor(self.uid(name), list(shape), dtype))
        b = Buf(t, name)
        self.scope_bufs[-1].append(b)
        return b

    def ps(self, name):
        t = self.scopes[-1].enter_context(self.nc.psum_tensor(self.uid(name), [128, 512], F32))
        return Buf(t, name)

    def dram(self, name, shape, dtype):
        t = self.nc.dram_tensor(name, list(shape), dtype, kind="Internal")
        b = Buf(t.ap(), name)
        b.dram = True
        return b

    def op(self, E, reads, writes, fn, inc=True):
        for b in reads:
            for tok in list(b.w.values()):
                E.need(tok, True)
        for b in writes:
            for tok in list(b.w.values()) + list(b.r.values()):
                E.need(tok, False)
        ins = fn(E.eng)
        tok = E.bump(ins, inc)
        for b in reads:
            b.r[tok[0]] = tok
        for b in writes:
            b.r = {}
            b.w[tok[0]] = tok
        return tok

    def dma(self, out_ap, in_ap, src, dst, Q=None, **kw):
        if Q is None:
            Q = self.pool if (dst.dram and self.stores_on_pool) else self.sp
        own = dst if not dst.dram else src
        assert not own.dram
        if own.dsem is None and self.free_dsems:
            own.dsem, own.dcnt = self.free_dsems.pop()
        if own.dsem is None or own.dcnt + 16 >= SEM_LIMIT:
            own.dsem = self.new_sem()
            self.dma_sems.add(own.dsem)
            own.dcnt = 0
        for tok in list(src.w.values()):
            Q.need(tok, True)
        toks = list(dst.r.values())
        if not dst.dram:
            toks += list(dst.w.values())
        for tok in toks:
            if tok[0] == own.dsem:
                continue
            Q.need(tok, False)
        ins = Q.eng.dma_start(out=out_ap, in_=in_ap, **kw)
        own.dcnt += 16
        ins.then_inc(own.dsem, 16)
        tok = (own.dsem, own.dcnt)
        self.latest[own.dsem] = own.dcnt
        src.r[own.dsem] = tok
        dst.r = {}
        dst.w[own.dsem] = tok
        return tok

    def barrier(self):
        assert not self.pe.pending
        items = list(self.latest.items())
        for E in self.engs:
            for sem, val in items:
                if E.seen.get(sem, 0) >= val:
                    continue
                if sem in E.own and E.kind in ("pe", "sp"):
                    continue
                E.eng.wait_ge(sem, val)
                E.seen[sem] = val

    class _Scope:
        def __init__(self, kb):
            self.kb = kb

        def __enter__(self):
            self.st = ExitStack()
            self.st.__enter__()
            self.kb.scopes.append(self.st)
            self.kb.scope_bufs.append([])
            return self

        def __exit__(self, *a):
            self.kb.barrier()
            self.kb.scopes.pop()
            for b in self.kb.scope_bufs.pop():
                if b.dsem is not None:
                    self.kb.free_dsems.append((b.dsem, b.dcnt))
                    b.dsem = None
            return self.st.__exit__(*a)

    def scope(self):
        return KB._Scope(self)


def build(S=4096, NL=2, dbg=False):
    NT = S // 128
    NTB = S // 512
    nc = bass.Bass("TRN2", target_bir_lowering=False)
    es = ExitStack()
    with es:
        es.enter_context(nc.allow_low_precision("bf16 matmul operands, fp32 accumulation"))
        try:
            es.enter_context(nc.allow_non_contiguous_dma("layout shuffles"))
        except Exception:
            pass
        _build(nc, es, S, NL, NT, NTB, dbg)
    return nc


def _build(nc, es, S, NL, NT, NTB, dbg):
    kb = KB(nc, es)
    pe, act, dve, pool, sp = kb.pe, kb.act, kb.dve, kb.pool, kb.sp

    def din(name, shape, dt=F32):
        b = Buf(nc.dram_tensor(name, list(shape), dt, kind="ExternalInput").ap(), name)
        b.dram = True
        return b

    x_in = din("x", [S, D])
    pos_in = din("positions", [S], I32)
    normg_in = din("norm_g", [2, D])
    win_in = din("w_in", [2, D, D_IN])
    sink_in = din("swa_sink", [2, 8])
    qn_in = din("mla_q_norm", [2, 384])
    wuq_in = din("mla_w_uq", [2, 384, 768])
    kvn_in = din("mla_kv_norm", [2, 256])
    wukv_in = din("mla_w_ukv", [2, 256, 1024])
    lam_in = din("diff_lambda", [2, 4, 64])
    subln_in = din("diff_subln", [2, 128])
    wout_in = din("w_out", [2, D, D])
    fnorm_in = din("final_norm", [D])
    okind = "ExternalOutput"
    out_d = Buf(nc.dram_tensor("out", [S, D], F32, kind=okind).ap(), "out")
    out_d.dram = True

    def scr(name, shape, dt=BF16):
        if dbg:
            b = Buf(nc.dram_tensor(name, list(shape), dt, kind="ExternalOutput").ap(), name)
            b.dram = True
            return b
        return kb.dram(name, shape, dt)

    QA = scr("QA", [1024, S]); KA = scr("KA", [256, S]); GA = scr("GA", [1024, S])
    CQ = scr("CQ", [384, S]); CKV = scr("CKV", [256, S]); KR = scr("KRr", [64, S]); GB = scr("GB", [512, S])
    QC = scr("QC", [512, S]); KC = scr("KC", [512, S]); GC = scr("GC", [512, S])
    VHA = scr("VHA", [2, 128, NT, 128]); VHC = scr("VHC", [4, 128, NT, 128]); VHB = scr("VHB", [4, 128, NT, 128])
    QBN = scr("QBN", [4, 128, S]); QBR = scr("QBR", [4, 64, S]); KN = scr("KN", [4, 128, S])
    YT = scr("YT", [D, S])
    X1 = scr("X1", [S, D], F32)
    HT1 = scr("HT1", [D, S])
    COS = scr("COS", [64, S], F32); SIN = scr("SIN", [64, S], F32)
    QAUG = scr("QAUG", [4, S]); KAUG = scr("KAUG", [4, 2, 4, S])

    ident = kb.sb("ident", [128, 128], BF16)
    ones = kb.sb("ones", [128, 128], BF16)
    BH = [kb.sb(f"bh{h}", [128, 384], BF16) for h in range(8)]
    CH = [kb.sb(f"ch{h}", [128, 128], BF16) for h in range(4)]
    esink = kb.sb("esink", [128, 16], F32)
    nlam = kb.sb("nlam", [128, 2], F32)
    subg = kb.sb("subg", [128, 2], F32)
    gcol = kb.sb("gcol", [128, 2, 16], F32)
    qncol = kb.sb("qncol", [128, 2, 3], F32)
    kvncol = kb.sb("kvncol", [128, 2, 2], F32)
    psbig = es.enter_context(nc.psum_tensor("psbig", [128, 4096], F32))
    PS = [Buf(psbig[:, i * 512:(i + 1) * 512], f"ps{i}") for i in range(8)]
    epsb = {}
    for addc_ in (float(D) * EPS, 384.0 * EPS, 256.0 * EPS, 128.0 * EPS):
        epsb[addc_] = kb.sb("epsb", [128, 1], F32)
        kb.op(dve, [], [epsb[addc_]], lambda e, a=addc_: e.memset(epsb[a][:], float(a)))

    sel32 = kb.sb("sel32", [128, 128], F32)
    selA = kb.sb("selA", [128, 128], F32)
    selB = kb.sb("selB", [128, 128], F32)
    kb.op(dve, [], [sel32], lambda e: e.memset(sel32[:], 1.0 / 32.0))
    kb.op(dve, [], [selA], lambda e: e.memset(selA[:], 0.0))
    kb.op(dve, [], [selB], lambda e: e.memset(selB[:], 0.0))
    for p0 in (0, 64):
        kb.op(dve, [], [selA], lambda e, p0=p0: e.memset(selA[p0:p0 + 32, :], 1.0 / 32.0))
        kb.op(dve, [], [selB], lambda e, p0=p0: e.memset(selB[p0 + 32:p0 + 64, :], 1.0 / 32.0))
    onecol = kb.sb("onecol", [128, 1], F32)
    kb.op(dve, [], [onecol], lambda e: e.memset(onecol[:], 1.0))

    def recip(src, src_ap, dst, dst_ap, bias_buf=None, bias_ap=None):
        rd_ = [src] + ([bias_buf] if bias_buf is not None else [])
        if bias_ap is not None:
            kb.op(act, rd_, [dst], lambda e: e.activation(out=dst_ap, in_=src_ap, func=AF.Ln, bias=bias_ap))
        else:
            kb.op(act, rd_, [dst], lambda e: e.activation(out=dst_ap, in_=src_ap, func=AF.Ln))
        kb.op(act, [dst], [dst], lambda e: e.activation(out=dst_ap, in_=dst_ap, func=AF.Exp, scale=-1.0))

    def rsqrt(src, src_ap, dst, dst_ap, addc):
        np_ = dst_ap.shape[0]
        kb.op(act, [src, epsb[addc]], [dst], lambda e: e.activation(out=dst_ap, in_=src_ap, func=AF.Ln, bias=epsb[addc][0:np_, 0:1]))
        kb.op(act, [dst], [dst], lambda e: e.activation(out=dst_ap, in_=dst_ap, func=AF.Exp, scale=-0.5))

    with kb.scope():
        ii = kb.sb("ii", [128, 384], I32)
        ff = kb.sb("ff", [128, 384], F32)
        f2 = kb.sb("f2", [128, 384], F32)
        kb.op(pool, [], [ii], lambda e: e.iota(ii[:, 0:128], [[1, 128]], base=0, channel_multiplier=-1))
        kb.op(dve, [ii], [ff], lambda e: e.tensor_copy(out=ff[:, 0:128], in_=ii[:, 0:128]))
        kb.op(dve, [ff], [ident], lambda e: e.tensor_single_scalar(out=ident[:], in_=ff[:, 0:128], scalar=0.0, op=ALU.is_equal))
        kb.op(dve, [], [ones], lambda e: e.memset(ones[:], 1.0))
        kb.op(pool, [], [ii], lambda e: e.iota(ii[:, 0:128], [[-1, 128]], base=0, channel_multiplier=1))
        kb.op(dve, [ii], [ff], lambda e: e.tensor_copy(out=ff[:, 0:128], in_=ii[:, 0:128]))
        for h in range(4):
            kb.op(dve, [ff], [CH[h]], lambda e, h=h: e.tensor_scalar(
                out=CH[h][:], in0=ff[:, 0:128], scalar1=0.0, scalar2=-2.0 * DIF_SLOPES[h], op0=ALU.max, op1=ALU.mult))
        kb.op(pool, [], [ii], lambda e: e.iota(ii[:], [[1, 384]], base=-128, channel_multiplier=-1))
        kb.op(dve, [ii], [ff], lambda e: e.tensor_copy(out=ff[:], in_=ii[:]))
        kb.op(act, [ff], [ff], lambda e: e.activation(out=ff[:], in_=ff[:], func=AF.Abs))
        kb.op(dve, [ff], [f2], lambda e: e.tensor_scalar(
            out=f2[:], in0=ff[:], scalar1=128.5, scalar2=NEG, op0=ALU.is_gt, op1=ALU.mult))
        for h in range(8):
            kb.op(dve, [ff, f2], [BH[h]], lambda e, h=h: e.scalar_tensor_tensor(
                out=BH[h][:], in0=ff[:], scalar=-SWA_SLOPES[h], in1=f2[:], op0=ALU.mult, op1=ALU.add))

        sk = kb.sb("sk", [128, 16], F32)
        kb.dma(sk[:], sink_in.t.rearrange("l h -> (l h)").partition_broadcast(128), sink_in, sk)
        kb.op(act, [sk], [esink], lambda e: e.activation(out=esink[:], in_=sk[:], func=AF.Exp))
        lm = kb.sb("lm", [128, 2, 4, 64], F32)
        kb.dma(lm[:].rearrange("p l a b -> p (l a b)"),
               lam_in.t.rearrange("l a b -> (l a b)").partition_broadcast(128), lam_in, lm)
        pr = kb.sb("pr", [128, 2, 2, 64], F32)
        sm = kb.sb("sm", [128, 4], F32)
        for l in range(2):
            for j in range(2):
                kb.op(dve, [lm], [pr], lambda e, l=l, j=j: e.tensor_tensor(
                    out=pr[:, l, j, :], in0=lm[:, l, 2 * j, :], in1=lm[:, l, 2 * j + 1, :], op=ALU.mult))
                kb.op(dve, [pr], [sm], lambda e, l=l, j=j: e.reduce_sum(
                    out=sm[:, 2 * l + j:2 * l + j + 1], in_=pr[:, l, j, :], axis=AX.X))
        se = kb.sb("se", [128, 4], F32)
        kb.op(act, [sm], [se], lambda e: e.activation(out=se[:], in_=sm[:], func=AF.Exp))
        for l in range(2):
            lam_init = 0.8 - 0.6 * math.exp(-0.3 * l)
            kb.op(dve, [se], [nlam], lambda e, l=l, li=lam_init: e.scalar_tensor_tensor(
                out=nlam[:, l:l + 1], in0=se[:, 2 * l + 1:2 * l + 2], scalar=-li, in1=se[:, 2 * l:2 * l + 1],
                op0=ALU.add, op1=ALU.subtract))
        sg = kb.sb("sg", [128, 2], F32)
        kb.dma(sg[:].unsqueeze(2), subln_in.t.rearrange("l (p o) -> p l o", o=1), subln_in, sg)
        for l in range(2):
            lam_init = 0.8 - 0.6 * math.exp(-0.3 * l)
            kb.op(dve, [sg], [subg], lambda e, l=l, li=lam_init: e.tensor_single_scalar(
                out=subg[:, l:l + 1], in_=sg[:, l:l + 1], scalar=(1.0 - li) * math.sqrt(128.0), op=ALU.mult))
        kb.dma(gcol[:].unsqueeze(3), normg_in.t.rearrange("l (c p o) -> p l c o", p=128, o=1), normg_in, gcol)
        kb.dma(qncol[:].unsqueeze(3), qn_in.t.rearrange("l (c p o) -> p l c o", p=128, o=1), qn_in, qncol)
        kb.dma(kvncol[:].unsqueeze(3), kvn_in.t.rearrange("l (c p o) -> p l c o", p=128, o=1), kvn_in, kvncol)

        pi_ = kb.sb("pi_", [64, S], I32)
        kb.dma(pi_[:], pos_in.t.partition_broadcast(64), pos_in, pi_)
        pf = kb.sb("pf", [64, S], F32)
        kb.op(dve, [pi_], [pf], lambda e: e.tensor_copy(out=pf[:], in_=pi_[:]))
        invrow = kb.sb("invrow", [1, 2, 32], F32)
        for i_ in range(32):
            val = float(np.float32(10000.0) ** np.float32(-(2.0 * i_) / 64.0))
            kb.op(dve, [], [invrow], lambda e, i_=i_, val=val: e.memset(invrow[:, :, i_:i_ + 1], val))
        INVD = kb.dram("INVD", [64], F32)
        kb.dma(INVD.t.rearrange("(o n) -> o n", o=1), invrow[:].rearrange("o a b -> o (a b)"), invrow, INVD)
        inv = kb.sb("inv", [64, 1], F32)
        kb.dma(inv[:], INVD.t.rearrange("(p o) -> p o", o=1), INVD, inv)
        ang = kb.sb("ang", [64, S], F32)
        kb.op(dve, [pf, inv], [ang], lambda e: e.tensor_scalar(
            out=ang[:], in0=pf[:], scalar1=inv[:, 0:1], scalar2=None, op0=ALU.mult))
        tr = kb.sb("tr", [64, S], F32)
        kf = kb.sb("kf", [64, S], F32)
        C1 = 6.28125
        C2 = 2.0 * math.pi - 6.28125
        for shift, dst in ((0.0, SIN), (0.5 * math.pi, COS)):
            kb.op(dve, [ang], [tr], lambda e, sh=shift: e.tensor_scalar(
                out=tr[:], in0=ang[:], scalar1=sh, scalar2=1.0 / (2.0 * math.pi), op0=ALU.add, op1=ALU.mult))
            kb.op(dve, [tr], [pi_], lambda e: e.tensor_copy(out=pi_[:], in_=tr[:]))
            kb.op(dve, [pi_], [kf], lambda e: e.tensor_copy(out=kf[:], in_=pi_[:]))
            kb.op(dve, [ang], [tr], lambda e, sh=shift: e.tensor_single_scalar(out=tr[:], in_=ang[:], scalar=sh, op=ALU.add))
            kb.op(dve, [kf, tr], [tr], lambda e: e.scalar_tensor_tensor(
                out=tr[:], in0=kf[:], scalar=-C1, in1=tr[:], op0=ALU.mult, op1=ALU.add))
            kb.op(dve, [kf, tr], [tr], lambda e: e.scalar_tensor_tensor(
                out=tr[:], in0=kf[:], scalar=-C2, in1=tr[:], op0=ALU.mult, op1=ALU.add))
            for thr, adj in ((math.pi, -2.0 * math.pi), (None, 2.0 * math.pi)):
                if thr is not None:
                    kb.op(dve, [tr], [kf], lambda e: e.tensor_scalar(
                        out=kf[:], in0=tr[:], scalar1=math.pi, scalar2=-2.0 * math.pi, op0=ALU.is_gt, op1=ALU.mult))
                else:
                    kb.op(dve, [tr], [kf], lambda e: e.tensor_scalar(
                        out=kf[:], in0=tr[:], scalar1=-math.pi, scalar2=2.0 * math.pi, op0=ALU.is_lt, op1=ALU.mult))
                kb.op(dve, [kf, tr], [tr], lambda e: e.tensor_tensor(out=tr[:], in0=tr[:], in1=kf[:], op=ALU.add))
            kb.op(dve, [tr], [tr], lambda e: e.tensor_scalar(
                out=tr[:], in0=tr[:], scalar1=3.1415925, scalar2=-3.1415925, op0=ALU.min, op1=ALU.max))
            kb.op(act, [tr], [pf], lambda e: e.activation(out=pf[:], in_=tr[:], func=AF.Sin))
            kb.dma(dst.t, pf[:], pf, dst)

        NJ = S // 128
        p2 = kb.sb("p2", [128, NJ], I32)
        kb.dma(p2[:], pos_in.t.rearrange("(p j) -> p j", p=128), pos_in, p2)
        hi_i = kb.sb("hi_i", [128, NJ], I32)
        lo_i = kb.sb("lo_i", [128, NJ], I32)
        kb.op(dve, [p2], [hi_i], lambda e: e.tensor_single_scalar(out=hi_i[:], in_=p2[:], scalar=6, op=ALU.arith_shift_right))
        kb.op(dve, [p2], [lo_i], lambda e: e.tensor_single_scalar(out=lo_i[:], in_=p2[:], scalar=63, op=ALU.bitwise_and))
        hi_f = kb.sb("hi_f", [128, NJ], F32)
        lo_f = kb.sb("lo_f", [128, NJ], F32)
        kb.op(dve, [hi_i], [hi_f], lambda e: e.tensor_copy(out=hi_f[:], in_=hi_i[:]))
        kb.op(dve, [lo_i], [lo_f], lambda e: e.tensor_copy(out=lo_f[:], in_=lo_i[:]))
        qa = kb.sb("qa", [128, 4, NJ], BF16)
        kb.op(dve, [hi_f], [qa], lambda e: e.tensor_copy(out=qa[:, 0, :], in_=hi_f[:]))
        kb.op(dve, [lo_f], [qa], lambda e: e.tensor_copy(out=qa[:, 1, :], in_=lo_f[:]))
        kb.op(dve, [], [qa], lambda e: e.memset(qa[:, 2:4, :], 1.0))
        kb.dma(QAUG.t.rearrange("r (p j) -> p r j", p=128), qa[:], qa, QAUG)
        ka = kb.sb("ka", [128, 4, 2, 4, NJ], BF16)
        for h in range(4):
            s_ = DIF_SLOPES[h]
            for v, sg_ in ((0, 1.0), (1, -1.0)):
                kb.op(dve, [], [ka], lambda e, h=h, v=v, c=-64.0 * s_ * sg_: e.memset(ka[:, h, v, 0, :], c))
                kb.op(dve, [], [ka], lambda e, h=h, v=v, c=-s_ * sg_: e.memset(ka[:, h, v, 1, :], c))
                kb.op(dve, [hi_f], [ka], lambda e, h=h, v=v, c=64.0 * s_ * sg_: e.tensor_single_scalar(
                    out=ka[:, h, v, 2, :], in_=hi_f[:], scalar=c, op=ALU.mult))
                kb.op(dve, [lo_f], [ka], lambda e, h=h, v=v, c=s_ * sg_: e.tensor_single_scalar(
                    out=ka[:, h, v, 3, :], in_=lo_f[:], scalar=c, op=ALU.mult))
        kb.dma(KAUG.t.rearrange("h v r (p j) -> p (h v r) j", p=128),
               ka[:].rearrange("p h v r j -> p (h v r) j"), ka, KAUG)

    kb.stores_on_pool = False
    for l in range(NL):
        xsrc = x_in if l == 0 else X1
        lam_init = 0.8 - 0.6 * math.exp(-0.3 * l)
        with kb.scope():
            hT = kb.sb("hT", [128, DC, S], BF16)
            hTv = [Buf(hT.t[:, :, tb * 512:(tb + 1) * 512], f"hTv{tb}") for tb in range(NTB)]
            if l > 0:
                for tb in range(NTB):
                    kb.dma(hTv[tb][:], HT1.t[:, tb * 512:(tb + 1) * 512].rearrange("(c p) s -> p c s", p=128), HT1, hTv[tb])
            with kb.scope():
                NA1 = 4
                xs = [kb.sb(f"xs{i}", [128, D], F32) for i in range(NA1)]
                hb = [kb.sb(f"hb{i}", [128, D], BF16) for i in range(NA1)]
                junk = kb.sb("junk", [128, D], BF16)
                ssq = [kb.sb(f"ssq{i}", [128, 1], F32) for i in range(NA1)]
                rstd = [kb.sb(f"rstd{i}", [128, 1], F32) for i in range(NA1)]
                def a1_front(t):
                    i = t % NA1
                    kb.dma(xs[i][:], xsrc.t[t * 128:(t + 1) * 128, :], xsrc, xs[i])
                    kb.op(act, [xs[i]], [junk, ssq[i]], lambda e, i=i: e.activation(
                        out=junk[:], in_=xs[i][:], func=AF.Square, accum_out=ssq[i][:, 0:1]))
                    rsqrt(ssq[i], ssq[i][:], rstd[i], rstd[i][:], float(D) * EPS)
                    kb.op(dve, [xs[i], rstd[i]], [hb[i]], lambda e, i=i: e.tensor_scalar(
                        out=hb[i][:], in0=xs[i][:], scalar1=rstd[i][:, 0:1], scalar2=math.sqrt(D), op0=ALU.mult, op1=ALU.mult))

                def a1_back(t):
                    i = t % NA1
                    for half in range(2):
                        pb = PS[(2 * t + half) % 8]
                        pv = pb.t[:].bitcast(BF16)
                        for c8 in range(8):
                            c = half * 8 + c8
                            kb.op(pe, [hb[i], ident], [pb], lambda e, c=c, c8=c8, pv=pv, i=i: e.transpose(
                                pv[:, c8 * 128:(c8 + 1) * 128], hb[i][:, c * 128:(c + 1) * 128], ident[:]), inc=(c8 == 7))
                        kb.op(dve, [pb], [hTv[t // 4]], lambda e, half=half, pv=pv, t=t: e.tensor_copy(
                            out=hT[:, half * 8:(half + 1) * 8, t * 128:(t + 1) * 128],
                            in_=pv[:, 0:1024].rearrange("p (c k) -> p c k", c=8)))

                if l == 0:
                    a1_front(0)
                    a1_front(1)
                    for t in range(NT):
                        if t + 2 < NT:
                            a1_front(t + 2)
                        a1_back(t)
            with kb.scope():
                wst = [kb.sb(f"wst{i}", [128, DC, 128], F32) for i in range(2)]
                wb = [kb.sb(f"wb{i}", [128, DC, 128], BF16) for i in range(2)]
                stg = [kb.sb(f"stg{i}", [128, 2048], BF16) for i in range(2)]
                cs = [kb.sb(f"cs{i}", [64, 512], F32) for i in range(2)]
                sn = [kb.sb(f"sn{i}", [64, 512], F32) for i in range(2)]
                r1 = kb.sb("r1", [64, 512], F32)
                r2 = kb.sb("r2", [64, 512], F32)
                gtmp = [kb.sb(f"gtmp{i}", [128, 512], F32) for i in range(2)]
                gb_ = gcol[:, l, :].unsqueeze(2).to_broadcast([128, DC, 128])
                chunks = []
                for j in range(8):
                    chunks.append(("fm", O_AQ + j * 128, QA, j * 128, 128.0 ** -0.5))
                for j in range(2):
                    chunks.append(("fm", O_AK + j * 128, KA, j * 128, 1.0))
                for j in range(2):
                    chunks.append(("tm", O_AV + j * 128, VHA, j, 1.0))
                for j in range(8):
                    chunks.append(("fmg", O_AG + j * 128, GA, j * 128, 1.0))
                for j in range(3):
                    chunks.append(("fm", O_CQB + j * 128, CQ, j * 128, 1.0))
                for j in range(2):
                    chunks.append(("fm", O_CKV + j * 128, CKV, j * 128, 1.0))
                chunks.append(("kr", O_KR, KR, 0, 1.0))
                for j in range(4):
                    chunks.append(("fmg", O_GB + j * 128, GB, j * 128, 1.0))
                for j in range(4):
                    chunks.append(("fm", O_QC + j * 128, QC, j * 128, 0.125))
                for j in range(4):
                    chunks.append(("fm", O_KC + j * 128, KC, j * 128, 1.0))
                for j in range(4):
                    chunks.append(("tm", O_VC + j * 128, VHC, j, 1.0))
                for j in range(4):
                    chunks.append(("fmg", O_GC + j * 128, GC, j * 128, 1.0))

                def load_w(ci):
                    kind, c0, _, _, _ = chunks[ci]
                    i = ci % 2
                    ew = 64 if kind == "kr" else 128
                    kb.dma(wst[i][:, :, 0:ew],
                           win_in.t[l, :, c0:c0 + ew].rearrange("(c p) e -> p c e", p=128), win_in, wst[i])

                def cast_w(ci):
                    kind = chunks[ci][0]
                    i = ci % 2
                    E = dve if ci % 2 == 0 else pool
                    if kind != "kr":
                        kb.op(E, [wst[i], gcol], [wb[i]], lambda e: e.tensor_tensor(
                            out=wb[i][:], in0=wst[i][:], in1=gb_, op=ALU.mult))
                    else:
                        g64 = gcol[:, l, :].unsqueeze(2).to_broadcast([128, DC, 64])
                        g32 = gcol[:, l, :].unsqueeze(2).to_broadcast([128, DC, 32])
                        kb.op(dve, [wst[i], gcol], [wb[i]], lambda e: e.tensor_tensor(
                            out=wb[i][:, :, 0:64], in0=wst[i][:, :, 0:64], in1=g64, op=ALU.mult))
                        kb.op(dve, [wst[i], gcol], [wb[i]], lambda e: e.scalar_tensor_tensor(
                            out=wb[i][:, :, 64:96], in0=wst[i][:, :, 32:64], scalar=-1.0, in1=g32, op0=ALU.mult, op1=ALU.mult))
                        kb.op(dve, [wst[i], gcol], [wb[i]], lambda e: e.tensor_tensor(
                            out=wb[i][:, :, 96:128], in0=wst[i][:, :, 0:32], in1=g32, op=ALU.mult))

                load_w(0)
                cast_w(0)
                psi = 0
                evi = 0
                for ci, (kind, c0, dst, r0, scl) in enumerate(chunks):
                    if ci + 1 < len(chunks):
                        load_w(ci + 1)
                    i = ci % 2
                    w = wb[i]
                    if kind in ("fm", "fmg"):
                        for tb in range(NTB):
                            if tb == NTB // 2 and ci + 1 < len(chunks):
                                cast_w(ci + 1)
                            pb = PS[psi % 4]; psi += 1
                            for c in range(DC):
                                kb.op(pe, [w, hTv[tb]], [pb], lambda e, c=c, pb=pb, tb=tb: e.matmul(
                                    pb[:, 0:512], lhsT=w[:, c, :], rhs=hT[:, c, tb * 512:(tb + 1) * 512],
                                    start=(c == 0), stop=(c == DC - 1)), inc=(c == DC - 1))
                            sg_ = stg[(tb // 4) % 2]
                            so = (tb % 4) * 512
                            if kind == "fmg":
                                tg = gtmp[evi % 2]
                                kb.op(act, [pb], [tg], lambda e, pb=pb, tg=tg: e.activation(
                                    out=tg[:], in_=pb[:, 0:512], func=AF.Exp, scale=-1.0))
                                kb.op(act, [tg, onecol], [tg], lambda e, tg=tg: e.activation(
                                    out=tg[:], in_=tg[:], func=AF.Ln, bias=onecol[:, 0:1]))
                                kb.op(act, [tg], [tg], lambda e, tg=tg: e.activation(
                                    out=tg[:], in_=tg[:], func=AF.Exp, scale=-1.0))
                                kb.op(dve, [pb, tg], [sg_], lambda e, pb=pb, sg_=sg_, so=so, tg=tg: e.tensor_tensor(
                                    out=sg_[:, so:so + 512], in0=pb[:, 0:512], in1=tg[:], op=ALU.mult))
                            elif evi % 2 == 0:
                                kb.op(act, [pb], [sg_], lambda e, pb=pb, sg_=sg_, so=so: e.activation(
                                    out=sg_[:, so:so + 512], in_=pb[:, 0:512], func=AF.Copy, scale=float(scl)))
                            else:
                                kb.op(dve, [pb], [sg_], lambda e, pb=pb, sg_=sg_, so=so: e.tensor_single_scalar(
                                    out=sg_[:, so:so + 512], in_=pb[:, 0:512], scalar=float(scl), op=ALU.mult))
                            evi += 1
                            if tb % 4 == 3 or tb == NTB - 1:
                                t0 = (tb // 4) * 2048
                                n = (tb % 4 + 1) * 512
                                kb.dma(dst.t[r0:r0 + 128, t0:t0 + n], sg_[:, 0:n], sg_, dst)
                    elif kind == "tm":
                        for tb in range(NTB):
                            if tb == NTB // 2 and ci + 1 < len(chunks):
                                cast_w(ci + 1)
                            pb = PS[psi % 4]; psi += 1
                            for ti in range(4):
                                t = tb * 4 + ti
                                for c in range(DC):
                                    kb.op(pe, [w, hTv[tb]], [pb], lambda e, c=c, pb=pb, t=t, ti=ti: e.matmul(
                                        pb[:, ti * 128:(ti + 1) * 128], lhsT=hT[:, c, t * 128:(t + 1) * 128], rhs=w[:, c, :],
                                        start=(c == 0), stop=(c == DC - 1)), inc=(c == DC - 1))
                            sg_ = stg[(tb // 4) % 2]
                            so = (tb % 4) * 512
                            kb.op(dve, [pb], [sg_], lambda e, pb=pb, sg_=sg_, so=so: e.tensor_copy(
                                out=sg_[:, so:so + 512], in_=pb[:, 0:512]))
                            if tb % 4 == 3 or tb == NTB - 1:
                                n0 = (tb // 4) * 16
                                nn = (tb % 4 + 1) * 4
                                kb.dma(dst.t[r0, :, n0:n0 + nn, :],
                                       sg_[:, 0:nn * 128].rearrange("p (n e) -> p n e", e=128), sg_, dst)
                    else:
                        for tb in range(NTB):
                            if tb == NTB // 2 and ci + 1 < len(chunks):
                                cast_w(ci + 1)
                            pa = PS[psi % 4]; psi += 1
                            pb2 = PS[psi % 4]; psi += 1
                            j = tb % 2
                            kb.dma(cs[j][:], COS.t[:, tb * 512:(tb + 1) * 512], COS, cs[j])
                            kb.dma(sn[j][:], SIN.t[:, tb * 512:(tb + 1) * 512], SIN, sn[j])
                            for (pp, o) in ((pa, 0), (pb2, 64)):
                                for c in range(DC):
                                    kb.op(pe, [w, hTv[tb]], [pp], lambda e, c=c, pp=pp, o=o, tb=tb: e.matmul(
                                        pp[0:64, 0:512], lhsT=w[:, c, o:o + 64], rhs=hT[:, c, tb * 512:(tb + 1) * 512],
                                        start=(c == 0), stop=(c == DC - 1)), inc=(c == DC - 1))
                            kb.op(dve, [pa, cs[j]], [r1], lambda e, pa=pa, j=j: e.tensor_tensor(
                                out=r1[:], in0=pa[0:64, 0:512], in1=cs[j][:], op=ALU.mult))
                            kb.op(dve, [pb2, sn[j]], [r2], lambda e, pb2=pb2, j=j: e.tensor_tensor(
                                out=r2[:], in0=pb2[0:64, 0:512], in1=sn[j][:], op=ALU.mult))
                            sg_ = stg[(tb // 4) % 2]
                            so = (tb % 4) * 512
                            kb.op(dve, [r1, r2], [sg_], lambda e, sg_=sg_, so=so: e.tensor_tensor(
                                out=sg_[0:64, so:so + 512], in0=r1[:], in1=r2[:], op=ALU.add))
                            if tb % 4 == 3 or tb == NTB - 1:
                                t0 = (tb // 4) * 2048
                                n = (tb % 4 + 1) * 512
                                kb.dma(dst.t[0:64, t0:t0 + n], sg_[0:64, 0:n], sg_, dst)

        with kb.scope():
            QSC = math.sqrt(384.0) / math.sqrt(192.0)
            wq_st = kb.sb("wq_st", [128, 3, 768], F32)
            wkv_st = kb.sb("wkv_st", [128, 2, 1024], F32)
            wq = kb.sb("wq", [128, 3, 1024], BF16)
            wkv = kb.sb("wkv", [128, 2, 1024], BF16)
            kb.dma(wq_st[:], wuq_in.t[l].rearrange("(c p) e -> p c e", p=128), wuq_in, wq_st)
            kb.dma(wkv_st[:], wukv_in.t[l].rearrange("(c p) e -> p c e", p=128), wukv_in, wkv_st)
            for c in range(3):
                kb.op(dve, [wq_st, qncol], [wq], lambda e, c=c: e.tensor_scalar(
                    out=wq[:, c, 0:768], in0=wq_st[:, c, :], scalar1=qncol[:, l, c:c + 1], scalar2=QSC, op0=ALU.mult, op1=ALU.mult))
                for h in range(4):
                    b0 = h * 192 + 128
                    kb.op(dve, [wq_st, qncol], [wq], lambda e, c=c, h=h, b0=b0: e.tensor_scalar(
                        out=wq[:, c, 768 + h * 64:768 + h * 64 + 32], in0=wq_st[:, c, b0 + 32:b0 + 64],
                        scalar1=qncol[:, l, c:c + 1], scalar2=-QSC, op0=ALU.mult, op1=ALU.mult))
                    kb.op(dve, [wq_st, qncol], [wq], lambda e, c=c, h=h, b0=b0: e.tensor_scalar(
                        out=wq[:, c, 768 + h * 64 + 32:768 + h * 64 + 64], in0=wq_st[:, c, b0:b0 + 32],
                        scalar1=qncol[:, l, c:c + 1], scalar2=QSC, op0=ALU.mult, op1=ALU.mult))
            for c in range(2):
                for h in range(4):
                    kb.op(dve, [wkv_st, kvncol], [wkv], lambda e, c=c, h=h: e.tensor_scalar(
                        out=wkv[:, c, h * 128:(h + 1) * 128], in0=wkv_st[:, c, h * 256:h * 256 + 128],
                        scalar1=kvncol[:, l, c:c + 1], scalar2=16.0, op0=ALU.mult, op1=ALU.mult))
                    kb.op(dve, [wkv_st, kvncol], [wkv], lambda e, c=c, h=h: e.tensor_scalar(
                        out=wkv[:, c, 512 + h * 128:512 + (h + 1) * 128], in0=wkv_st[:, c, h * 256 + 128:h * 256 + 256],
                        scalar1=kvncol[:, l, c:c + 1], scalar2=16.0, op0=ALU.mult, op1=ALU.mult))
            NB0 = 3
            cq = [kb.sb(f"cq{i}", [128, 3, 512], BF16) for i in range(NB0)]
            ckv = [kb.sb(f"ckv{i}", [128, 2, 512], BF16) for i in range(NB0)]
            sq_ = [kb.sb(f"sq{i}", [128, 3, 512], BF16) for i in range(NB0)]
            sqk_ = [kb.sb(f"sqk{i}", [128, 2, 512], BF16) for i in range(NB0)]
            rq_ = [kb.sb(f"rq{i}", [128, 512], F32) for i in range(NB0)]
            rk_ = [kb.sb(f"rk{i}", [128, 512], F32) for i in range(NB0)]
            cqn_ = [kb.sb(f"cqn{i}", [128, 3, 512], BF16) for i in range(NB0)]
            ckvn_ = [kb.sb(f"ckvn{i}", [128, 2, 512], BF16) for i in range(NB0)]
            cs2 = kb.sb("cs2", [64, 512], F32)
            sn2 = kb.sb("sn2", [64, 512], F32)
            t1 = kb.sb("t1", [64, 512], F32)
            t2 = kb.sb("t2", [64, 512], F32)
            so_ = [kb.sb(f"so{i}", [128, 512], BF16) for i in range(4)]
            soi = 0
            psi = 0

            def ldc(tb):
                i = tb % NB0
                kb.dma(cq[i][:], CQ.t[:, tb * 512:(tb + 1) * 512].rearrange("(c p) s -> p c s", p=128), CQ, cq[i])
                kb.dma(ckv[i][:], CKV.t[:, tb * 512:(tb + 1) * 512].rearrange("(c p) s -> p c s", p=128), CKV, ckv[i])

            def prologue(tb):
                if tb + 1 < NTB:
                    ldc(tb + 1)
                i = tb % NB0
                sq, sqk, rq, rk, cqn, ckvn = sq_[i], sqk_[i], rq_[i], rk_[i], cqn_[i], ckvn_[i]
                kb.op(pool, [cq[i]], [sq], lambda e, i=i: e.tensor_tensor(out=sq[:], in0=cq[i][:], in1=cq[i][:], op=ALU.mult))
                kb.op(pool, [ckv[i]], [sqk], lambda e, i=i: e.tensor_tensor(out=sqk[:], in0=ckv[i][:], in1=ckv[i][:], op=ALU.mult))
                pA = PS[psi_[0] % 8]; psi_[0] += 1
                for c in range(3):
                    kb.op(pe, [ones, sq], [pA], lambda e, c=c, pA=pA: e.matmul(
                        pA[:, 0:512], lhsT=ones[:], rhs=sq[:, c, :], start=(c == 0), stop=(c == 2)), inc=(c == 2))
                rsqrt(pA, pA[:, 0:512], rq, rq[:], 384.0 * EPS)
                pB = PS[psi_[0] % 8]; psi_[0] += 1
                for c in range(2):
                    kb.op(pe, [ones, sqk], [pB], lambda e, c=c, pB=pB: e.matmul(
                        pB[:, 0:512], lhsT=ones[:], rhs=sqk[:, c, :], start=(c == 0), stop=(c == 1)), inc=(c == 1))
                rsqrt(pB, pB[:, 0:512], rk, rk[:], 256.0 * EPS)
                kb.op(pool, [cq[i], rq], [cqn], lambda e, i=i: e.tensor_tensor(
                    out=cqn[:], in0=cq[i][:], in1=rq[:].unsqueeze(1).to_broadcast([128, 3, 512]), op=ALU.mult))
                kb.op(pool, [ckv[i], rk], [ckvn], lambda e, i=i: e.tensor_tensor(
                    out=ckvn[:], in0=ckv[i][:], in1=rk[:].unsqueeze(1).to_broadcast([128, 2, 512]), op=ALU.mult))

            psi_ = [0]
            ldc(0)
            prologue(0)
            if NTB > 1:
                prologue(1)
            for tb in range(NTB):
                if tb + 2 < NTB:
                    prologue(tb + 2)
                i = tb % NB0
                cqn, ckvn = cqn_[i], ckvn_[i]
                psi = psi_[0]
                tsl = slice(tb * 512, (tb + 1) * 512)
                kb.dma(cs2[:], COS.t[:, tsl], COS, cs2)
                kb.dma(sn2[:], SIN.t[:, tsl], SIN, sn2)
                for h in range(4):
                    pq = PS[psi % 8]; psi += 1
                    for c in range(3):
                        kb.op(pe, [wq, cqn], [pq], lambda e, c=c, h=h, pq=pq, i=i: e.matmul(
                            pq[:, 0:512], lhsT=wq[:, c, h * 192:h * 192 + 128], rhs=cqn[:, c, :],
                            start=(c == 0), stop=(c == 2)), inc=(c == 2))
                    s_ = so_[soi % 4]; soi += 1
                    kb.op(act, [pq], [s_], lambda e, pq=pq, s_=s_: e.activation(out=s_[:], in_=pq[:, 0:512], func=AF.Copy))
                    kb.dma(QBN.t[h, :, tsl], s_[:], s_, QBN)
                    pr1 = PS[psi % 8]; psi += 1
                    pr2 = PS[psi % 8]; psi += 1
                    for (pp, o) in ((pr1, h * 192 + 128), (pr2, 768 + h * 64)):
                        for c in range(3):
                            kb.op(pe, [wq, cqn], [pp], lambda e, c=c, pp=pp, o=o, i=i: e.matmul(
                                pp[0:64, 0:512], lhsT=wq[:, c, o:o + 64], rhs=cqn[:, c, :],
                                start=(c == 0), stop=(c == 2)), inc=(c == 2))
                    kb.op(dve, [pr1, cs2], [t1], lambda e, pr1=pr1: e.tensor_tensor(
                        out=t1[:], in0=pr1[0:64, 0:512], in1=cs2[:], op=ALU.mult))
                    kb.op(dve, [pr2, sn2], [t2], lambda e, pr2=pr2: e.tensor_tensor(
                        out=t2[:], in0=pr2[0:64, 0:512], in1=sn2[:], op=ALU.mult))
                    s_ = so_[soi % 4]; soi += 1
                    kb.op(dve, [t1, t2], [s_], lambda e, s_=s_: e.tensor_tensor(out=s_[0:64, :], in0=t1[:], in1=t2[:], op=ALU.add))
                    kb.dma(QBR.t[h, :, tsl], s_[0:64, :], s_, QBR)
                    pk = PS[psi % 8]; psi += 1
                    for c in range(2):
                        kb.op(pe, [wkv, ckvn], [pk], lambda e, c=c, h=h, pk=pk, i=i: e.matmul(
                            pk[:, 0:512], lhsT=wkv[:, c, h * 128:(h + 1) * 128], rhs=ckvn[:, c, :],
                            start=(c == 0), stop=(c == 1)), inc=(c == 1))
                    s_ = so_[soi % 4]; soi += 1
                    kb.op(act, [pk], [s_], lambda e, pk=pk, s_=s_: e.activation(out=s_[:], in_=pk[:, 0:512], func=AF.Copy))
                    kb.dma(KN.t[h, :, tsl], s_[:], s_, KN)
                for ti in range(4):
                    pvv = PS[psi % 8]; psi += 1
                    for c in range(2):
                        kb.op(pe, [wkv, ckvn], [pvv], lambda e, c=c, ti=ti, pvv=pvv, i=i: e.matmul(
                            pvv[:, 0:512], lhsT=ckvn[:, c, ti * 128:(ti + 1) * 128], rhs=wkv[:, c, 512:1024],
                            start=(c == 0), stop=(c == 1)), inc=(c == 1))
                    s_ = so_[soi % 4]; soi += 1
                    if ti % 2 == 0:
                        kb.op(dve, [pvv], [s_], lambda e, pvv=pvv, s_=s_: e.tensor_copy(out=s_[:], in_=pvv[:, 0:512]))
                    else:
                        kb.op(act, [pvv], [s_], lambda e, pvv=pvv, s_=s_: e.activation(out=s_[:], in_=pvv[:, 0:512], func=AF.Copy))
                    kb.dma(VHB.t[:, :, tb * 4 + ti, :].rearrange("h p e -> p h e"),
                           s_[:].rearrange("p (h e) -> p h e", e=128), s_, VHB)
                psi_[0] = psi

        with kb.scope():
            NPT = 12
            ptbig = kb.sb("ptbig", [128, NPT * 512], BF16)
            pT = [Buf(ptbig.t[:, i * 512:(i + 1) * 512], f"pT{i}") for i in range(NPT)]
            pdsb = kb.sb("pdsb", [128, 512], F32)
            rd = kb.sb("rd", [128, 512], F32)
            rd2 = kb.sb("rd2", [128, 512], F32)
            oa = kb.sb("oa", [128, 512], F32)
            ob = kb.sb("ob", [128, 512], F32)
            osq = kb.sb("osq", [128, 512], BF16)
            rs = kb.sb("rs", [128, 512], F32)
            yo = [kb.sb(f"yo{i}", [128, 512], BF16) for i in range(2)]
            st = {"pt": 0, "g": 0, "y": 0}

            ghd = [kb.sb(f"ghd{i}", [128, S], BF16) for i in range(2)]

            def gate_load(slot, Gsrc, grow):
                kb.dma(ghd[slot][:], Gsrc.t[grow:grow + 128, :], Gsrc, ghd[slot])

            def gate_and_store(o_buf, slot, yrow, q0, n=512):
                g = ghd[slot]
                y = yo[st["y"] % 2]; st["y"] += 1
                kb.op(dve, [g, o_buf], [y], lambda e: e.tensor_tensor(out=y[:, 0:n], in0=g[:, q0:q0 + n], in1=o_buf[:, 0:n], op=ALU.mult))
                kb.dma(YT.t[yrow:yrow + 128, q0:q0 + n], y[:, 0:n], y, YT)

            deferred = []

            def defer(k, fn):
                deferred.append([k, fn])

            def run_deferred(flush=False):
                while True:
                    for d in deferred:
                        d[0] -= 1
                    due = [d for d in deferred if d[0] <= 0]
                    for d in due:
                        deferred.remove(d)
                        d[1]()
                    if not flush or not deferred:
                        break

            def run_pipeline(steps, LA=2):
                n = len(steps)
                inflight = []
                for i_ in range(n + LA):
                    if i_ < n:
                        qk_fn, sbanks, pv_fn, ncols = steps[i_]
                        qk_fn()
                        pts = []
                        if len(sbanks) == 2:
                            if st["pt"] % 2:
                                st["pt"] += 1
                            i0 = st["pt"] % NPT
                            st["pt"] += 2
                            b0 = PS.index(sbanks[0])
                            assert PS.index(sbanks[1]) == b0 + 1 and ncols == 512
                            kb.op(act, list(sbanks), [pT[i0], pT[i0 + 1]], lambda e, i0=i0, b0=b0: e.activation(
                                out=ptbig.t[:, i0 * 512:(i0 + 2) * 512], in_=psbig[:, b0 * 512:(b0 + 2) * 512], func=AF.Exp))
                            pts = [pT[i0], pT[i0 + 1]]
                        else:
                            for sbk in sbanks:
                                p_ = pT[st["pt"] % NPT]; st["pt"] += 1
                                kb.op(act, [sbk], [p_], lambda e, sbk=sbk, p_=p_, ncols=ncols: e.activation(
                                    out=p_[:, 0:ncols], in_=sbk[:, 0:ncols], func=AF.Exp))
                                pts.append(p_)
                        inflight.append((pv_fn, pts))
                    if i_ - LA >= 0:
                        pv_fn, pts = inflight[i_ - LA]
                        pv_fn(pts)
                    run_deferred()
                run_deferred(flush=True)

            ktl = [[kb.sb(f"ktl{i}{c}", [68, S], BF16) for c in range(2)] for i in range(2)]
            ktu = [[kb.sb(f"ktu{i}{c}", [68, S], BF16) for c in range(2)] for i in range(2)]
            vhc = [kb.sb(f"vhc{i}", [128, NT, 128], BF16) for i in range(2)]
            qtc = [[kb.sb(f"qtc{i}{c}", [68, 512], BF16) for c in range(2)] for i in range(2)]
            SBk = [(PS[0], PS[1]), (PS[2], PS[3])]
            po1, po2, pdd = PS[4], PS[5], PS[6]
            pend = []
            jobs = [(h, qb) for h in range(4) for qb in range(NTB)]
            cnt = {"s": 0}

            def load_head_d(h):
                i = h % 2
                for c in range(2):
                    r0 = h * 128 + c * 64
                    kb.dma(ktl[i][c][0:64, :], KC.t[r0:r0 + 64, :], KC, ktl[i][c])
                    kb.dma(ktl[i][c][64:68, :], KAUG.t[h, 0], KAUG, ktl[i][c])
                    kb.dma(ktu[i][c][0:64, :], KC.t[r0:r0 + 64, :], KC, ktu[i][c])
                    kb.dma(ktu[i][c][64:68, :], KAUG.t[h, 1], KAUG, ktu[i][c])
                kb.dma(vhc[i][:], VHC.t[h], VHC, vhc[i])
                if h == 0:
                    gate_load(0, GC, 0)

            def load_q_d(ji):
                h, qb = jobs[ji]
                for c in range(2):
                    r0 = h * 128 + c * 64
                    kb.dma(qtc[ji % 2][c][0:64, :], QC.t[r0:r0 + 64, qb * 512:(qb + 1) * 512], QC, qtc[ji % 2][c])
                    kb.dma(qtc[ji % 2][c][64:68, :], QAUG.t[:, qb * 512:(qb + 1) * 512], QAUG, qtc[ji % 2][c])

            qta1 = kb.sb("qta1", [128, S], BF16)

            def swa_init_loads():
                for kv_ in range(2):
                    kb.dma(ktas[kv_][:], KA.t[kv_ * 128:(kv_ + 1) * 128, :], KA, ktas[kv_])
                    kb.dma(vhas[kv_][:], VHA.t[kv_], VHA, vhas[kv_])
                kb.dma(qta[0][:], QA.t[0:128, :], QA, qta[0])
                gate_load(0, GA, 0)

            if True:
                knb = [kb.sb(f"knb{i}", [128, S], BF16) for i in range(2)]
                vhb = [kb.sb(f"vhb{i}", [128, NT, 128], BF16) for i in range(2)]
                krb2 = kb.sb("krb2", [128, S], BF16)
                qn = [kb.sb(f"qn{i}", [128, 512], BF16) for i in range(2)]
                qr2 = [kb.sb(f"qr2{i}", [128, 512], BF16) for i in range(2)]
                kb.dma(krb2[0:64, :], KR.t, KR, krb2)
                kb.dma(krb2[64:128, :], KR.t, KR, krb2)
                ktas = knb
                vhas = vhb
                qta = [krb2, qta1]
                SP_ = [(PS[0], PS[1]), (PS[2], PS[3])]
                POs = [PS[4], PS[7]]
                pd = PS[5]
                aux = PS[6]
                steps = []
                pend = []
                cnt = {"s": 0}
                jobs = [(h, qb) for h in range(4) for qb in range(NTB)]
                NKP = NT // 2

                def load_head(h):
                    kb.dma(knb[h % 2][:], KN.t[h], KN, knb[h % 2])
                    kb.dma(vhb[h % 2][:], VHB.t[h], VHB, vhb[h % 2])
                    if h == 0:
                        gate_load(0, GB, 0)

                def load_q(ji):
                    h, qb = jobs[ji]
                    kb.dma(qn[ji % 2][:], QBN.t[h, :, qb * 512:(qb + 1) * 512], QBN, qn[ji % 2])
                    kb.dma(qr2[ji % 2][0:64, :], QBR.t[h, :, qb * 512:(qb + 1) * 512], QBR, qr2[ji % 2])
                    kb.dma(qr2[ji % 2][64:128, :], QBR.t[h, :, qb * 512:(qb + 1) * 512], QBR, qr2[ji % 2])

                for ji, (h, qb) in enumerate(jobs):
                    po = POs[ji % 2]
                    for kp in range(NKP):
                        sb2 = SP_[cnt["s"] % 2]; cnt["s"] += 1

                        def qk(ji=ji, h=h, qb=qb, kp=kp, sb2=sb2):
                            if kp == 0:
                                if qb == 0 and h == 0:
                                    load_head(0)
                                if ji == 0:
                                    load_q(0)
                                if ji + 1 < len(jobs):
                                    load_q(ji + 1)
                            if kp == 2 and qb == 0 and h + 1 < 4:
                                load_head(h + 1)
                            if kp == 0 and qb == NTB - 1 and h + 1 < 4:
                                gate_load((h + 1) % 2, GB, (h + 1) * 128)
                            if kp == 2 and ji == len(jobs) - 1:
                                load_head_d(0)
                                load_q_d(0)
                            kA, kB = 2 * kp, 2 * kp + 1
                            q_ = qn[ji % 2]
                            r_ = qr2[ji % 2]
                            for kk, sbk in ((kA, sb2[0]), (kB, sb2[1])):
                                kb.op(pe, [knb[h % 2], q_], [sbk], lambda e, kk=kk, sbk=sbk: e.matmul(
                                    sbk[:, 0:512], lhsT=knb[h % 2][:, kk * 128:(kk + 1) * 128], rhs=q_[:],
                                    start=True, stop=False), inc=False)
                            kb.op(pe, [krb2, r_], [sb2[0]], lambda e: e.matmul(
                                sb2[0][:, 0:512], lhsT=krb2[0:64, kA * 128:(kA + 1) * 128], rhs=r_[0:64, :],
                                start=False, stop=True, tile_position=(0, 0)), inc=False)
                            kb.op(pe, [krb2, r_], [sb2[1]], lambda e: e.matmul(
                                sb2[1][:, 0:512], lhsT=krb2[64:128, kB * 128:(kB + 1) * 128], rhs=r_[64:128, :],
                                start=False, stop=True, tile_position=(64, 0)))

                        def pv(pts, ji=ji, h=h, qb=qb, kp=kp, po=po):
                            for idx, kk in enumerate((2 * kp, 2 * kp + 1)):
                                p_ = pts[idx]
                                kb.op(pe, [vhb[h % 2], p_], [po], lambda e, kk=kk, p_=p_: e.matmul(
                                    po[:, 0:512], lhsT=vhb[h % 2][:, kk, :], rhs=p_[:], start=(kk == 0), stop=(kk == NT - 1)))
                                pend.append(p_)
                            if kp % 2 == 1:
                                for j, pj in enumerate(pend):
                                    kb.op(pe, [ones, pj], [pd], lambda e, j=j, pj=pj: e.matmul(
                                        pd[32 * j:32 * j + 32, 0:512], lhsT=ones[:, 0:32], rhs=pj[:], start=(kp == 1),
                                        stop=(kp == NKP - 1), tile_position=(0, 32 * j), skip_group_check=True), inc=(j == 3))
                                del pend[:]
                            if kp == NKP - 1:
                                kb.op(dve, [pd], [pdsb], lambda e: e.tensor_copy(out=pdsb[:], in_=pd[:, 0:512]))

                                def ep1():
                                    kb.op(pe, [sel32, pdsb], [aux], lambda e: e.matmul(
                                        aux[:, 0:512], lhsT=sel32[:], rhs=pdsb[:], start=True, stop=True))
                                    recip(aux, aux[:, 0:512], rd, rd[:])

                                def ep2():
                                    kb.op(dve, [po, rd], [oa], lambda e: e.tensor_tensor(out=oa[:], in0=po[:, 0:512], in1=rd[:], op=ALU.mult))
                                    gate_and_store(oa, h % 2, 1024 + h * 128, qb * 512)
                                defer(1, ep1)
                                defer(2, ep2)

                        steps.append((qk, list(sb2), pv, 512))
                run_pipeline(steps)

            if True:
                pend = []
                jobs = [(h, qb) for h in range(4) for qb in range(NTB)]
                cnt = {"s": 0}
                steps = []
                for ji, (h, qb) in enumerate(jobs):
                    for kbk in range(NT):
                        sb2 = SBk[cnt["s"] % 2]; cnt["s"] += 1

                        def qk(ji=ji, h=h, qb=qb, kbk=kbk, sb2=sb2):
                            if kbk == 0:
                                if ji + 1 < len(jobs):
                                    load_q_d(ji + 1)
                            if kbk == 3 and ji == len(jobs) - 1:
                                swa_init_loads()
                            if kbk == 3 and qb == 0 and h + 1 < 4:
                                load_head_d(h + 1)
                            if kbk == 0 and qb == NTB - 1 and h + 1 < 4:
                                gate_load((h + 1) % 2, GC, (h + 1) * 128)
                            i = h % 2
                            ks = slice(kbk * 128, (kbk + 1) * 128)
                            for c in range(2):
                                sbk = sb2[c]
                                q_ = qtc[ji % 2][c]
                                if kbk < 4 * qb or kbk >= 4 * qb + 4:
                                    kt = ktl[i][c] if kbk < 4 * qb else ktu[i][c]
                                    kb.op(pe, [kt, q_], [sbk], lambda e, kt=kt, sbk=sbk, q_=q_: e.matmul(
                                        sbk[:, 0:512], lhsT=kt[:, ks], rhs=q_[:], start=True, stop=True))
                                else:
                                    d_ = kbk - 4 * qb
                                    if d_ > 0:
                                        kb.op(pe, [ktu[i][c], q_], [sbk], lambda e, sbk=sbk, q_=q_, c=c, d_=d_: e.matmul(
                                            sbk[:, 0:d_ * 128], lhsT=ktu[i][c][:, ks], rhs=q_[:, 0:d_ * 128],
                                            start=True, stop=True, skip_group_check=True), inc=False)
                                    if d_ < 3:
                                        kb.op(pe, [ktl[i][c], q_], [sbk], lambda e, sbk=sbk, q_=q_, c=c, d_=d_: e.matmul(
                                            sbk[:, (d_ + 1) * 128:512], lhsT=ktl[i][c][:, ks], rhs=q_[:, (d_ + 1) * 128:512],
                                            start=True, stop=True, skip_group_check=True), inc=False)
                                    kb.op(pe, [ktl[i][c], q_], [sbk], lambda e, sbk=sbk, q_=q_, c=c, d_=d_: e.matmul(
                                        sbk[:, d_ * 128:(d_ + 1) * 128], lhsT=ktl[i][c][:, ks], rhs=q_[:, d_ * 128:(d_ + 1) * 128],
                                        start=True, stop=False, skip_group_check=True), inc=False)
                                    kb.op(pe, [ident, CH[h]], [sbk], lambda e, sbk=sbk, d_=d_: e.matmul(
                                        sbk[:, d_ * 128:(d_ + 1) * 128], lhsT=ident[:], rhs=CH[h][:],
                                        start=False, stop=True, skip_group_check=True))

                        def pv(pts, ji=ji, h=h, qb=qb, kbk=kbk):
                            i = h % 2
                            first, last = (kbk == 0), (kbk == NT - 1)
                            for (p_, po) in ((pts[0], po1), (pts[1], po2)):
                                kb.op(pe, [vhc[i], p_], [po], lambda e, p_=p_, po=po: e.matmul(
                                    po[:, 0:512], lhsT=vhc[i][:, kbk, :], rhs=p_[:], start=first, stop=last))
                                pend.append(p_)
                            if kbk % 2 == 1:
                                for j, pj in enumerate(pend):
                                    kb.op(pe, [ones, pj], [pdd], lambda e, j=j, pj=pj: e.matmul(
                                        pdd[32 * j:32 * j + 32, 0:512], lhsT=ones[:, 0:32], rhs=pj[:], start=(kbk == 1),
                                        stop=last, tile_position=(0, 32 * j), skip_group_check=True), inc=(j == 3))
                                del pend[:]
                            if last:
                                aux = PS[7]
                                kb.op(dve, [pdd], [pdsb], lambda e: e.tensor_copy(out=pdsb[:], in_=pdd[:, 0:512]))
                                kb.op(dve, [po1], [oa], lambda e: e.tensor_copy(out=oa[:], in_=po1[:, 0:512]))
                                kb.op(dve, [po2], [ob], lambda e: e.tensor_copy(out=ob[:], in_=po2[:, 0:512]))

                                def ep1():
                                    kb.op(pe, [selA, pdsb], [aux], lambda e: e.matmul(
                                        aux[:, 0:512], lhsT=selA[:], rhs=pdsb[:], start=True, stop=True))
                                    recip(aux, aux[:, 0:512], rd, rd[:])

                                def ep2():
                                    kb.op(pe, [selB, pdsb], [aux], lambda e: e.matmul(
                                        aux[:, 0:512], lhsT=selB[:], rhs=pdsb[:], start=True, stop=True))
                                    recip(aux, aux[:, 0:512], rd2, rd2[:])

                                def ep3():
                                    kb.op(dve, [oa, rd], [oa], lambda e: e.tensor_tensor(out=oa[:], in0=oa[:], in1=rd[:], op=ALU.mult))
                                    kb.op(dve, [ob, rd2], [ob], lambda e: e.tensor_tensor(out=ob[:], in0=ob[:], in1=rd2[:], op=ALU.mult))
                                    kb.op(dve, [ob, nlam, oa], [oa], lambda e: e.scalar_tensor_tensor(
                                        out=oa[:], in0=ob[:], scalar=nlam[:, l:l + 1], in1=oa[:], op0=ALU.mult, op1=ALU.add))
                                    kb.op(act, [oa], [osq], lambda e: e.activation(out=osq[:], in_=oa[:], func=AF.Square))

                                def ep4():
                                    kb.op(pe, [ones, osq], [aux], lambda e: e.matmul(
                                        aux[:, 0:512], lhsT=ones[:], rhs=osq[:], start=True, stop=True))
                                    rsqrt(aux, aux[:, 0:512], rs, rs[:], 128.0 * EPS)

                                def ep5():
                                    kb.op(dve, [oa, rs, subg], [ob], lambda e: e.scalar_tensor_tensor(
                                        out=ob[:], in0=oa[:], scalar=subg[:, l:l + 1], in1=rs[:], op0=ALU.mult, op1=ALU.mult))
                                    gate_and_store(ob, h % 2, 1536 + h * 128, qb * 512)
                                dg_ = 2 if NT >= 24 else 1
                                defer(1, ep1)
                                defer(1 + dg_, ep2)
                                defer(1 + 2 * dg_, ep3)
                                defer(1 + 3 * dg_, ep4)
                                defer(1 + 4 * dg_, ep5)

                        steps.append((qk, list(sb2), pv, 512))
                run_pipeline(steps)

            if True:
                SB_ = [PS[0], PS[1], PS[2]]
                OB_ = [(PS[3], PS[4]), (PS[5], PS[6])]
                cnt = {"s": 0}
                steps = []
                NG = NTB
                for hh in range(8):
                    kvh = hh // 4
                    kta = ktas[kvh]
                    vha = vhas[kvh]
                    for kbk in range(NT):
                        sbk = SB_[cnt["s"] % 3]; cnt["s"] += 1
                        qs0 = max(kbk - 1, 0)
                        qs1 = min(kbk + 1, NT - 1)
                        ncols = (qs1 - qs0 + 1) * 128
                        boff = (qs0 - (kbk - 1)) * 128

                        def qk(hh=hh, kvh=kvh, kbk=kbk, sbk=sbk, qs0=qs0, ncols=ncols, boff=boff, kta=kta, vha=vha):
                            if kbk == 0:
                                if hh + 1 < 8:
                                    kb.dma(qta[(hh + 1) % 2][:], QA.t[(hh + 1) * 128:(hh + 2) * 128, :], QA, qta[(hh + 1) % 2])
                            if kbk == 3 and hh + 1 < 8:
                                gate_load((hh + 1) % 2, GA, (hh + 1) * 128)
                            q_ = qta[hh % 2]
                            kb.op(pe, [kta, q_], [sbk], lambda e: e.matmul(
                                sbk[:, 0:ncols], lhsT=kta[:, kbk * 128:(kbk + 1) * 128], rhs=q_[:, qs0 * 128:qs0 * 128 + ncols],
                                start=True, stop=False), inc=False)
                            kb.op(pe, [ident, BH[hh]], [sbk], lambda e: e.matmul(
                                sbk[:, 0:ncols], lhsT=ident[:], rhs=BH[hh][:, boff:boff + ncols], start=False, stop=True))

                        def pv(pts, hh=hh, kbk=kbk, qs0=qs0, qs1=qs1, vha=vha):
                            p_ = pts[0]
                            for qs in range(qs0, qs1 + 1):
                                G = qs // 4
                                po, pd = OB_[G % 2]
                                col = (qs % 4) * 128
                                first_in_epoch = (qs == 4 * G) and (kbk == max(4 * G - 1, 0))
                                pc = (qs - qs0) * 128
                                kb.op(pe, [vha, p_], [po], lambda e, po=po, col=col, pc=pc, f=first_in_epoch: e.matmul(
                                    po[:, col:col + 128], lhsT=vha[:, kbk, :], rhs=p_[:, pc:pc + 128],
                                    start=f, stop=True, skip_group_check=True), inc=False)
                                kb.op(pe, [ones, p_], [pd], lambda e, pd=pd, col=col, pc=pc, f=first_in_epoch: e.matmul(
                                    pd[:, col:col + 128], lhsT=ones[:], rhs=p_[:, pc:pc + 128],
                                    start=f, stop=True, skip_group_check=True))
                            for G in range(NG):
                                if kbk == min(4 * G + 4, NT - 1):
                                    po, pd = OB_[G % 2]

                                    def ep1(pd=pd):
                                        recip(pd, pd[:, 0:512], rd, rd[:], esink, esink[:, l * 8 + hh:l * 8 + hh + 1])

                                    def ep2(po=po, G=G):
                                        kb.op(dve, [po, rd], [oa], lambda e, po=po: e.tensor_tensor(out=oa[:], in0=po[:, 0:512], in1=rd[:], op=ALU.mult))
                                        gate_and_store(oa, hh % 2, hh * 128, G * 512)
                                    defer(1, ep1)
                                    defer(2, ep2)

                        steps.append((qk, [sbk], pv, ncols))
                run_pipeline(steps)

        with kb.scope():
            wo_t = kb.sb("wo", [128, DC, D], BF16)
            woc = [Buf(wo_t.t[:, c, :], f"wo{c}") for c in range(DC)]
            wos = [kb.sb(f"wos{i}", [128, D], F32) for i in range(4)]
            for c in range(4):
                kb.dma(wos[c % 4][:], wout_in.t[l, c * 128:(c + 1) * 128, :], wout_in, wos[c % 4])
            for c in range(DC):
                if c >= 1 and c + 3 < DC:
                    kb.dma(wos[(c + 3) % 4][:], wout_in.t[l, (c + 3) * 128:(c + 4) * 128, :], wout_in, wos[(c + 3) % 4])
                if c % 2 == 0:
                    kb.op(dve, [wos[c % 4]], [woc[c]], lambda e, c=c: e.tensor_copy(out=woc[c][:], in_=wos[c % 4][:]))
                else:
                    kb.op(act, [wos[c % 4]], [woc[c]], lambda e, c=c: e.activation(out=woc[c][:], in_=wos[c % 4][:], func=AF.Copy))
            yt = [kb.sb(f"yt{i}", [128, DC, 512], BF16) for i in range(2)]
            xr = [kb.sb(f"xr{i}", [128, D], F32) for i in range(2)]
            xn = [kb.sb(f"xn{i}", [128, D], F32) for i in range(2)]
            last = (l == NL - 1)
            jk = kb.sb("jk", [128, D], BF16)
            ss2 = [kb.sb(f"ss2{i}", [128, 1], F32) for i in range(2)]
            rr2 = [kb.sb(f"rr2{i}", [128, 1], F32) for i in range(2)]
            if last:
                fg = kb.sb("fg", [128, D], F32)
                kb.dma(fg[:], fnorm_in.t.partition_broadcast(128), fnorm_in, fg)
            else:
                hb2 = [kb.sb(f"hb2{i}", [128, D], BF16) for i in range(2)]
                hst = [kb.sb(f"hst{i}", [128, DC, 256], BF16) for i in range(2)]
            psi_c = [0]

            def post_a(t):
                i = t % 2
                xo = xn[i]
                kb.op(act, [xo], [jk, ss2[i]], lambda e: e.activation(
                    out=jk[:], in_=xo[:], func=AF.Square, accum_out=ss2[i][:, 0:1]))
                rsqrt(ss2[i], ss2[i][:], rr2[i], rr2[i][:], float(D) * EPS)
                kb.op(dve, [xo, rr2[i]], [hb2[i]], lambda e: e.tensor_scalar(
                    out=hb2[i][:], in0=xo[:], scalar1=rr2[i][:, 0:1], scalar2=math.sqrt(D), op0=ALU.mult, op1=ALU.mult))

            def post(t):
                i = t % 2
                hs = hst[(t // 2) % 2]
                for half in range(2):
                    pb = PS[psi_c[0] % 8]; psi_c[0] += 1
                    pv = pb.t[:].bitcast(BF16)
                    for c8 in range(8):
                        c = half * 8 + c8
                        kb.op(pe, [hb2[i], ident], [pb], lambda e, c=c, c8=c8, pv=pv: e.transpose(
                            pv[:, c8 * 128:(c8 + 1) * 128], hb2[i][:, c * 128:(c + 1) * 128], ident[:]), inc=(c8 == 7))
                    o_ap = hs[:, half * 8:(half + 1) * 8, (t % 2) * 128:(t % 2 + 1) * 128]
                    i_ap = pv[:, 0:1024].rearrange("p (c k) -> p c k", c=8)
                    if half == 0:
                        kb.op(dve, [pb], [hs], lambda e: e.tensor_copy(out=o_ap, in_=i_ap))
                    else:
                        kb.op(act, [pb], [hs], lambda e: e.activation(out=o_ap, in_=i_ap, func=AF.Copy))
                if t % 2 == 1:
                    kb.dma(HT1.t[:, (t - 1) * 128:(t + 1) * 128].rearrange("(c p) s -> p c s", p=128), hs[:], hs, HT1)

            def ld_y(tb):
                kb.dma(yt[tb % 2][:], YT.t[:, tb * 512:(tb + 1) * 512].rearrange("(c p) s -> p c s", p=128), YT, yt[tb % 2])

            def ld_x(t):
                kb.dma(xr[t % 2][:], xsrc.t[t * 128:(t + 1) * 128, :], xsrc, xr[t % 2])

            ld_y(0)
            ld_x(0)
            for tb in range(NTB):
                if tb + 1 < NTB:
                    ld_y(tb + 1)
                for ti in range(4):
                    t = tb * 4 + ti
                    if t + 1 < NT:
                        ld_x(t + 1)
                    xo = xn[t % 2]
                    for dg in range(4):
                        pb = PS[psi_c[0] % 8]; psi_c[0] += 1
                        for c in range(DC):
                            kb.op(pe, [yt[tb % 2], woc[c]], [pb], lambda e, c=c, pb=pb, ti=ti, dg=dg, tb=tb: e.matmul(
                                pb[:, 0:512], lhsT=yt[tb % 2][:, c, ti * 128:(ti + 1) * 128], rhs=woc[c][:, dg * 512:(dg + 1) * 512],
                                start=(c == 0), stop=(c == DC - 1)), inc=(c == DC - 1))
                        kb.op(dve, [pb, xr[t % 2]], [xo], lambda e, pb=pb, dg=dg, t=t, xo=xo: e.tensor_tensor(
                            out=xo[:, dg * 512:(dg + 1) * 512], in0=pb[:, 0:512], in1=xr[t % 2][:, dg * 512:(dg + 1) * 512], op=ALU.add))
                    if not last:
                        kb.dma(X1.t[t * 128:(t + 1) * 128, :], xo[:], xo, X1)
                        post_a(t)
                        if t >= 1:
                            post(t - 1)
                        if t == NT - 1:
                            post(t)
                    else:
                        i = t % 2
                        kb.op(act, [xo], [jk, ss2[i]], lambda e, i=i, xo=xo: e.activation(
                            out=jk[:], in_=xo[:], func=AF.Square, accum_out=ss2[i][:, 0:1]))
                        rsqrt(ss2[i], ss2[i][:], rr2[i], rr2[i][:], float(D) * EPS)
                        kb.op(pool, [rr2[i]], [rr2[i]], lambda e, i=i: e.tensor_single_scalar(
                            out=rr2[i][:], in_=rr2[i][:], scalar=math.sqrt(D), op=ALU.mult))
                        kb.op(dve, [xo, rr2[i], fg], [xo], lambda e, i=i, xo=xo: e.scalar_tensor_tensor(
                            out=xo[:], in0=xo[:], scalar=rr2[i][:, 0:1], in1=fg[:], op0=ALU.mult, op1=ALU.mult))
                        kb.dma(out_d.t[t * 128:(t + 1) * 128, :], xo[:], xo, out_d)
    kb.barrier()


_NC_CACHE = {}


def _get_nc(S, NL, dbg=False):
    key = (S, NL, dbg)
    if key not in _NC_CACHE:
        _NC_CACHE[key] = build(S, NL, dbg)
    return _NC_CACHE[key]


def kernel(x, positions, norm_g, w_in, swa_sink, mla_q_norm, mla_w_uq, mla_kv_norm,
           mla_w_ukv, diff_lambda, diff_subln, w_out, final_norm):
    x = np.asarray(x, dtype=np.float32)
    B, S, _ = x.shape
    nc = _get_nc(S, 2)
    shared = {
        "positions": np.ascontiguousarray(np.asarray(positions, dtype=np.int32)),
        "norm_g": np.ascontiguousarray(np.asarray(norm_g, dtype=np.float32)),
        "w_in": np.ascontiguousarray(np.asarray(w_in, dtype=np.float32)),
        "swa_sink": np.ascontiguousarray(np.asarray(swa_sink, dtype=np.float32)),
        "mla_q_norm": np.ascontiguousarray(np.asarray(mla_q_norm, dtype=np.float32)),
        "mla_w_uq": np.ascontiguousarray(np.asarray(mla_w_uq, dtype=np.float32)),
        "mla_kv_norm": np.ascontiguousarray(np.asarray(mla_kv_norm, dtype=np.float32)),
        "mla_w_ukv": np.ascontiguousarray(np.asarray(mla_w_ukv, dtype=np.float32)),
        "diff_lambda": np.ascontiguousarray(np.asarray(diff_lambda, dtype=np.float32)),
        "diff_subln": np.ascontiguousarray(np.asarray(diff_subln, dtype=np.float32)),
        "w_out": np.ascontiguousarray(np.asarray(w_out, dtype=np.float32)),
        "final_norm": np.ascontiguousarray(np.asarray(final_norm, dtype=np.float32)),
    }
    in_maps = []
    for b in range(B):
        m = dict(shared)
        m["x"] = np.ascontiguousarray(x[b])
        in_maps.append(m)
    res = run_bass_kernel_spmd(nc, in_maps, core_ids=list(range(B)))
    return np.stack([np.asarray(r["out"], dtype=np.float32) for r in res.results], axis=0)
```

```python
import math
from contextlib import ExitStack

import numpy as np
import concourse.bass as bass
import concourse.mybir as mybir
from concourse.bass_utils import run_bass_kernel_spmd

F32 = mybir.dt.float32
BF16 = mybir.dt.bfloat16
I32 = mybir.dt.int32
ALU = mybir.AluOpType
AF = mybir.ActivationFunctionType
AX = mybir.AxisListType

D = 2048
DC = 16
D_IN = 5824
EPS = 1e-6
NEG = -30000.0
O_AQ, O_AK, O_AV, O_AG, O_CQB, O_CKV, O_KR, O_GB, O_QC, O_KC, O_VC, O_GC = (
    0, 1024, 1280, 1536, 2560, 2944, 3200, 3264, 3776, 4288, 4800, 5312)
SWA_SLOPES = [2.0 ** (-8.0 * (i + 1) / 8) for i in range(8)]
DIF_SLOPES = [2.0 ** (-8.0 * (i + 1) / 4) for i in range(4)]
SEM_LIMIT = 30000


class Buf:
    def __init__(self, t, name):
        self.t = t
        self.name = name
        self.w = {}
        self.r = {}
        self.dsem = None
        self.dcnt = 0
        self.dram = False

    def __getitem__(self, k):
        return self.t[k]


class Eng:
    def __init__(self, kb, eng, kind):
        self.kb = kb
        self.eng = eng
        self.kind = kind
        self.sem = kb.new_sem()
        self.own = {self.sem}
        self.cnt = 0
        self.seen = {}
        self.pending = False

    def need(self, tok, raw):
        sem, val = tok
        if sem in self.own:
            if self.kind in ("pe", "sp"):
                return
        if sem in self.kb.dma_sems:
            val = self.kb.latest[sem]
        if self.seen.get(sem, 0) >= val:
            return
        self.eng.wait_ge(sem, val)
        self.seen[sem] = val

    def bump(self, ins, inc=True):
        if inc:
            self.cnt += 1
            ins.then_inc(self.sem, 1)
            tok = (self.sem, self.cnt)
            self.pending = False
            self.kb.latest[self.sem] = self.cnt
            if self.cnt >= SEM_LIMIT:
                self.sem = self.kb.new_sem()
                self.own.add(self.sem)
                self.cnt = 0
            return tok
        self.pending = True
        return (self.sem, self.cnt + 1)


class KB:
    def __init__(self, nc, es):
        self.nc = nc
        self.es = es
        self.nsem = 0
        self.dma_sems = set()
        self.stores_on_pool = False
        self.free_dsems = []
        self.scope_bufs = [[]]
        self.latest = {}
        self.scopes = [es]
        self.pe = Eng(self, nc.tensor, "pe")
        self.act = Eng(self, nc.scalar, "act")
        self.dve = Eng(self, nc.vector, "dve")
        self.pool = Eng(self, nc.gpsimd, "pool")
        self.sp = Eng(self, nc.sync, "sp")
        self.engs = [self.pe, self.act, self.dve, self.pool, self.sp]
        self.nuid = 0

    def new_sem(self):
        self.nsem += 1
        return self.es.enter_context(self.nc.semaphore(f"sm{self.nsem}"))

    def uid(self, n):
        self.nuid += 1
        return f"{n}_{self.nuid}"

    def sb(self, name, shape, dtype):
        t = self.scopes[-1].enter_context(self.nc.sbuf_tensor(self.uid(name), list(shape), dtype))
        b = Buf(t, name)
        self.scope_bufs[-1].append(b)
        return b

    def ps(self, name):
        t = self.scopes[-1].enter_context(self.nc.psum_tensor(self.uid(name), [128, 512], F32))
        return Buf(t, name)

    def dram(self, name, shape, dtype):
        t = self.nc.dram_tensor(name, list(shape), dtype, kind="Internal")
        b = Buf(t.ap(), name)
        b.dram = True
        return b

    def op(self, E, reads, writes, fn, inc=True):
        for b in reads:
            for tok in list(b.w.values()):
                E.need(tok, True)
        for b in writes:
            for tok in list(b.w.values()) + list(b.r.values()):
                E.need(tok, False)
        ins = fn(E.eng)
        tok = E.bump(ins, inc)
        for b in reads:
            b.r[tok[0]] = tok
        for b in writes:
            b.r = {}
            b.w[tok[0]] = tok
        return tok

    def dma(self, out_ap, in_ap, src, dst, Q=None, **kw):
        if Q is None:
            Q = self.pool if (dst.dram and self.stores_on_pool) else self.sp
        own = dst if not dst.dram else src
        assert not own.dram
        if own.dsem is None and self.free_dsems:
            own.dsem, own.dcnt = self.free_dsems.pop()
        if own.dsem is None or own.dcnt + 16 >= SEM_LIMIT:
            own.dsem = self.new_sem()
            self.dma_sems.add(own.dsem)
            own.dcnt = 0
        for tok in list(src.w.values()):
            Q.need(tok, True)
        toks = list(dst.r.values())
        if not dst.dram:
            toks += list(dst.w.values())
        for tok in toks:
            if tok[0] == own.dsem:
                continue
            Q.need(tok, False)
        ins = Q.eng.dma_start(out=out_ap, in_=in_ap, **kw)
        own.dcnt += 16
        ins.then_inc(own.dsem, 16)
        tok = (own.dsem, own.dcnt)
        self.latest[own.dsem] = own.dcnt
        src.r[own.dsem] = tok
        dst.r = {}
        dst.w[own.dsem] = tok
        return tok

    def barrier(self):
        assert not self.pe.pending
        items = list(self.latest.items())
        for E in self.engs:
            for sem, val in items:
                if E.seen.get(sem, 0) >= val:
                    continue
                if sem in E.own and E.kind in ("pe", "sp"):
                    continue
                E.eng.wait_ge(sem, val)
                E.seen[sem] = val

    class _Scope:
        def __init__(self, kb):
            self.kb = kb

        def __enter__(self):
            self.st = ExitStack()
            self.st.__enter__()
            self.kb.scopes.append(self.st)
            self.kb.scope_bufs.append([])
            return self

        def __exit__(self, *a):
            self.kb.barrier()
            self.kb.scopes.pop()
            for b in self.kb.scope_bufs.pop():
                if b.dsem is not None:
                    self.kb.free_dsems.append((b.dsem, b.dcnt))
                    b.dsem = None
            return self.st.__exit__(*a)

    def scope(self):
        return KB._Scope(self)


def build(S=4096, NL=2, dbg=False):
    NT = S // 128
    NTB = S // 512
    nc = bass.Bass("TRN2", target_bir_lowering=False)
    es = ExitStack()
    with es:
        es.enter_context(nc.allow_low_precision("bf16 matmul operands, fp32 accumulation"))
        try:
            es.enter_context(nc.allow_non_contiguous_dma("layout shuffles"))
        except Exception:
            pass
        _build(nc, es, S, NL, NT, NTB, dbg)
    return nc


def _build(nc, es, S, NL, NT, NTB, dbg):
    kb = KB(nc, es)
    pe, act, dve, pool, sp = kb.pe, kb.act, kb.dve, kb.pool, kb.sp

    def din(name, shape, dt=F32):
        b = Buf(nc.dram_tensor(name, list(shape), dt, kind="ExternalInput").ap(), name)
        b.dram = True
        return b

    x_in = din("x", [S, D])
    pos_in = din("positions", [S], I32)
    normg_in = din("norm_g", [2, D])
    win_in = din("w_in", [2, D, D_IN])
    sink_in = din("swa_sink", [2, 8])
    qn_in = din("mla_q_norm", [2, 384])
    wuq_in = din("mla_w_uq", [2, 384, 768])
    kvn_in = din("mla_kv_norm", [2, 256])
    wukv_in = din("mla_w_ukv", [2, 256, 1024])
    lam_in = din("diff_lambda", [2, 4, 64])
    subln_in = din("diff_subln", [2, 128])
    wout_in = din("w_out", [2, D, D])
    fnorm_in = din("final_norm", [D])
    okind = "ExternalOutput"
    out_d = Buf(nc.dram_tensor("out", [S, D], F32, kind=okind).ap(), "out")
    out_d.dram = True

    def scr(name, shape, dt=BF16):
        if dbg:
            b = Buf(nc.dram_tensor(name, list(shape), dt, kind="ExternalOutput").ap(), name)
            b.dram = True
            return b
        return kb.dram(name, shape, dt)

    QA = scr("QA", [1024, S]); KA = scr("KA", [256, S]); GA = scr("GA", [1024, S])
    CQ = scr("CQ", [384, S]); CKV = scr("CKV", [256, S]); KR = scr("KRr", [64, S]); GB = scr("GB", [512, S])
    QC = scr("QC", [512, S]); KC = scr("KC", [512, S]); GC = scr("GC", [512, S])
    VHA = scr("VHA", [2, 128, NT, 128]); VHC = scr("VHC", [4, 128, NT, 128]); VHB = scr("VHB", [4, 128, NT, 128])
    QBN = scr("QBN", [4, 128, S]); QBR = scr("QBR", [4, 64, S]); KN = scr("KN", [4, 128, S])
    YT = scr("YT", [D, S])
    X1 = scr("X1", [S, D], F32)
    HT1 = scr("HT1", [D, S])
    COS = scr("COS", [64, S], F32); SIN = scr("SIN", [64, S], F32)
    QAUG = scr("QAUG", [4, S]); KAUG = scr("KAUG", [4, 2, 4, S])

    ident = kb.sb("ident", [128, 128], BF16)
    ones = kb.sb("ones", [128, 128], BF16)
    BH = [kb.sb(f"bh{h}", [128, 384], BF16) for h in range(8)]
    CH = [kb.sb(f"ch{h}", [128, 128], BF16) for h in range(4)]
    esink = kb.sb("esink", [128, 16], F32)
    nlam = kb.sb("nlam", [128, 2], F32)
    subg = kb.sb("subg", [128, 2], F32)
    gcol = kb.sb("gcol", [128, 2, 16], F32)
    qncol = kb.sb("qncol", [128, 2, 3], F32)
    kvncol = kb.sb("kvncol", [128, 2, 2], F32)
    psbig = es.enter_context(nc.psum_tensor("psbig", [128, 4096], F32))
    PS = [Buf(psbig[:, i * 512:(i + 1) * 512], f"ps{i}") for i in range(8)]
    epsb = {}
    for addc_ in (float(D) * EPS, 384.0 * EPS, 256.0 * EPS, 128.0 * EPS):
        epsb[addc_] = kb.sb("epsb", [128, 1], F32)
        kb.op(dve, [], [epsb[addc_]], lambda e, a=addc_: e.memset(epsb[a][:], float(a)))

    sel32 = kb.sb("sel32", [128, 128], F32)
    selA = kb.sb("selA", [128, 128], F32)
    selB = kb.sb("selB", [128, 128], F32)
    kb.op(dve, [], [sel32], lambda e: e.memset(sel32[:], 1.0 / 32.0))
    kb.op(dve, [], [selA], lambda e: e.memset(selA[:], 0.0))
    kb.op(dve, [], [selB], lambda e: e.memset(selB[:], 0.0))
    for p0 in (0, 64):
        kb.op(dve, [], [selA], lambda e, p0=p0: e.memset(selA[p0:p0 + 32, :], 1.0 / 32.0))
        kb.op(dve, [], [selB], lambda e, p0=p0: e.memset(selB[p0 + 32:p0 + 64, :], 1.0 / 32.0))
    onecol = kb.sb("onecol", [128, 1], F32)
    kb.op(dve, [], [onecol], lambda e: e.memset(onecol[:], 1.0))

    def recip(src, src_ap, dst, dst_ap, bias_buf=None, bias_ap=None):
        rd_ = [src] + ([bias_buf] if bias_buf is not None else [])
        if bias_ap is not None:
            kb.op(act, rd_, [dst], lambda e: e.activation(out=dst_ap, in_=src_ap, func=AF.Ln, bias=bias_ap))
        else:
            kb.op(act, rd_, [dst], lambda e: e.activation(out=dst_ap, in_=src_ap, func=AF.Ln))
        kb.op(act, [dst], [dst], lambda e: e.activation(out=dst_ap, in_=dst_ap, func=AF.Exp, scale=-1.0))

    def rsqrt(src, src_ap, dst, dst_ap, addc):
        np_ = dst_ap.shape[0]
        kb.op(act, [src, epsb[addc]], [dst], lambda e: e.activation(out=dst_ap, in_=src_ap, func=AF.Ln, bias=epsb[addc][0:np_, 0:1]))
        kb.op(act, [dst], [dst], lambda e: e.activation(out=dst_ap, in_=dst_ap, func=AF.Exp, scale=-0.5))

    with kb.scope():
        ii = kb.sb("ii", [128, 384], I32)
        ff = kb.sb("ff", [128, 384], F32)
        f2 = kb.sb("f2", [128, 384], F32)
        kb.op(pool, [], [ii], lambda e: e.iota(ii[:, 0:128], [[1, 128]], base=0, channel_multiplier=-1))
        kb.op(dve, [ii], [ff], lambda e: e.tensor_copy(out=ff[:, 0:128], in_=ii[:, 0:128]))
        kb.op(dve, [ff], [ident], lambda e: e.tensor_single_scalar(out=ident[:], in_=ff[:, 0:128], scalar=0.0, op=ALU.is_equal))
        kb.op(dve, [], [ones], lambda e: e.memset(ones[:], 1.0))
        kb.op(pool, [], [ii], lambda e: e.iota(ii[:, 0:128], [[-1, 128]], base=0, channel_multiplier=1))
        kb.op(dve, [ii], [ff], lambda e: e.tensor_copy(out=ff[:, 0:128], in_=ii[:, 0:128]))
        for h in range(4):
            kb.op(dve, [ff], [CH[h]], lambda e, h=h: e.tensor_scalar(
                out=CH[h][:], in0=ff[:, 0:128], scalar1=0.0, scalar2=-2.0 * DIF_SLOPES[h], op0=ALU.max, op1=ALU.mult))
        kb.op(pool, [], [ii], lambda e: e.iota(ii[:], [[1, 384]], base=-128, channel_multiplier=-1))
        kb.op(dve, [ii], [ff], lambda e: e.tensor_copy(out=ff[:], in_=ii[:]))
        kb.op(act, [ff], [ff], lambda e: e.activation(out=ff[:], in_=ff[:], func=AF.Abs))
        kb.op(dve, [ff], [f2], lambda e: e.tensor_scalar(
            out=f2[:], in0=ff[:], scalar1=128.5, scalar2=NEG, op0=ALU.is_gt, op1=ALU.mult))
        for h in range(8):
            kb.op(dve, [ff, f2], [BH[h]], lambda e, h=h: e.scalar_tensor_tensor(
                out=BH[h][:], in0=ff[:], scalar=-SWA_SLOPES[h], in1=f2[:], op0=ALU.mult, op1=ALU.add))

        sk = kb.sb("sk", [128, 16], F32)
        kb.dma(sk[:], sink_in.t.rearrange("l h -> (l h)").partition_broadcast(128), sink_in, sk)
        kb.op(act, [sk], [esink], lambda e: e.activation(out=esink[:], in_=sk[:], func=AF.Exp))
        lm = kb.sb("lm", [128, 2, 4, 64], F32)
        kb.dma(lm[:].rearrange("p l a b -> p (l a b)"),
               lam_in.t.rearrange("l a b -> (l a b)").partition_broadcast(128), lam_in, lm)
        pr = kb.sb("pr", [128, 2, 2, 64], F32)
        sm = kb.sb("sm", [128, 4], F32)
        for l in range(2):
            for j in range(2):
                kb.op(dve, [lm], [pr], lambda e, l=l, j=j: e.tensor_tensor(
                    out=pr[:, l, j, :], in0=lm[:, l, 2 * j, :], in1=lm[:, l, 2 * j + 1, :], op=ALU.mult))
                kb.op(dve, [pr], [sm], lambda e, l=l, j=j: e.reduce_sum(
                    out=sm[:, 2 * l + j:2 * l + j + 1], in_=pr[:, l, j, :], axis=AX.X))
        se = kb.sb("se", [128, 4], F32)
        kb.op(act, [sm], [se], lambda e: e.activation(out=se[:], in_=sm[:], func=AF.Exp))
        for l in range(2):
            lam_init = 0.8 - 0.6 * math.exp(-0.3 * l)
            kb.op(dve, [se], [nlam], lambda e, l=l, li=lam_init: e.scalar_tensor_tensor(
                out=nlam[:, l:l + 1], in0=se[:, 2 * l + 1:2 * l + 2], scalar=-li, in1=se[:, 2 * l:2 * l + 1],
                op0=ALU.add, op1=ALU.subtract))
        sg = kb.sb("sg", [128, 2], F32)
        kb.dma(sg[:].unsqueeze(2), subln_in.t.rearrange("l (p o) -> p l o", o=1), subln_in, sg)
        for l in range(2):
            lam_init = 0.8 - 0.6 * math.exp(-0.3 * l)
            kb.op(dve, [sg], [subg], lambda e, l=l, li=lam_init: e.tensor_single_scalar(
                out=subg[:, l:l + 1], in_=sg[:, l:l + 1], scalar=(1.0 - li) * math.sqrt(128.0), op=ALU.mult))
        kb.dma(gcol[:].unsqueeze(3), normg_in.t.rearrange("l (c p o) -> p l c o", p=128, o=1), normg_in, gcol)
        kb.dma(qncol[:].unsqueeze(3), qn_in.t.rearrange("l (c p o) -> p l c o", p=128, o=1), qn_in, qncol)
        kb.dma(kvncol[:].unsqueeze(3), kvn_in.t.rearrange("l (c p o) -> p l c o", p=128, o=1), kvn_in, kvncol)

        H2 = S // 2
        pi_ = kb.sb("pi_", [128, H2], I32)
        kb.dma(pi_[0:64, :], pos_in.t[0:H2].partition_broadcast(64), pos_in, pi_)
        kb.dma(pi_[64:128, :], pos_in.t[H2:S].partition_broadcast(64), pos_in, pi_)
        pf = kb.sb("pf", [128, H2], F32)
        kb.op(dve, [pi_], [pf], lambda e: e.tensor_copy(out=pf[:], in_=pi_[:]))
        invrow = kb.sb("invrow", [1, 4, 32], F32)
        for i_ in range(32):
            val = float(np.float32(10000.0) ** np.float32(-(2.0 * i_) / 64.0))
            kb.op(dve, [], [invrow], lambda e, i_=i_, val=val: e.memset(invrow[:, :, i_:i_ + 1], val))
        INVD = kb.dram("INVD", [128], F32)
        kb.dma(INVD.t.rearrange("(o n) -> o n", o=1), invrow[:].rearrange("o a b -> o (a b)"), invrow, INVD)
        inv = kb.sb("inv", [128, 1], F32)
        kb.dma(inv[:], INVD.t.rearrange("(p o) -> p o", o=1), INVD, inv)
        ang = kb.sb("ang", [128, H2], F32)
        kb.op(dve, [pf, inv], [ang], lambda e: e.tensor_scalar(
            out=ang[:], in0=pf[:], scalar1=inv[:, 0:1], scalar2=None, op0=ALU.mult))
        tr = kb.sb("tr", [128, H2], F32)
        kf = kb.sb("kf", [128, H2], F32)
        C1 = 6.28125
        C2 = 2.0 * math.pi - 6.28125
        for shift, dst in ((0.0, SIN), (0.5 * math.pi, COS)):
            kb.op(dve, [ang], [tr], lambda e, sh=shift: e.tensor_scalar(
                out=tr[:], in0=ang[:], scalar1=sh, scalar2=1.0 / (2.0 * math.pi), op0=ALU.add, op1=ALU.mult))
            kb.op(dve, [tr], [pi_], lambda e: e.tensor_copy(out=pi_[:], in_=tr[:]))
            kb.op(dve, [pi_], [kf], lambda e: e.tensor_copy(out=kf[:], in_=pi_[:]))
            kb.op(dve, [ang], [tr], lambda e, sh=shift: e.tensor_single_scalar(out=tr[:], in_=ang[:], scalar=sh, op=ALU.add))
            kb.op(dve, [kf, tr], [tr], lambda e: e.scalar_tensor_tensor(
                out=tr[:], in0=kf[:], scalar=-C1, in1=tr[:], op0=ALU.mult, op1=ALU.add))
            kb.op(dve, [kf, tr], [tr], lambda e: e.scalar_tensor_tensor(
                out=tr[:], in0=kf[:], scalar=-C2, in1=tr[:], op0=ALU.mult, op1=ALU.add))
            for thr, adj in ((math.pi, -2.0 * math.pi), (None, 2.0 * math.pi)):
                if thr is not None:
                    kb.op(dve, [tr], [kf], lambda e: e.tensor_scalar(
                        out=kf[:], in0=tr[:], scalar1=math.pi, scalar2=-2.0 * math.pi, op0=ALU.is_gt, op1=ALU.mult))
                else:
                    kb.op(dve, [tr], [kf], lambda e: e.tensor_scalar(
                        out=kf[:], in0=tr[:], scalar1=-math.pi, scalar2=2.0 * math.pi, op0=ALU.is_lt, op1=ALU.mult))
                kb.op(dve, [kf, tr], [tr], lambda e: e.tensor_tensor(out=tr[:], in0=tr[:], in1=kf[:], op=ALU.add))
            kb.op(dve, [tr], [tr], lambda e: e.tensor_scalar(
                out=tr[:], in0=tr[:], scalar1=3.1415925, scalar2=-3.1415925, op0=ALU.min, op1=ALU.max))
            kb.op(act, [tr], [pf], lambda e: e.activation(out=pf[:], in_=tr[:], func=AF.Sin))
            kb.dma(dst.t[:, 0:H2], pf[0:64, :], pf, dst)
            kb.dma(dst.t[:, H2:S], pf[64:128, :], pf, dst)

        NJ = S // 128
        p2 = kb.sb("p2", [128, NJ], I32)
        kb.dma(p2[:], pos_in.t.rearrange("(p j) -> p j", p=128), pos_in, p2)
        hi_i = kb.sb("hi_i", [128, NJ], I32)
        lo_i = kb.sb("lo_i", [128, NJ], I32)
        kb.op(dve, [p2], [hi_i], lambda e: e.tensor_single_scalar(out=hi_i[:], in_=p2[:], scalar=6, op=ALU.arith_shift_right))
        kb.op(dve, [p2], [lo_i], lambda e: e.tensor_single_scalar(out=lo_i[:], in_=p2[:], scalar=63, op=ALU.bitwise_and))
        hi_f = kb.sb("hi_f", [128, NJ], F32)
        lo_f = kb.sb("lo_f", [128, NJ], F32)
        kb.op(dve, [hi_i], [hi_f], lambda e: e.tensor_copy(out=hi_f[:], in_=hi_i[:]))
        kb.op(dve, [lo_i], [lo_f], lambda e: e.tensor_copy(out=lo_f[:], in_=lo_i[:]))
        qa = kb.sb("qa", [128, 4, NJ], BF16)
        kb.op(dve, [hi_f], [qa], lambda e: e.tensor_copy(out=qa[:, 0, :], in_=hi_f[:]))
        kb.op(dve, [lo_f], [qa], lambda e: e.tensor_copy(out=qa[:, 1, :], in_=lo_f[:]))
        kb.op(dve, [], [qa], lambda e: e.memset(qa[:, 2:4, :], 1.0))
        kb.dma(QAUG.t.rearrange("r (p j) -> p r j", p=128), qa[:], qa, QAUG)
        ka = kb.sb("ka", [128, 4, 2, 4, NJ], BF16)
        for h in range(4):
            s_ = DIF_SLOPES[h]
            for v, sg_ in ((0, 1.0), (1, -1.0)):
                kb.op(dve, [], [ka], lambda e, h=h, v=v, c=-64.0 * s_ * sg_: e.memset(ka[:, h, v, 0, :], c))
                kb.op(dve, [], [ka], lambda e, h=h, v=v, c=-s_ * sg_: e.memset(ka[:, h, v, 1, :], c))
                kb.op(dve, [hi_f], [ka], lambda e, h=h, v=v, c=64.0 * s_ * sg_: e.tensor_single_scalar(
                    out=ka[:, h, v, 2, :], in_=hi_f[:], scalar=c, op=ALU.mult))
                kb.op(dve, [lo_f], [ka], lambda e, h=h, v=v, c=s_ * sg_: e.tensor_single_scalar(
                    out=ka[:, h, v, 3, :], in_=lo_f[:], scalar=c, op=ALU.mult))
        kb.dma(KAUG.t.rearrange("h v r (p j) -> p (h v r) j", p=128),
               ka[:].rearrange("p h v r j -> p (h v r) j"), ka, KAUG)

    kb.stores_on_pool = False
    for l in range(NL):
        xsrc = x_in if l == 0 else X1
        lam_init = 0.8 - 0.6 * math.exp(-0.3 * l)
        with kb.scope():
            hT = kb.sb("hT", [128, DC, S], BF16)
            hTv = [Buf(hT.t[:, :, tb * 512:(tb + 1) * 512], f"hTv{tb}") for tb in range(NTB)]
            if l > 0:
                for tb in range(NTB):
                    kb.dma(hTv[tb][:], HT1.t[:, tb * 512:(tb + 1) * 512].rearrange("(c p) s -> p c s", p=128), HT1, hTv[tb])
            with kb.scope():
                NA1 = 4
                xs = [kb.sb(f"xs{i}", [128, D], F32) for i in range(NA1)]
                hb = [kb.sb(f"hb{i}", [128, D], BF16) for i in range(NA1)]
                junk = kb.sb("junk", [128, D], BF16)
                ssq = [kb.sb(f"ssq{i}", [128, 1], F32) for i in range(NA1)]
                rstd = [kb.sb(f"rstd{i}", [128, 1], F32) for i in range(NA1)]
                def a1_front(t):
                    i = t % NA1
                    kb.dma(xs[i][:], xsrc.t[t * 128:(t + 1) * 128, :], xsrc, xs[i])
                    kb.op(act, [xs[i]], [junk, ssq[i]], lambda e, i=i: e.activation(
                        out=junk[:], in_=xs[i][:], func=AF.Square, accum_out=ssq[i][:, 0:1]))
                    rsqrt(ssq[i], ssq[i][:], rstd[i], rstd[i][:], float(D) * EPS)
                    kb.op(dve, [xs[i], rstd[i]], [hb[i]], lambda e, i=i: e.tensor_scalar(
                        out=hb[i][:], in0=xs[i][:], scalar1=rstd[i][:, 0:1], scalar2=math.sqrt(D), op0=ALU.mult, op1=ALU.mult))

                def a1_back(t):
                    i = t % NA1
                    for half in range(2):
                        pb = PS[(2 * t + half) % 8]
                        pv = pb.t[:].bitcast(BF16)
                        for c8 in range(8):
                            c = half * 8 + c8
                            kb.op(pe, [hb[i], ident], [pb], lambda e, c=c, c8=c8, pv=pv, i=i: e.transpose(
                                pv[:, c8 * 128:(c8 + 1) * 128], hb[i][:, c * 128:(c + 1) * 128], ident[:]), inc=(c8 == 7))
                        kb.op(dve, [pb], [hTv[t // 4]], lambda e, half=half, pv=pv, t=t: e.tensor_copy(
                            out=hT[:, half * 8:(half + 1) * 8, t * 128:(t + 1) * 128],
                            in_=pv[:, 0:1024].rearrange("p (c k) -> p c k", c=8)))

                if l == 0:
                    a1_front(0)
                    a1_front(1)
                    for t in range(NT):
                        if t + 2 < NT:
                            a1_front(t + 2)
                        a1_back(t)
            with kb.scope():
                wst = [kb.sb(f"wst{i}", [128, DC, 128], F32) for i in range(2)]
                wb = [kb.sb(f"wb{i}", [128, DC, 128], BF16) for i in range(2)]
                stg = [kb.sb(f"stg{i}", [128, 2048], BF16) for i in range(2)]
                cs = [kb.sb(f"cs{i}", [64, 512], F32) for i in range(2)]
                sn = [kb.sb(f"sn{i}", [64, 512], F32) for i in range(2)]
                r1 = kb.sb("r1", [64, 512], F32)
                r2 = kb.sb("r2", [64, 512], F32)
                gtmp = [kb.sb(f"gtmp{i}", [128, 512], F32) for i in range(2)]
                gb_ = gcol[:, l, :].unsqueeze(2).to_broadcast([128, DC, 128])
                chunks = []
                for j in range(8):
                    chunks.append(("fm", O_AQ + j * 128, QA, j * 128, 128.0 ** -0.5))
                for j in range(2):
                    chunks.append(("fm", O_AK + j * 128, KA, j * 128, 1.0))
                for j in range(2):
                    chunks.append(("tm", O_AV + j * 128, VHA, j, 1.0))
                for j in range(8):
                    chunks.append(("fmg", O_AG + j * 128, GA, j * 128, 1.0))
                for j in range(3):
                    chunks.append(("fm", O_CQB + j * 128, CQ, j * 128, 1.0))
                for j in range(2):
                    chunks.append(("fm", O_CKV + j * 128, CKV, j * 128, 1.0))
                chunks.append(("kr", O_KR, KR, 0, 1.0))
                for j in range(4):
                    chunks.append(("fmg", O_GB + j * 128, GB, j * 128, 1.0))
                for j in range(4):
                    chunks.append(("fm", O_QC + j * 128, QC, j * 128, 0.125))
                for j in range(4):
                    chunks.append(("fm", O_KC + j * 128, KC, j * 128, 1.0))
                for j in range(4):
                    chunks.append(("tm", O_VC + j * 128, VHC, j, 1.0))
                for j in range(4):
                    chunks.append(("fmg", O_GC + j * 128, GC, j * 128, 1.0))

                def load_w(ci):
                    kind, c0, _, _, _ = chunks[ci]
                    i = ci % 2
                    ew = 64 if kind == "kr" else 128
                    kb.dma(wst[i][:, :, 0:ew],
                           win_in.t[l, :, c0:c0 + ew].rearrange("(c p) e -> p c e", p=128), win_in, wst[i])

                def cast_w(ci):
                    kind = chunks[ci][0]
                    i = ci % 2
                    E = dve if ci % 2 == 0 else pool
                    if kind != "kr":
                        kb.op(E, [wst[i], gcol], [wb[i]], lambda e: e.tensor_tensor(
                            out=wb[i][:], in0=wst[i][:], in1=gb_, op=ALU.mult))
                    else:
                        g64 = gcol[:, l, :].unsqueeze(2).to_broadcast([128, DC, 64])
                        g32 = gcol[:, l, :].unsqueeze(2).to_broadcast([128, DC, 32])
                        kb.op(dve, [wst[i], gcol], [wb[i]], lambda e: e.tensor_tensor(
                            out=wb[i][:, :, 0:64], in0=wst[i][:, :, 0:64], in1=g64, op=ALU.mult))
                        kb.op(dve, [wst[i], gcol], [wb[i]], lambda e: e.scalar_tensor_tensor(
                            out=wb[i][:, :, 64:96], in0=wst[i][:, :, 32:64], scalar=-1.0, in1=g32, op0=ALU.mult, op1=ALU.mult))
                        kb.op(dve, [wst[i], gcol], [wb[i]], lambda e: e.tensor_tensor(
                            out=wb[i][:, :, 96:128], in0=wst[i][:, :, 0:32], in1=g32, op=ALU.mult))

                load_w(0)
                cast_w(0)
                psi = 0
                evi = 0
                for ci, (kind, c0, dst, r0, scl) in enumerate(chunks):
                    if ci + 1 < len(chunks):
                        load_w(ci + 1)
                    i = ci % 2
                    w = wb[i]
                    if kind in ("fm", "fmg"):
                        for tb in range(NTB):
                            if tb == NTB // 2 and ci + 1 < len(chunks):
                                cast_w(ci + 1)
                            pb = PS[psi % 4]; psi += 1
                            for c in range(DC):
                                kb.op(pe, [w, hTv[tb]], [pb], lambda e, c=c, pb=pb, tb=tb: e.matmul(
                                    pb[:, 0:512], lhsT=w[:, c, :], rhs=hT[:, c, tb * 512:(tb + 1) * 512],
                                    start=(c == 0), stop=(c == DC - 1)), inc=(c == DC - 1))
                            sg_ = stg[(tb // 4) % 2]
                            so = (tb % 4) * 512
                            if kind == "fmg":
                                tg = gtmp[evi % 2]
                                kb.op(act, [pb], [tg], lambda e, pb=pb, tg=tg: e.activation(
                                    out=tg[:], in_=pb[:, 0:512], func=AF.Exp, scale=-1.0))
                                kb.op(act, [tg, onecol], [tg], lambda e, tg=tg: e.activation(
                                    out=tg[:], in_=tg[:], func=AF.Ln, bias=onecol[:, 0:1]))
                                kb.op(act, [tg], [tg], lambda e, tg=tg: e.activation(
                                    out=tg[:], in_=tg[:], func=AF.Exp, scale=-1.0))
                                kb.op(dve, [pb, tg], [sg_], lambda e, pb=pb, sg_=sg_, so=so, tg=tg: e.tensor_tensor(
                                    out=sg_[:, so:so + 512], in0=pb[:, 0:512], in1=tg[:], op=ALU.mult))
                            elif evi % 2 == 0:
                                kb.op(act, [pb], [sg_], lambda e, pb=pb, sg_=sg_, so=so: e.activation(
                                    out=sg_[:, so:so + 512], in_=pb[:, 0:512], func=AF.Copy, scale=float(scl)))
                            else:
                                kb.op(dve, [pb], [sg_], lambda e, pb=pb, sg_=sg_, so=so: e.tensor_single_scalar(
                                    out=sg_[:, so:so + 512], in_=pb[:, 0:512], scalar=float(scl), op=ALU.mult))
                            evi += 1
                            if tb % 4 == 3 or tb == NTB - 1:
                                t0 = (tb // 4) * 2048
                                n = (tb % 4 + 1) * 512
                                kb.dma(dst.t[r0:r0 + 128, t0:t0 + n], sg_[:, 0:n], sg_, dst)
                    elif kind == "tm":
                        for tb in range(NTB):
                            if tb == NTB // 2 and ci + 1 < len(chunks):
                                cast_w(ci + 1)
                            pb = PS[psi % 4]; psi += 1
                            for ti in range(4):
                                t = tb * 4 + ti
                                for c in range(DC):
                                    kb.op(pe, [w, hTv[tb]], [pb], lambda e, c=c, pb=pb, t=t, ti=ti: e.matmul(
                                        pb[:, ti * 128:(ti + 1) * 128], lhsT=hT[:, c, t * 128:(t + 1) * 128], rhs=w[:, c, :],
                                        start=(c == 0), stop=(c == DC - 1)), inc=(c == DC - 1))
                            sg_ = stg[(tb // 4) % 2]
                            so = (tb % 4) * 512
                            kb.op(dve, [pb], [sg_], lambda e, pb=pb, sg_=sg_, so=so: e.tensor_copy(
                                out=sg_[:, so:so + 512], in_=pb[:, 0:512]))
                            if tb % 4 == 3 or tb == NTB - 1:
                                n0 = (tb // 4) * 16
                                nn = (tb % 4 + 1) * 4
                                kb.dma(dst.t[r0, :, n0:n0 + nn, :],
                                       sg_[:, 0:nn * 128].rearrange("p (n e) -> p n e", e=128), sg_, dst)
                    else:
                        for tb in range(NTB):
                            if tb == NTB // 2 and ci + 1 < len(chunks):
                                cast_w(ci + 1)
                            pa = PS[psi % 4]; psi += 1
                            pb2 = PS[psi % 4]; psi += 1
                            j = tb % 2
                            kb.dma(cs[j][:], COS.t[:, tb * 512:(tb + 1) * 512], COS, cs[j])
                            kb.dma(sn[j][:], SIN.t[:, tb * 512:(tb + 1) * 512], SIN, sn[j])
                            for (pp, o) in ((pa, 0), (pb2, 64)):
                                for c in range(DC):
                                    kb.op(pe, [w, hTv[tb]], [pp], lambda e, c=c, pp=pp, o=o, tb=tb: e.matmul(
                                        pp[0:64, 0:512], lhsT=w[:, c, o:o + 64], rhs=hT[:, c, tb * 512:(tb + 1) * 512],
                                        start=(c == 0), stop=(c == DC - 1)), inc=(c == DC - 1))
                            kb.op(dve, [pa, cs[j]], [r1], lambda e, pa=pa, j=j: e.tensor_tensor(
                                out=r1[:], in0=pa[0:64, 0:512], in1=cs[j][:], op=ALU.mult))
                            kb.op(dve, [pb2, sn[j]], [r2], lambda e, pb2=pb2, j=j: e.tensor_tensor(
                                out=r2[:], in0=pb2[0:64, 0:512], in1=sn[j][:], op=ALU.mult))
                            sg_ = stg[(tb // 4) % 2]
                            so = (tb % 4) * 512
                            kb.op(dve, [r1, r2], [sg_], lambda e, sg_=sg_, so=so: e.tensor_tensor(
                                out=sg_[0:64, so:so + 512], in0=r1[:], in1=r2[:], op=ALU.add))
                            if tb % 4 == 3 or tb == NTB - 1:
                                t0 = (tb // 4) * 2048
                                n = (tb % 4 + 1) * 512
                                kb.dma(dst.t[0:64, t0:t0 + n], sg_[0:64, 0:n], sg_, dst)

        with kb.scope():
            QSC = math.sqrt(384.0) / math.sqrt(192.0)
            wq_st = kb.sb("wq_st", [128, 3, 768], F32)
            wkv_st = kb.sb("wkv_st", [128, 2, 1024], F32)
            wq = kb.sb("wq", [128, 3, 1024], BF16)
            wkv = kb.sb("wkv", [128, 2, 1024], BF16)
            kb.dma(wq_st[:], wuq_in.t[l].rearrange("(c p) e -> p c e", p=128), wuq_in, wq_st)
            kb.dma(wkv_st[:], wukv_in.t[l].rearrange("(c p) e -> p c e", p=128), wukv_in, wkv_st)
            for c in range(3):
                kb.op(dve, [wq_st, qncol], [wq], lambda e, c=c: e.tensor_scalar(
                    out=wq[:, c, 0:768], in0=wq_st[:, c, :], scalar1=qncol[:, l, c:c + 1], scalar2=QSC, op0=ALU.mult, op1=ALU.mult))
                for h in range(4):
                    b0 = h * 192 + 128
                    kb.op(dve, [wq_st, qncol], [wq], lambda e, c=c, h=h, b0=b0: e.tensor_scalar(
                        out=wq[:, c, 768 + h * 64:768 + h * 64 + 32], in0=wq_st[:, c, b0 + 32:b0 + 64],
                        scalar1=qncol[:, l, c:c + 1], scalar2=-QSC, op0=ALU.mult, op1=ALU.mult))
                    kb.op(dve, [wq_st, qncol], [wq], lambda e, c=c, h=h, b0=b0: e.tensor_scalar(
                        out=wq[:, c, 768 + h * 64 + 32:768 + h * 64 + 64], in0=wq_st[:, c, b0:b0 + 32],
                        scalar1=qncol[:, l, c:c + 1], scalar2=QSC, op0=ALU.mult, op1=ALU.mult))
            for c in range(2):
                for h in range(4):
                    kb.op(dve, [wkv_st, kvncol], [wkv], lambda e, c=c, h=h: e.tensor_scalar(
                        out=wkv[:, c, h * 128:(h + 1) * 128], in0=wkv_st[:, c, h * 256:h * 256 + 128],
                        scalar1=kvncol[:, l, c:c + 1], scalar2=16.0, op0=ALU.mult, op1=ALU.mult))
                    kb.op(dve, [wkv_st, kvncol], [wkv], lambda e, c=c, h=h: e.tensor_scalar(
                        out=wkv[:, c, 512 + h * 128:512 + (h + 1) * 128], in0=wkv_st[:, c, h * 256 + 128:h * 256 + 256],
                        scalar1=kvncol[:, l, c:c + 1], scalar2=16.0, op0=ALU.mult, op1=ALU.mult))
            NB0 = 3
            cq = [kb.sb(f"cq{i}", [128, 3, 512], BF16) for i in range(NB0)]
            ckv = [kb.sb(f"ckv{i}", [128, 2, 512], BF16) for i in range(NB0)]
            sq_ = [kb.sb(f"sq{i}", [128, 3, 512], BF16) for i in range(NB0)]
            sqk_ = [kb.sb(f"sqk{i}", [128, 2, 512], BF16) for i in range(NB0)]
            rq_ = [kb.sb(f"rq{i}", [128, 512], F32) for i in range(NB0)]
            rk_ = [kb.sb(f"rk{i}", [128, 512], F32) for i in range(NB0)]
            cqn_ = [kb.sb(f"cqn{i}", [128, 3, 512], BF16) for i in range(NB0)]
            ckvn_ = [kb.sb(f"ckvn{i}", [128, 2, 512], BF16) for i in range(NB0)]
            cs2 = kb.sb("cs2", [64, 512], F32)
            sn2 = kb.sb("sn2", [64, 512], F32)
            t1 = kb.sb("t1", [64, 512], F32)
            t2 = kb.sb("t2", [64, 512], F32)
            so_ = [kb.sb(f"so{i}", [128, 512], BF16) for i in range(4)]
            soi = 0
            psi = 0

            def ldc(tb):
                i = tb % NB0
                kb.dma(cq[i][:], CQ.t[:, tb * 512:(tb + 1) * 512].rearrange("(c p) s -> p c s", p=128), CQ, cq[i])
                kb.dma(ckv[i][:], CKV.t[:, tb * 512:(tb + 1) * 512].rearrange("(c p) s -> p c s", p=128), CKV, ckv[i])

            def prologue(tb):
                if tb + 1 < NTB:
                    ldc(tb + 1)
                i = tb % NB0
                sq, sqk, rq, rk, cqn, ckvn = sq_[i], sqk_[i], rq_[i], rk_[i], cqn_[i], ckvn_[i]
                kb.op(pool, [cq[i]], [sq], lambda e, i=i: e.tensor_tensor(out=sq[:], in0=cq[i][:], in1=cq[i][:], op=ALU.mult))
                kb.op(pool, [ckv[i]], [sqk], lambda e, i=i: e.tensor_tensor(out=sqk[:], in0=ckv[i][:], in1=ckv[i][:], op=ALU.mult))
                pA = PS[psi_[0] % 8]; psi_[0] += 1
                for c in range(3):
                    kb.op(pe, [ones, sq], [pA], lambda e, c=c, pA=pA: e.matmul(
                        pA[:, 0:512], lhsT=ones[:], rhs=sq[:, c, :], start=(c == 0), stop=(c == 2)), inc=(c == 2))
                rsqrt(pA, pA[:, 0:512], rq, rq[:], 384.0 * EPS)
                pB = PS[psi_[0] % 8]; psi_[0] += 1
                for c in range(2):
                    kb.op(pe, [ones, sqk], [pB], lambda e, c=c, pB=pB: e.matmul(
                        pB[:, 0:512], lhsT=ones[:], rhs=sqk[:, c, :], start=(c == 0), stop=(c == 1)), inc=(c == 1))
                rsqrt(pB, pB[:, 0:512], rk, rk[:], 256.0 * EPS)
                kb.op(pool, [cq[i], rq], [cqn], lambda e, i=i: e.tensor_tensor(
                    out=cqn[:], in0=cq[i][:], in1=rq[:].unsqueeze(1).to_broadcast([128, 3, 512]), op=ALU.mult))
                kb.op(pool, [ckv[i], rk], [ckvn], lambda e, i=i: e.tensor_tensor(
                    out=ckvn[:], in0=ckv[i][:], in1=rk[:].unsqueeze(1).to_broadcast([128, 2, 512]), op=ALU.mult))

            psi_ = [0]
            ldc(0)
            prologue(0)
            if NTB > 1:
                prologue(1)
            for tb in range(NTB):
                if tb + 2 < NTB:
                    prologue(tb + 2)
                i = tb % NB0
                cqn, ckvn = cqn_[i], ckvn_[i]
                psi = psi_[0]
                tsl = slice(tb * 512, (tb + 1) * 512)
                kb.dma(cs2[:], COS.t[:, tsl], COS, cs2)
                kb.dma(sn2[:], SIN.t[:, tsl], SIN, sn2)
                for h in range(4):
                    pq = PS[psi % 8]; psi += 1
                    for c in range(3):
                        kb.op(pe, [wq, cqn], [pq], lambda e, c=c, h=h, pq=pq, i=i: e.matmul(
                            pq[:, 0:512], lhsT=wq[:, c, h * 192:h * 192 + 128], rhs=cqn[:, c, :],
                            start=(c == 0), stop=(c == 2)), inc=(c == 2))
                    s_ = so_[soi % 4]; soi += 1
                    kb.op(act, [pq], [s_], lambda e, pq=pq, s_=s_: e.activation(out=s_[:], in_=pq[:, 0:512], func=AF.Copy))
                    kb.dma(QBN.t[h, :, tsl], s_[:], s_, QBN)
                    pr1 = PS[psi % 8]; psi += 1
                    pr2 = PS[psi % 8]; psi += 1
                    for (pp, o) in ((pr1, h * 192 + 128), (pr2, 768 + h * 64)):
                        for c in range(3):
                            kb.op(pe, [wq, cqn], [pp], lambda e, c=c, pp=pp, o=o, i=i: e.matmul(
                                pp[0:64, 0:512], lhsT=wq[:, c, o:o + 64], rhs=cqn[:, c, :],
                                start=(c == 0), stop=(c == 2)), inc=(c == 2))
                    kb.op(dve, [pr1, cs2], [t1], lambda e, pr1=pr1: e.tensor_tensor(
                        out=t1[:], in0=pr1[0:64, 0:512], in1=cs2[:], op=ALU.mult))
                    kb.op(dve, [pr2, sn2], [t2], lambda e, pr2=pr2: e.tensor_tensor(
                        out=t2[:], in0=pr2[0:64, 0:512], in1=sn2[:], op=ALU.mult))
                    s_ = so_[soi % 4]; soi += 1
                    kb.op(dve, [t1, t2], [s_], lambda e, s_=s_: e.tensor_tensor(out=s_[0:64, :], in0=t1[:], in1=t2[:], op=ALU.add))
                    kb.dma(QBR.t[h, :, tsl], s_[0:64, :], s_, QBR)
                    pk = PS[psi % 8]; psi += 1
                    for c in range(2):
                        kb.op(pe, [wkv, ckvn], [pk], lambda e, c=c, h=h, pk=pk, i=i: e.matmul(
                            pk[:, 0:512], lhsT=wkv[:, c, h * 128:(h + 1) * 128], rhs=ckvn[:, c, :],
                            start=(c == 0), stop=(c == 1)), inc=(c == 1))
                    s_ = so_[soi % 4]; soi += 1
                    kb.op(act, [pk], [s_], lambda e, pk=pk, s_=s_: e.activation(out=s_[:], in_=pk[:, 0:512], func=AF.Copy))
                    kb.dma(KN.t[h, :, tsl], s_[:], s_, KN)
                for ti in range(4):
                    pvv = PS[psi % 8]; psi += 1
                    for c in range(2):
                        kb.op(pe, [wkv, ckvn], [pvv], lambda e, c=c, ti=ti, pvv=pvv, i=i: e.matmul(
                            pvv[:, 0:512], lhsT=ckvn[:, c, ti * 128:(ti + 1) * 128], rhs=wkv[:, c, 512:1024],
                            start=(c == 0), stop=(c == 1)), inc=(c == 1))
                    s_ = so_[soi % 4]; soi += 1
                    if ti % 2 == 0:
                        kb.op(dve, [pvv], [s_], lambda e, pvv=pvv, s_=s_: e.tensor_copy(out=s_[:], in_=pvv[:, 0:512]))
                    else:
                        kb.op(act, [pvv], [s_], lambda e, pvv=pvv, s_=s_: e.activation(out=s_[:], in_=pvv[:, 0:512], func=AF.Copy))
                    kb.dma(VHB.t[:, :, tb * 4 + ti, :].rearrange("h p e -> p h e"),
                           s_[:].rearrange("p (h e) -> p h e", e=128), s_, VHB)
                psi_[0] = psi

        with kb.scope():
            NPT = 12
            ptbig = kb.sb("ptbig", [128, NPT * 512], BF16)
            pT = [Buf(ptbig.t[:, i * 512:(i + 1) * 512], f"pT{i}") for i in range(NPT)]
            pdsb = kb.sb("pdsb", [128, 512], F32)
            rd = kb.sb("rd", [128, 512], F32)
            rd2 = kb.sb("rd2", [128, 512], F32)
            oa = kb.sb("oa", [128, 512], F32)
            ob = kb.sb("ob", [128, 512], F32)
            osq = kb.sb("osq", [128, 512], BF16)
            rs = kb.sb("rs", [128, 512], F32)
            yo = [kb.sb(f"yo{i}", [128, 512], BF16) for i in range(2)]
            st = {"pt": 0, "g": 0, "y": 0}

            ghd = [kb.sb(f"ghd{i}", [128, S], BF16) for i in range(2)]

            def gate_load(slot, Gsrc, grow):
                kb.dma(ghd[slot][:], Gsrc.t[grow:grow + 128, :], Gsrc, ghd[slot])

            def gate_and_store(o_buf, slot, yrow, q0, n=512):
                g = ghd[slot]
                y = yo[st["y"] % 2]; st["y"] += 1
                kb.op(dve, [g, o_buf], [y], lambda e: e.tensor_tensor(out=y[:, 0:n], in0=g[:, q0:q0 + n], in1=o_buf[:, 0:n], op=ALU.mult))
                kb.dma(YT.t[yrow:yrow + 128, q0:q0 + n], y[:, 0:n], y, YT)

            deferred = []

            def defer(k, fn):
                deferred.append([k, fn])

            def run_deferred(flush=False):
                while True:
                    for d in deferred:
                        d[0] -= 1
                    due = [d for d in deferred if d[0] <= 0]
                    for d in due:
                        deferred.remove(d)
                        d[1]()
                    if not flush or not deferred:
                        break

            def run_pipeline(steps, LA=2):
                n = len(steps)
                inflight = []
                for i_ in range(n + LA):
                    if i_ < n:
                        qk_fn, sbanks, pv_fn, ncols = steps[i_]
                        qk_fn()
                        pts = []
                        if len(sbanks) == 2:
                            if st["pt"] % 2:
                                st["pt"] += 1
                            i0 = st["pt"] % NPT
                            st["pt"] += 2
                            b0 = PS.index(sbanks[0])
                            assert PS.index(sbanks[1]) == b0 + 1 and ncols == 512
                            kb.op(act, list(sbanks), [pT[i0], pT[i0 + 1]], lambda e, i0=i0, b0=b0: e.activation(
                                out=ptbig.t[:, i0 * 512:(i0 + 2) * 512], in_=psbig[:, b0 * 512:(b0 + 2) * 512], func=AF.Exp))
                            pts = [pT[i0], pT[i0 + 1]]
                        else:
                            for sbk in sbanks:
                                p_ = pT[st["pt"] % NPT]; st["pt"] += 1
                                kb.op(act, [sbk], [p_], lambda e, sbk=sbk, p_=p_, ncols=ncols: e.activation(
                                    out=p_[:, 0:ncols], in_=sbk[:, 0:ncols], func=AF.Exp))
                                pts.append(p_)
                        inflight.append((pv_fn, pts))
                    if i_ - LA >= 0:
                        pv_fn, pts = inflight[i_ - LA]
                        pv_fn(pts)
                    run_deferred()
                run_deferred(flush=True)

            ktl = [[kb.sb(f"ktl{i}{c}", [68, S], BF16) for c in range(2)] for i in range(2)]
            ktu = [[kb.sb(f"ktu{i}{c}", [68, S], BF16) for c in range(2)] for i in range(2)]
            vhc = [kb.sb(f"vhc{i}", [128, NT, 128], BF16) for i in range(2)]
            qtc = [[kb.sb(f"qtc{i}{c}", [68, 512], BF16) for c in range(2)] for i in range(2)]
            SBk = [(PS[0], PS[1]), (PS[2], PS[3])]
            po1, po2, pdd = PS[4], PS[5], PS[6]
            pend = []
            jobs = [(h, qb) for h in range(4) for qb in range(NTB)]
            cnt = {"s": 0}

            def load_head_d(h):
                i = h % 2
                for c in range(2):
                    r0 = h * 128 + c * 64
                    kb.dma(ktl[i][c][0:64, :], KC.t[r0:r0 + 64, :], KC, ktl[i][c])
                    kb.dma(ktl[i][c][64:68, :], KAUG.t[h, 0], KAUG, ktl[i][c])
                    kb.dma(ktu[i][c][0:64, :], KC.t[r0:r0 + 64, :], KC, ktu[i][c])
                    kb.dma(ktu[i][c][64:68, :], KAUG.t[h, 1], KAUG, ktu[i][c])
                kb.dma(vhc[i][:], VHC.t[h], VHC, vhc[i])
                if h == 0:
                    gate_load(0, GC, 0)

            def load_q_d(ji):
                h, qb = jobs[ji]
                for c in range(2):
                    r0 = h * 128 + c * 64
                    kb.dma(qtc[ji % 2][c][0:64, :], QC.t[r0:r0 + 64, qb * 512:(qb + 1) * 512], QC, qtc[ji % 2][c])
                    kb.dma(qtc[ji % 2][c][64:68, :], QAUG.t[:, qb * 512:(qb + 1) * 512], QAUG, qtc[ji % 2][c])

            qta1 = kb.sb("qta1", [128, S], BF16)

            def swa_init_loads():
                for kv_ in range(2):
                    kb.dma(ktas[kv_][:], KA.t[kv_ * 128:(kv_ + 1) * 128, :], KA, ktas[kv_])
                    kb.dma(vhas[kv_][:], VHA.t[kv_], VHA, vhas[kv_])
                kb.dma(qta[0][:], QA.t[0:128, :], QA, qta[0])
                gate_load(0, GA, 0)

            if True:
                knb = [kb.sb(f"knb{i}", [128, S], BF16) for i in range(2)]
                vhb = [kb.sb(f"vhb{i}", [128, NT, 128], BF16) for i in range(2)]
                krb2 = kb.sb("krb2", [128, S], BF16)
                qn = [kb.sb(f"qn{i}", [128, 512], BF16) for i in range(2)]
                qr2 = [kb.sb(f"qr2{i}", [128, 512], BF16) for i in range(2)]
                kb.dma(krb2[0:64, :], KR.t, KR, krb2)
                kb.dma(krb2[64:128, :], KR.t, KR, krb2)
                ktas = knb
                vhas = vhb
                qta = [krb2, qta1]
                SP_ = [(PS[0], PS[1]), (PS[2], PS[3])]
                POs = [PS[4], PS[7]]
                pd = PS[5]
                aux = PS[6]
                steps = []
                pend = []
                cnt = {"s": 0}
                jobs = [(h, qb) for h in range(4) for qb in range(NTB)]
                NKP = NT // 2

                def load_head(h):
                    kb.dma(knb[h % 2][:], KN.t[h], KN, knb[h % 2])
                    kb.dma(vhb[h % 2][:], VHB.t[h], VHB, vhb[h % 2])
                    if h == 0:
                        gate_load(0, GB, 0)

                def load_q(ji):
                    h, qb = jobs[ji]
                    kb.dma(qn[ji % 2][:], QBN.t[h, :, qb * 512:(qb + 1) * 512], QBN, qn[ji % 2])
                    kb.dma(qr2[ji % 2][0:64, :], QBR.t[h, :, qb * 512:(qb + 1) * 512], QBR, qr2[ji % 2])
                    kb.dma(qr2[ji % 2][64:128, :], QBR.t[h, :, qb * 512:(qb + 1) * 512], QBR, qr2[ji % 2])

                for ji, (h, qb) in enumerate(jobs):
                    po = POs[ji % 2]
                    for kp in range(NKP):
                        sb2 = SP_[cnt["s"] % 2]; cnt["s"] += 1

                        def qk(ji=ji, h=h, qb=qb, kp=kp, sb2=sb2):
                            if kp == 0:
                                if qb == 0 and h == 0:
                                    load_head(0)
                                if ji == 0:
                                    load_q(0)
                                if ji + 1 < len(jobs):
                                    load_q(ji + 1)
                            if kp == 2 and qb == 0 and h + 1 < 4:
                                load_head(h + 1)
                            if kp == 0 and qb == NTB - 1 and h + 1 < 4:
                                gate_load((h + 1) % 2, GB, (h + 1) * 128)
                            if kp == 2 and ji == len(jobs) - 1:
                                load_head_d(0)
                                load_q_d(0)
                            kA, kB = 2 * kp, 2 * kp + 1
                            q_ = qn[ji % 2]
                            r_ = qr2[ji % 2]
                            for kk, sbk in ((kA, sb2[0]), (kB, sb2[1])):
                                kb.op(pe, [knb[h % 2], q_], [sbk], lambda e, kk=kk, sbk=sbk: e.matmul(
                                    sbk[:, 0:512], lhsT=knb[h % 2][:, kk * 128:(kk + 1) * 128], rhs=q_[:],
                                    start=True, stop=False), inc=False)
                            kb.op(pe, [krb2, r_], [sb2[0]], lambda e: e.matmul(
                                sb2[0][:, 0:512], lhsT=krb2[0:64, kA * 128:(kA + 1) * 128], rhs=r_[0:64, :],
                                start=False, stop=True, tile_position=(0, 0)), inc=False)
                            kb.op(pe, [krb2, r_], [sb2[1]], lambda e: e.matmul(
                                sb2[1][:, 0:512], lhsT=krb2[64:128, kB * 128:(kB + 1) * 128], rhs=r_[64:128, :],
                                start=False, stop=True, tile_position=(64, 0)))

                        def pv(pts, ji=ji, h=h, qb=qb, kp=kp, po=po):
                            for idx, kk in enumerate((2 * kp, 2 * kp + 1)):
                                p_ = pts[idx]
                                kb.op(pe, [vhb[h % 2], p_], [po], lambda e, kk=kk, p_=p_: e.matmul(
                                    po[:, 0:512], lhsT=vhb[h % 2][:, kk, :], rhs=p_[:], start=(kk == 0), stop=(kk == NT - 1)))
                                pend.append(p_)
                            if kp % 2 == 1:
                                for j, pj in enumerate(pend):
                                    kb.op(pe, [ones, pj], [pd], lambda e, j=j, pj=pj: e.matmul(
                                        pd[32 * j:32 * j + 32, 0:512], lhsT=ones[:, 0:32], rhs=pj[:], start=(kp == 1),
                                        stop=(kp == NKP - 1), tile_position=(0, 32 * j), skip_group_check=True), inc=(j == 3))
                                del pend[:]
                            if kp == NKP - 1:
                                kb.op(dve, [pd], [pdsb], lambda e: e.tensor_copy(out=pdsb[:], in_=pd[:, 0:512]))

                                def ep1():
                                    kb.op(pe, [sel32, pdsb], [aux], lambda e: e.matmul(
                                        aux[:, 0:512], lhsT=sel32[:], rhs=pdsb[:], start=True, stop=True))
                                    recip(aux, aux[:, 0:512], rd, rd[:])

                                def ep2():
                                    kb.op(dve, [po, rd], [oa], lambda e: e.tensor_tensor(out=oa[:], in0=po[:, 0:512], in1=rd[:], op=ALU.mult))
                                    gate_and_store(oa, h % 2, 1024 + h * 128, qb * 512)
                                defer(1, ep1)
                                defer(2, ep2)

                        steps.append((qk, list(sb2), pv, 512))
                run_pipeline(steps)

            if True:
                pend = []
                jobs = [(h, qb) for h in range(4) for qb in range(NTB)]
                cnt = {"s": 0}
                steps = []
                for ji, (h, qb) in enumerate(jobs):
                    for kbk in range(NT):
                        sb2 = SBk[cnt["s"] % 2]; cnt["s"] += 1

                        def qk(ji=ji, h=h, qb=qb, kbk=kbk, sb2=sb2):
                            if kbk == 0:
                                if ji + 1 < len(jobs):
                                    load_q_d(ji + 1)
                            if kbk == 3 and ji == len(jobs) - 1:
                                swa_init_loads()
                            if kbk == 3 and qb == 0 and h + 1 < 4:
                                load_head_d(h + 1)
                            if kbk == 0 and qb == NTB - 1 and h + 1 < 4:
                                gate_load((h + 1) % 2, GC, (h + 1) * 128)
                            i = h % 2
                            ks = slice(kbk * 128, (kbk + 1) * 128)
                            for c in range(2):
                                sbk = sb2[c]
                                q_ = qtc[ji % 2][c]
                                if kbk < 4 * qb or kbk >= 4 * qb + 4:
                                    kt = ktl[i][c] if kbk < 4 * qb else ktu[i][c]
                                    kb.op(pe, [kt, q_], [sbk], lambda e, kt=kt, sbk=sbk, q_=q_: e.matmul(
                                        sbk[:, 0:512], lhsT=kt[:, ks], rhs=q_[:], start=True, stop=True))
                                else:
                                    d_ = kbk - 4 * qb
                                    if d_ > 0:
                                        kb.op(pe, [ktu[i][c], q_], [sbk], lambda e, sbk=sbk, q_=q_, c=c, d_=d_: e.matmul(
                                            sbk[:, 0:d_ * 128], lhsT=ktu[i][c][:, ks], rhs=q_[:, 0:d_ * 128],
                                            start=True, stop=True, skip_group_check=True), inc=False)
                                    if d_ < 3:
                                        kb.op(pe, [ktl[i][c], q_], [sbk], lambda e, sbk=sbk, q_=q_, c=c, d_=d_: e.matmul(
                                            sbk[:, (d_ + 1) * 128:512], lhsT=ktl[i][c][:, ks], rhs=q_[:, (d_ + 1) * 128:512],
                                            start=True, stop=True, skip_group_check=True), inc=False)
                                    kb.op(pe, [ktl[i][c], q_], [sbk], lambda e, sbk=sbk, q_=q_, c=c, d_=d_: e.matmul(
                                        sbk[:, d_ * 128:(d_ + 1) * 128], lhsT=ktl[i][c][:, ks], rhs=q_[:, d_ * 128:(d_ + 1) * 128],
                                        start=True, stop=False, skip_group_check=True), inc=False)
                                    kb.op(pe, [ident, CH[h]], [sbk], lambda e, sbk=sbk, d_=d_: e.matmul(
                                        sbk[:, d_ * 128:(d_ + 1) * 128], lhsT=ident[:], rhs=CH[h][:],
                                        start=False, stop=True, skip_group_check=True))

                        def pv(pts, ji=ji, h=h, qb=qb, kbk=kbk):
                            i = h % 2
                            first, last = (kbk == 0), (kbk == NT - 1)
                            for (p_, po) in ((pts[0], po1), (pts[1], po2)):
                                kb.op(pe, [vhc[i], p_], [po], lambda e, p_=p_, po=po: e.matmul(
                                    po[:, 0:512], lhsT=vhc[i][:, kbk, :], rhs=p_[:], start=first, stop=last))
                                pend.append(p_)
                            if kbk % 2 == 1:
                                for j, pj in enumerate(pend):
                                    kb.op(pe, [ones, pj], [pdd], lambda e, j=j, pj=pj: e.matmul(
                                        pdd[32 * j:32 * j + 32, 0:512], lhsT=ones[:, 0:32], rhs=pj[:], start=(kbk == 1),
                                        stop=last, tile_position=(0, 32 * j), skip_group_check=True), inc=(j == 3))
                                del pend[:]
                            if last:
                                aux = PS[7]
                                kb.op(dve, [pdd], [pdsb], lambda e: e.tensor_copy(out=pdsb[:], in_=pdd[:, 0:512]))
                                kb.op(dve, [po1], [oa], lambda e: e.tensor_copy(out=oa[:], in_=po1[:, 0:512]))
                                kb.op(dve, [po2], [ob], lambda e: e.tensor_copy(out=ob[:], in_=po2[:, 0:512]))

                                def ep1():
                                    kb.op(pe, [selA, pdsb], [aux], lambda e: e.matmul(
                                        aux[:, 0:512], lhsT=selA[:], rhs=pdsb[:], start=True, stop=True))
                                    recip(aux, aux[:, 0:512], rd, rd[:])

                                def ep2():
                                    kb.op(pe, [selB, pdsb], [aux], lambda e: e.matmul(
                                        aux[:, 0:512], lhsT=selB[:], rhs=pdsb[:], start=True, stop=True))
                                    recip(aux, aux[:, 0:512], rd2, rd2[:])

                                def ep3():
                                    kb.op(dve, [oa, rd], [oa], lambda e: e.tensor_tensor(out=oa[:], in0=oa[:], in1=rd[:], op=ALU.mult))
                                    kb.op(dve, [ob, rd2], [ob], lambda e: e.tensor_tensor(out=ob[:], in0=ob[:], in1=rd2[:], op=ALU.mult))
                                    kb.op(dve, [ob, nlam, oa], [oa], lambda e: e.scalar_tensor_tensor(
                                        out=oa[:], in0=ob[:], scalar=nlam[:, l:l + 1], in1=oa[:], op0=ALU.mult, op1=ALU.add))
                                    kb.op(act, [oa], [osq], lambda e: e.activation(out=osq[:], in_=oa[:], func=AF.Square))

                                def ep4():
                                    kb.op(pe, [ones, osq], [aux], lambda e: e.matmul(
                                        aux[:, 0:512], lhsT=ones[:], rhs=osq[:], start=True, stop=True))
                                    rsqrt(aux, aux[:, 0:512], rs, rs[:], 128.0 * EPS)

                                def ep5():
                                    kb.op(dve, [oa, rs, subg], [ob], lambda e: e.scalar_tensor_tensor(
                                        out=ob[:], in0=oa[:], scalar=subg[:, l:l + 1], in1=rs[:], op0=ALU.mult, op1=ALU.mult))
                                    gate_and_store(ob, h % 2, 1536 + h * 128, qb * 512)
                                dg_ = 2 if NT >= 24 else 1
                                defer(1, ep1)
                                defer(1 + dg_, ep2)
                                defer(1 + 2 * dg_, ep3)
                                defer(1 + 3 * dg_, ep4)
                                defer(1 + 4 * dg_, ep5)

                        steps.append((qk, list(sb2), pv, 512))
                run_pipeline(steps)

            if True:
                SB_ = [PS[0], PS[1], PS[2]]
                OB_ = [(PS[3], PS[4]), (PS[5], PS[6])]
                cnt = {"s": 0}
                steps = []
                NG = NTB
                for hh in range(8):
                    kvh = hh // 4
                    kta = ktas[kvh]
                    vha = vhas[kvh]
                    for kbk in range(NT):
                        sbk = SB_[cnt["s"] % 3]; cnt["s"] += 1
                        qs0 = max(kbk - 1, 0)
                        qs1 = min(kbk + 1, NT - 1)
                        ncols = (qs1 - qs0 + 1) * 128
                        boff = (qs0 - (kbk - 1)) * 128

                        def qk(hh=hh, kvh=kvh, kbk=kbk, sbk=sbk, qs0=qs0, ncols=ncols, boff=boff, kta=kta, vha=vha):
                            if kbk == 0:
                                if hh + 1 < 8:
                                    kb.dma(qta[(hh + 1) % 2][:], QA.t[(hh + 1) * 128:(hh + 2) * 128, :], QA, qta[(hh + 1) % 2])
                            if kbk == 3 and hh + 1 < 8:
                                gate_load((hh + 1) % 2, GA, (hh + 1) * 128)
                            q_ = qta[hh % 2]
                            kb.op(pe, [kta, q_], [sbk], lambda e: e.matmul(
                                sbk[:, 0:ncols], lhsT=kta[:, kbk * 128:(kbk + 1) * 128], rhs=q_[:, qs0 * 128:qs0 * 128 + ncols],
                                start=True, stop=False), inc=False)
                            kb.op(pe, [ident, BH[hh]], [sbk], lambda e: e.matmul(
                                sbk[:, 0:ncols], lhsT=ident[:], rhs=BH[hh][:, boff:boff + ncols], start=False, stop=True))

                        def pv(pts, hh=hh, kbk=kbk, qs0=qs0, qs1=qs1, vha=vha):
                            p_ = pts[0]
                            for qs in range(qs0, qs1 + 1):
                                G = qs // 4
                                po, pd = OB_[G % 2]
                                col = (qs % 4) * 128
                                first_in_epoch = (qs == 4 * G) and (kbk == max(4 * G - 1, 0))
                                pc = (qs - qs0) * 128
                                kb.op(pe, [vha, p_], [po], lambda e, po=po, col=col, pc=pc, f=first_in_epoch: e.matmul(
                                    po[:, col:col + 128], lhsT=vha[:, kbk, :], rhs=p_[:, pc:pc + 128],
                                    start=f, stop=True, skip_group_check=True), inc=False)
                                kb.op(pe, [ones, p_], [pd], lambda e, pd=pd, col=col, pc=pc, f=first_in_epoch: e.matmul(
                                    pd[:, col:col + 128], lhsT=ones[:], rhs=p_[:, pc:pc + 128],
                                    start=f, stop=True, skip_group_check=True))
                            for G in range(NG):
                                if kbk == min(4 * G + 4, NT - 1):
                                    po, pd = OB_[G % 2]

                                    def ep1(pd=pd):
                                        recip(pd, pd[:, 0:512], rd, rd[:], esink, esink[:, l * 8 + hh:l * 8 + hh + 1])

                                    def ep2(po=po, G=G):
                                        kb.op(dve, [po, rd], [oa], lambda e, po=po: e.tensor_tensor(out=oa[:], in0=po[:, 0:512], in1=rd[:], op=ALU.mult))
                                        gate_and_store(oa, hh % 2, hh * 128, G * 512)
                                    defer(1, ep1)
                                    defer(2, ep2)

                        steps.append((qk, [sbk], pv, ncols))
                run_pipeline(steps)

        with kb.scope():
            wo_t = kb.sb("wo", [128, DC, D], BF16)
            woc = [Buf(wo_t.t[:, c, :], f"wo{c}") for c in range(DC)]
            wos = [kb.sb(f"wos{i}", [128, D], F32) for i in range(4)]
            for c in range(4):
                kb.dma(wos[c % 4][:], wout_in.t[l, c * 128:(c + 1) * 128, :], wout_in, wos[c % 4])
            for c in range(DC):
                if c >= 1 and c + 3 < DC:
                    kb.dma(wos[(c + 3) % 4][:], wout_in.t[l, (c + 3) * 128:(c + 4) * 128, :], wout_in, wos[(c + 3) % 4])
                if c % 2 == 0:
                    kb.op(dve, [wos[c % 4]], [woc[c]], lambda e, c=c: e.tensor_copy(out=woc[c][:], in_=wos[c % 4][:]))
                else:
                    kb.op(act, [wos[c % 4]], [woc[c]], lambda e, c=c: e.activation(out=woc[c][:], in_=wos[c % 4][:], func=AF.Copy))
            yt = [kb.sb(f"yt{i}", [128, DC, 512], BF16) for i in range(2)]
            xr = [kb.sb(f"xr{i}", [128, D], F32) for i in range(2)]
            xn = [kb.sb(f"xn{i}", [128, D], F32) for i in range(2)]
            last = (l == NL - 1)
            jk = kb.sb("jk", [128, D], BF16)
            ss2 = [kb.sb(f"ss2{i}", [128, 1], F32) for i in range(2)]
            rr2 = [kb.sb(f"rr2{i}", [128, 1], F32) for i in range(2)]
            if last:
                fg = kb.sb("fg", [128, D], F32)
                kb.dma(fg[:], fnorm_in.t.partition_broadcast(128), fnorm_in, fg)
            else:
                hb2 = [kb.sb(f"hb2{i}", [128, D], BF16) for i in range(2)]
                hst = [kb.sb(f"hst{i}", [128, DC, 256], BF16) for i in range(2)]
            psi_c = [0]

            def post_a(t):
                i = t % 2
                xo = xn[i]
                kb.op(act, [xo], [jk, ss2[i]], lambda e: e.activation(
                    out=jk[:], in_=xo[:], func=AF.Square, accum_out=ss2[i][:, 0:1]))
                rsqrt(ss2[i], ss2[i][:], rr2[i], rr2[i][:], float(D) * EPS)
                kb.op(dve, [xo, rr2[i]], [hb2[i]], lambda e: e.tensor_scalar(
                    out=hb2[i][:], in0=xo[:], scalar1=rr2[i][:, 0:1], scalar2=math.sqrt(D), op0=ALU.mult, op1=ALU.mult))

            def post(t):
                i = t % 2
                hs = hst[(t // 2) % 2]
                for half in range(2):
                    pb = PS[psi_c[0] % 8]; psi_c[0] += 1
                    pv = pb.t[:].bitcast(BF16)
                    for c8 in range(8):
                        c = half * 8 + c8
                        kb.op(pe, [hb2[i], ident], [pb], lambda e, c=c, c8=c8, pv=pv: e.transpose(
                            pv[:, c8 * 128:(c8 + 1) * 128], hb2[i][:, c * 128:(c + 1) * 128], ident[:]), inc=(c8 == 7))
                    o_ap = hs[:, half * 8:(half + 1) * 8, (t % 2) * 128:(t % 2 + 1) * 128]
                    i_ap = pv[:, 0:1024].rearrange("p (c k) -> p c k", c=8)
                    if half == 0:
                        kb.op(dve, [pb], [hs], lambda e: e.tensor_copy(out=o_ap, in_=i_ap))
                    else:
                        kb.op(act, [pb], [hs], lambda e: e.activation(out=o_ap, in_=i_ap, func=AF.Copy))
                if t % 2 == 1:
                    kb.dma(HT1.t[:, (t - 1) * 128:(t + 1) * 128].rearrange("(c p) s -> p c s", p=128), hs[:], hs, HT1)

            def ld_y(tb):
                kb.dma(yt[tb % 2][:], YT.t[:, tb * 512:(tb + 1) * 512].rearrange("(c p) s -> p c s", p=128), YT, yt[tb % 2])

            def ld_x(t):
                kb.dma(xr[t % 2][:], xsrc.t[t * 128:(t + 1) * 128, :], xsrc, xr[t % 2])

            ld_y(0)
            ld_x(0)
            for tb in range(NTB):
                if tb + 1 < NTB:
                    ld_y(tb + 1)
                for ti in range(4):
                    t = tb * 4 + ti
                    if t + 1 < NT:
                        ld_x(t + 1)
                    xo = xn[t % 2]
                    for dg in range(4):
                        pb = PS[psi_c[0] % 8]; psi_c[0] += 1
                        for c in range(DC):
                            kb.op(pe, [yt[tb % 2], woc[c]], [pb], lambda e, c=c, pb=pb, ti=ti, dg=dg, tb=tb: e.matmul(
                                pb[:, 0:512], lhsT=yt[tb % 2][:, c, ti * 128:(ti + 1) * 128], rhs=woc[c][:, dg * 512:(dg + 1) * 512],
                                start=(c == 0), stop=(c == DC - 1)), inc=(c == DC - 1))
                        kb.op(dve, [pb, xr[t % 2]], [xo], lambda e, pb=pb, dg=dg, t=t, xo=xo: e.tensor_tensor(
                            out=xo[:, dg * 512:(dg + 1) * 512], in0=pb[:, 0:512], in1=xr[t % 2][:, dg * 512:(dg + 1) * 512], op=ALU.add))
                    if not last:
                        kb.dma(X1.t[t * 128:(t + 1) * 128, :], xo[:], xo, X1)
                        post_a(t)
                        if t >= 1:
                            post(t - 1)
                        if t == NT - 1:
                            post(t)
                    else:
                        i = t % 2
                        kb.op(act, [xo], [jk, ss2[i]], lambda e, i=i, xo=xo: e.activation(
                            out=jk[:], in_=xo[:], func=AF.Square, accum_out=ss2[i][:, 0:1]))
                        rsqrt(ss2[i], ss2[i][:], rr2[i], rr2[i][:], float(D) * EPS)
                        kb.op(pool, [rr2[i]], [rr2[i]], lambda e, i=i: e.tensor_single_scalar(
                            out=rr2[i][:], in_=rr2[i][:], scalar=math.sqrt(D), op=ALU.mult))
                        kb.op(dve, [xo, rr2[i], fg], [xo], lambda e, i=i, xo=xo: e.scalar_tensor_tensor(
                            out=xo[:], in0=xo[:], scalar=rr2[i][:, 0:1], in1=fg[:], op0=ALU.mult, op1=ALU.mult))
                        kb.dma(out_d.t[t * 128:(t + 1) * 128, :], xo[:], xo, out_d)
    kb.barrier()


_NC_CACHE = {}


def _get_nc(S, NL, dbg=False):
    key = (S, NL, dbg)
    if key not in _NC_CACHE:
        _NC_CACHE[key] = build(S, NL, dbg)
    return _NC_CACHE[key]


def kernel(x, positions, norm_g, w_in, swa_sink, mla_q_norm, mla_w_uq, mla_kv_norm,
           mla_w_ukv, diff_lambda, diff_subln, w_out, final_norm):
    x = np.asarray(x, dtype=np.float32)
    B, S, _ = x.shape
    nc = _get_nc(S, 2)
    shared = {
        "positions": np.ascontiguousarray(np.asarray(positions, dtype=np.int32)),
        "norm_g": np.ascontiguousarray(np.asarray(norm_g, dtype=np.float32)),
        "w_in": np.ascontiguousarray(np.asarray(w_in, dtype=np.float32)),
        "swa_sink": np.ascontiguousarray(np.asarray(swa_sink, dtype=np.float32)),
        "mla_q_norm": np.ascontiguousarray(np.asarray(mla_q_norm, dtype=np.float32)),
        "mla_w_uq": np.ascontiguousarray(np.asarray(mla_w_uq, dtype=np.float32)),
        "mla_kv_norm": np.ascontiguousarray(np.asarray(mla_kv_norm, dtype=np.float32)),
        "mla_w_ukv": np.ascontiguousarray(np.asarray(mla_w_ukv, dtype=np.float32)),
        "diff_lambda": np.ascontiguousarray(np.asarray(diff_lambda, dtype=np.float32)),
        "diff_subln": np.ascontiguousarray(np.asarray(diff_subln, dtype=np.float32)),
        "w_out": np.ascontiguousarray(np.asarray(w_out, dtype=np.float32)),
        "final_norm": np.ascontiguousarray(np.asarray(final_norm, dtype=np.float32)),
    }
    in_maps = []
    for b in range(B):
        m = dict(shared)
        m["x"] = np.ascontiguousarray(x[b])
        in_maps.append(m)
    res = run_bass_kernel_spmd(nc, in_maps, core_ids=list(range(B)))
    return np.stack([np.asarray(r["out"], dtype=np.float32) for r in res.results], axis=0)
```
